# Optimizing a Trainium2 kernel written in Bass

```python
import jax
import jax.numpy as jnp
from jax import lax
import numpy as np

D_MODEL = 1024
BATCH = 2
SEQ = 8192
DEPTH = 2
DEC_BATCH = 8
DEC_SEQ = 16
PAST_LEN = 4096

CHUNK = 64
LEFT_CHUNKS = 8
ATT_WINDOW = LEFT_CHUNKS * CHUNK
BAND = (LEFT_CHUNKS + 1) * CHUNK
N_HEADS_A = 8
HEAD_DIM_A = 64
WIDTH_A = N_HEADS_A * HEAD_DIM_A
REL_CLIP = 256
N_HEADS_B = 4
HEAD_DIM_B = 128
WIDTH_B = N_HEADS_B * HEAD_DIM_B
MIX_WIDTH = WIDTH_A + WIDTH_B
D_FF = 2816
CONV_W = 3
EPS = 1e-6
IN_SPLITS = (WIDTH_A, 2 * WIDTH_A, 3 * WIDTH_A,
             3 * WIDTH_A + WIDTH_B, 3 * WIDTH_A + 2 * WIDTH_B, 3 * WIDTH_A + 3 * WIDTH_B,
             3 * WIDTH_A + 4 * WIDTH_B, 3 * WIDTH_A + 4 * WIDTH_B + N_HEADS_B)
D_IN = 3 * WIDTH_A + 4 * WIDTH_B + 2 * N_HEADS_B

kernel_name = "hymba_chunkattn_mlstm_convffn_step"


def rms_norm(x, g):
    xf = x.astype(jnp.float32)
    y = xf * lax.rsqrt(jnp.mean(xf * xf, axis=-1, keepdims=True) + EPS)
    return (y * g.astype(jnp.float32)).astype(x.dtype)


def rel_bias(rel_table, q_pos, k_pos):
    idx = jnp.clip(q_pos[:, None] - k_pos[None, :], -REL_CLIP, REL_CLIP) + REL_CLIP
    return rel_table[:, idx].astype(jnp.float32)


def band_attention_prompt(q, k, v, rel_table):
    B, S = q.shape[:2]
    n_chunks = S // CHUNK
    qc = q.reshape(B, n_chunks, CHUNK, N_HEADS_A, HEAD_DIM_A)
    pad = ((0, 0), (ATT_WINDOW, 0), (0, 0), (0, 0))
    kp = jnp.pad(k, pad).reshape(B, n_chunks + LEFT_CHUNKS, CHUNK, N_HEADS_A, HEAD_DIM_A)
    vp = jnp.pad(v, pad).reshape(B, n_chunks + LEFT_CHUNKS, CHUNK, N_HEADS_A, HEAD_DIM_A)
    band_idx = jnp.arange(n_chunks)[:, None] + jnp.arange(LEFT_CHUNKS + 1)[None, :]
    kb = kp[:, band_idx].reshape(B, n_chunks, BAND, N_HEADS_A, HEAD_DIM_A)
    vb = vp[:, band_idx].reshape(B, n_chunks, BAND, N_HEADS_A, HEAD_DIM_A)
    s = jnp.einsum('bcqhd,bckhd->bhcqk', qc, kb, preferred_element_type=jnp.float32)
    s = s * (HEAD_DIM_A ** -0.5)
    bias = rel_bias(rel_table, ATT_WINDOW + jnp.arange(CHUNK), jnp.arange(BAND))
    s = s + bias[None, :, None]
    valid = (jnp.arange(n_chunks)[:, None] + jnp.arange(BAND)[None, :] // CHUNK) >= LEFT_CHUNKS
    s = jnp.where(valid[None, None, :, None, :], s, -jnp.inf)
    p = jax.nn.softmax(s, axis=-1).astype(v.dtype)
    o = jnp.einsum('bhcqk,bckhd->bcqhd', p, vb)
    return o.reshape(B, S, WIDTH_A)


def band_attention_step(q, k, v, k_cache, v_cache, rel_table):
    B, S = q.shape[:2]
    L = k_cache.shape[2]
    kk = jnp.concatenate([k_cache.astype(k.dtype), k.transpose(0, 2, 1, 3)], axis=2)
    vv = jnp.concatenate([v_cache.astype(v.dtype), v.transpose(0, 2, 1, 3)], axis=2)
    s = jnp.einsum('bqhd,bhkd->bhqk', q, kk, preferred_element_type=jnp.float32)
    s = s * (HEAD_DIM_A ** -0.5) + rel_bias(rel_table, L + jnp.arange(S), jnp.arange(L + S))[None]
    p = jax.nn.softmax(s, axis=-1).astype(v.dtype)
    o = jnp.einsum('bhqk,bhkd->bqhd', p, vv)
    return o.reshape(B, S, WIDTH_A)


def mlstm_block(carry, inp):
    c_prev, n_prev, m_prev = carry
    q, k, v, ig, lf = inp
    L = q.shape[2]
    b = jnp.cumsum(lf, axis=-1)
    causal = jnp.tril(jnp.ones((L, L), dtype=bool))
    d_mat = b[..., :, None] - b[..., None, :] + ig[..., None, :]
    d_mat = jnp.where(causal, d_mat, -jnp.inf)
    inter = b + m_prev[..., None]
    m_t = jnp.maximum(inter, jnp.max(d_mat, axis=-1))
    w_intra = jnp.exp(d_mat - m_t[..., None])
    w_inter = jnp.exp(inter - m_t)
    qk = jnp.einsum('bhtd,bhsd->bhts', q, k) * w_intra
    num = jnp.einsum('bhts,bhse->bhte', qk, v) + w_inter[..., None] * jnp.einsum('bhtd,bhde->bhte', q, c_prev)
    den = jnp.sum(qk, axis=-1) + w_inter * jnp.einsum('bhtd,bhd->bht', q, n_prev)
    h = num / jnp.maximum(jnp.abs(den), jnp.exp(-m_t))[..., None]
    m_new = m_t[..., -1]
    w_s = jnp.exp(b[..., -1:] - b + ig - m_new[..., None])
    decay = jnp.exp(b[..., -1] + m_prev - m_new)
    c_new = decay[..., None, None] * c_prev + jnp.einsum('bhs,bhsd,bhse->bhde', w_s, k, v)
    n_new = decay[..., None] * n_prev + jnp.einsum('bhs,bhsd->bhd', w_s, k)
    return (c_new, n_new, m_new), h


def mlstm_sequence(q, k, v, ig, lf, c0, n0, m0):
    B, H, S, d = q.shape
    carry0 = (c0.astype(jnp.float32), n0.astype(jnp.float32), m0.astype(jnp.float32))
    if S <= CHUNK:
        carry, h = mlstm_block(carry0, (q, k, v, ig, lf))
        return h, carry
    nb = S // CHUNK

    def to_blocks(t):
        return jnp.moveaxis(t.reshape((B, H, nb, CHUNK) + t.shape[3:]), 2, 0)

    xs = (to_blocks(q), to_blocks(k), to_blocks(v), to_blocks(ig), to_blocks(lf))
    carry, hb = lax.scan(mlstm_block, carry0, xs)
    h = jnp.moveaxis(hb, 0, 2).reshape(B, H, S, d)
    return h, carry


def conv_ffn(h, buf, w_up, w_conv, b_conv, w_down):
    S = h.shape[1]
    u = h @ w_up
    ext = jnp.concatenate([buf.astype(u.dtype), u], axis=1)
    y = b_conv
    for j in range(CONV_W):
        y = y + ext[:, j:j + S] * w_conv[j]
    gate, up = jnp.split(y, 2, axis=-1)
    out = (jax.nn.gelu(gate, approximate=True) * up) @ w_down
    return out, ext[:, S:]


def layer(x, k_cache, v_cache, c0, n0, m0, conv_buf, norm_g, w_in, b_i, b_f, rel_table,
          g_att, g_mlstm, w_out, w_up, w_conv, b_conv, w_down):
    B, S, _ = x.shape
    h = rms_norm(x, norm_g[0])
    z = h @ w_in
    qa, ka, va, qb, kb, vb, ob, ib, fb = jnp.split(z, IN_SPLITS, axis=-1)
    qa = qa.reshape(B, S, N_HEADS_A, HEAD_DIM_A)
    ka = ka.reshape(B, S, N_HEADS_A, HEAD_DIM_A)
    va = va.reshape(B, S, N_HEADS_A, HEAD_DIM_A)
    if k_cache is None:
        att = band_attention_prompt(qa, ka, va, rel_table)
        keep = min(ATT_WINDOW, S)
        new_k = ka[:, S - keep:].transpose(0, 2, 1, 3)
        new_v = va[:, S - keep:].transpose(0, 2, 1, 3)
    else:
        att = band_attention_step(qa, ka, va, k_cache, v_cache, rel_table)
        new_k = ka.transpose(0, 2, 1, 3)
        new_v = va.transpose(0, 2, 1, 3)
    att = rms_norm(att, g_att)

    def heads_b(t):
        return t.reshape(B, S, N_HEADS_B, HEAD_DIM_B).transpose(0, 2, 1, 3).astype(jnp.float32)

    qh = heads_b(qb)
    kh = heads_b(kb) * (HEAD_DIM_B ** -0.5)
    vh = heads_b(vb)
    ig = (ib + b_i).astype(jnp.float32).transpose(0, 2, 1)
    lf = jax.nn.log_sigmoid((fb + b_f).astype(jnp.float32)).transpose(0, 2, 1)
    hb, (c1, n1, m1) = mlstm_sequence(qh, kh, vh, ig, lf, c0, n0, m0)
    hb = hb * lax.rsqrt(jnp.mean(hb * hb, axis=-1, keepdims=True) + EPS)
    hb = hb.transpose(0, 2, 1, 3).reshape(B, S, WIDTH_B)
    mlstm_out = (hb * g_mlstm.astype(jnp.float32) * jax.nn.sigmoid(ob.astype(jnp.float32))).astype(x.dtype)

    mix = jnp.concatenate([att, mlstm_out], axis=-1) @ w_out
    x = x + rms_norm(mix, norm_g[1])
    ffn, new_buf = conv_ffn(rms_norm(x, norm_g[2]), conv_buf, w_up, w_conv, b_conv, w_down)
    x = x + rms_norm(ffn, norm_g[3])
    return x, new_k, new_v, c1, n1, m1, new_buf


def setup_inputs(seed: int = 0) -> dict:
    key = jax.random.key(seed)
    ks = jax.random.split(key, 20)
    att_cache = min(ATT_WINDOW, PAST_LEN)
    f32 = jnp.float32
    nrm = lambda k, shape, s: (jax.random.normal(k, shape, f32) * s)
    return {
        "x_prompt": nrm(ks[0], (BATCH, SEQ, D_MODEL), 1.0),
        "x_sample": nrm(ks[1], (DEC_BATCH, DEC_SEQ, D_MODEL), 1.0),
        "cache_k_att": nrm(ks[2], (DEPTH, DEC_BATCH, N_HEADS_A, att_cache, HEAD_DIM_A), 1.0),
        "cache_v_att": nrm(ks[3], (DEPTH, DEC_BATCH, N_HEADS_A, att_cache, HEAD_DIM_A), 1.0),
        "state_mlstm_c": nrm(ks[4], (DEPTH, DEC_BATCH, N_HEADS_B, HEAD_DIM_B, HEAD_DIM_B), 0.1),
        "state_mlstm_n": nrm(ks[5], (DEPTH, DEC_BATCH, N_HEADS_B, HEAD_DIM_B), 0.1),
        "state_mlstm_m": nrm(ks[6], (DEPTH, DEC_BATCH, N_HEADS_B), 0.5),
        "cache_ffn_conv": nrm(ks[7], (DEPTH, DEC_BATCH, CONV_W - 1, 2 * D_FF), 1.0),
        "norm_g": 1.0 + nrm(ks[8], (DEPTH, 4, D_MODEL), 0.05),
        "w_in": nrm(ks[9], (DEPTH, D_MODEL, D_IN), D_MODEL ** -0.5),
        "b_i": nrm(ks[10], (DEPTH, N_HEADS_B), 0.1),
        "b_f": jnp.linspace(3.0, 6.0, N_HEADS_B, dtype=f32)[None, :] + nrm(ks[11], (DEPTH, N_HEADS_B), 0.1),
        "rel_table": nrm(ks[12], (DEPTH, N_HEADS_A, 2 * REL_CLIP + 1), 0.5),
        "g_att": 1.0 + nrm(ks[13], (DEPTH, WIDTH_A), 0.05),
        "g_mlstm": 1.0 + nrm(ks[14], (DEPTH, WIDTH_B), 0.05),
        "w_out": nrm(ks[15], (DEPTH, MIX_WIDTH, D_MODEL), MIX_WIDTH ** -0.5),
        "w_up": nrm(ks[16], (DEPTH, D_MODEL, 2 * D_FF), D_MODEL ** -0.5),
        "w_conv": nrm(ks[17], (DEPTH, CONV_W, 2 * D_FF), CONV_W ** -0.5),
        "b_conv": nrm(ks[18], (DEPTH, 2 * D_FF), 0.02),
        "w_down": nrm(ks[19], (DEPTH, D_FF, D_MODEL), D_FF ** -0.5),
    }


def reference(x_prompt, x_sample, cache_k_att, cache_v_att, state_mlstm_c, state_mlstm_n,
              state_mlstm_m, cache_ffn_conv, norm_g, w_in, b_i, b_f, rel_table, g_att, g_mlstm,
              w_out, w_up, w_conv, b_conv, w_down):
    bp = x_prompt.shape[0]
    c_zero = jnp.zeros((bp, N_HEADS_B, HEAD_DIM_B, HEAD_DIM_B), jnp.float32)
    n_zero = jnp.zeros((bp, N_HEADS_B, HEAD_DIM_B), jnp.float32)
    m_zero = jnp.zeros((bp, N_HEADS_B), jnp.float32)
    buf_zero = jnp.zeros((bp, CONV_W - 1, 2 * D_FF), x_prompt.dtype)
    xp = x_prompt
    xs = x_sample
    pk, pv, pc, pn, pm, pconv = [], [], [], [], [], []
    sk, sv, sc, sn, sm, sconv = [], [], [], [], [], []
    for l in range(DEPTH):
        xp, k1, v1, c1, n1, m1, b1 = layer(
            xp, None, None, c_zero, n_zero, m_zero, buf_zero, norm_g[l], w_in[l], b_i[l], b_f[l],
            rel_table[l], g_att[l], g_mlstm[l], w_out[l], w_up[l], w_conv[l], b_conv[l], w_down[l])
        pk.append(k1); pv.append(v1); pc.append(c1); pn.append(n1); pm.append(m1); pconv.append(b1)
        xs, k2, v2, c2, n2, m2, b2 = layer(
            xs, cache_k_att[l], cache_v_att[l], state_mlstm_c[l], state_mlstm_n[l], state_mlstm_m[l],
            cache_ffn_conv[l], norm_g[l], w_in[l], b_i[l], b_f[l], rel_table[l], g_att[l], g_mlstm[l],
            w_out[l], w_up[l], w_conv[l], b_conv[l], w_down[l])
        sk.append(k2); sv.append(v2); sc.append(c2); sn.append(n2); sm.append(m2); sconv.append(b2)
    k_att_prompt = jnp.stack(pk)
    v_att_prompt = jnp.stack(pv)
    mlstm_c_prompt = jnp.stack(pc)
    mlstm_n_prompt = jnp.stack(pn)
    mlstm_m_prompt = jnp.stack(pm)
    ffn_conv_prompt = jnp.stack(pconv)
    k_att_sample = jnp.stack(sk)
    v_att_sample = jnp.stack(sv)
    mlstm_c_sample = jnp.stack(sc)
    mlstm_n_sample = jnp.stack(sn)
    mlstm_m_sample = jnp.stack(sm)
    ffn_conv_sample = jnp.stack(sconv)
    return (xp, xs, k_att_prompt, v_att_prompt, mlstm_c_prompt, mlstm_n_prompt, mlstm_m_prompt,
            ffn_conv_prompt, k_att_sample, v_att_sample, mlstm_c_sample, mlstm_n_sample,
            mlstm_m_sample, ffn_conv_sample)
```

```python
import contextlib
import math
import numpy as np
import concourse.bass as bass
import concourse.mybir as mybir
from concourse.bass_utils import run_bass_kernel_spmd

F32 = mybir.dt.float32
BF16 = mybir.dt.bfloat16
AF = mybir.ActivationFunctionType
ALU = mybir.AluOpType
AX = mybir.AxisListType

D = 1024
TOK = 2048
NTILE = 16
DIN = 3592
DFF = 2816
NCH = 22
EPS = 1e-6
C_QA, C_KA, C_VA, C_QB, C_KB, C_VB, C_OB, C_G = 0, 512, 1024, 1536, 2048, 2560, 3072, 3584
LNK = -0.5 * math.log(128.0)
NEG = -1.0e30
DSZ = 128 * 769 + 768
XROWS = 129
XCOLS = 516


class Dep:
    __slots__ = ("name", "w", "r", "dsem")

    def __init__(self, name):
        self.name = name
        self.w = None
        self.r = {}
        self.dsem = None


class Sem:
    __slots__ = ("key", "h", "count", "is_dma")

    def __init__(self, key, h, is_dma):
        self.key = key
        self.h = h
        self.count = 0
        self.is_dma = is_dma


class Eng:
    __slots__ = ("name", "eng", "sem", "seen", "same_sync")

    def __init__(self, name, eng, sem, same_sync):
        self.name = name
        self.eng = eng
        self.sem = sem
        self.seen = {}
        self.same_sync = same_sync


class Ctx:
    def __init__(self, nc, same_sync=True):
        self.nc = nc
        self.sems = {}
        self.engs = {}
        for name, eng, ss in (("pe", nc.tensor, False), ("act", nc.scalar, same_sync),
                              ("dve", nc.vector, same_sync), ("pool", nc.gpsimd, same_sync),
                              ("sp", nc.sync, False)):
            s = self.new_sem("e_" + name, False)
            self.engs[name] = Eng(name, eng, s, ss)
        self.n_wait = 0
        self.n_inst = 0
        self.shared = {}

    def new_sem(self, key, is_dma=True):
        h = self.nc.alloc_semaphore(key)
        s = Sem(key, h, is_dma)
        self.sems[key] = s
        return s

    def shared_sem(self, key):
        if key not in self.shared:
            self.shared[key] = self.new_sem("sh_" + key)
        return self.shared[key]

    def _waits(self, es, reads, writes):
        need = {}
        for d in reads:
            if d.w is not None:
                k, v = d.w
                if need.get(k, 0) < v:
                    need[k] = v
        for d in writes:
            if d.w is not None:
                k, v = d.w
                if need.get(k, 0) < v:
                    need[k] = v
            for k, v in d.r.items():
                if need.get(k, 0) < v:
                    need[k] = v
        for k, v in need.items():
            if k == es.sem.key and not es.same_sync:
                continue
            s = self.sems[k]
            if s.is_dma:
                v = s.count
            if es.seen.get(k, 0) < v:
                es.eng.wait_ge(s.h, v)
                es.seen[k] = v
                self.n_wait += 1

    def op(self, E, fns, reads=(), writes=()):
        es = self.engs[E]
        self._waits(es, reads, writes)
        if not isinstance(fns, (list, tuple)):
            fns = [fns]
        inst = None
        for f in fns:
            inst = f(es.eng)
            self.n_inst += 1
        es.sem.count += 1
        inst.then_inc(es.sem.h, 1)
        k, c = es.sem.key, es.sem.count
        for d in reads:
            d.r[k] = c
        for d in writes:
            d.w = (k, c)
            d.r = {}
        return inst

    def _sem_for(self, reads, writes, sem):
        if sem is not None:
            return sem
        d0 = writes[0] if writes else reads[0]
        if d0.dsem is None:
            key = "d_" + d0.name
            if key not in self.sems:
                self.new_sem(key)
            d0.dsem = key
        return self.sems[d0.dsem]

    def dma(self, Q, out, in_, reads=(), writes=(), sem=None, **kw):
        es = self.engs[Q]
        self._waits(es, reads, writes)
        sem = self._sem_for(reads, writes, sem)
        inst = es.eng.dma_start(out=out, in_=in_, **kw)
        self.n_inst += 1
        sem.count += 16
        inst.then_inc(sem.h, 16)
        for d in reads:
            d.r[sem.key] = sem.count
        for d in writes:
            d.w = (sem.key, sem.count)
            d.r = {}
        return inst

    def allgather(self, src_ap, dst_ap, reads, writes, sem):
        es = self.engs["pool"]
        self._waits(es, reads, writes)
        inst = es.eng.collective_compute("AllGather", ALU.bypass, replica_groups=[[0, 1, 2, 3], [4, 5, 6, 7]],
                                         ins=[src_ap], outs=[dst_ap])
        self.n_inst += 1
        sem.count += 1
        inst.then_inc(sem.h, 1)
        for d in reads:
            d.r[sem.key] = sem.count
        for d in writes:
            d.w = (sem.key, sem.count)
            d.r = {}

    def barrier(self, exclude=()):
        for es in self.engs.values():
            for k, s in self.sems.items():
                if k in exclude:
                    continue
                if s.count > 0 and k != es.sem.key and es.seen.get(k, 0) < s.count:
                    es.eng.wait_ge(s.h, s.count)
                    es.seen[k] = s.count
                    self.n_wait += 1

    def finish(self, E="sp"):
        es = self.engs[E]
        for k, s in self.sems.items():
            if s.count > 0 and k != es.sem.key and es.seen.get(k, 0) < s.count:
                es.eng.wait_ge(s.h, s.count)
                es.seen[k] = s.count


class TD:
    __slots__ = ("t", "d")

    def __init__(self, t, name):
        self.t = t
        self.d = Dep(name)


class _Stop(Exception):
    pass


def build(do_sample=True):
    import os
    kstop = int(os.environ.get("KSTOP", "99"))

    def stop_at(n):
        if kstop == n:
            raise _Stop()
    nc = bass.Bass("TRN2", target_bir_lowering=False)
    cx = Ctx(nc)
    uid = [0]

    def nm(s):
        uid[0] += 1
        return f"{s}_{uid[0]}"

    def din(name, shape):
        return nc.dram_tensor(name, list(shape), F32, kind="ExternalInput")

    def dout(name, shape):
        return nc.dram_tensor(name, list(shape), F32, kind="ExternalOutput")

    def dscr(name, shape, dt=F32):
        return nc.dram_tensor(name, list(shape), dt)

    XP = din("xp", [TOK, D]); XS = din("xs", [16, D])
    WIN = din("w_in", [2, D, DIN]); WOUT = din("w_out", [2, D, D])
    WUP = din("w_up", [2, D, 2 * DFF]); WDN = din("w_down", [2, DFF, D])
    NG = din("norm_g", [2, 4, D]); GATT = din("g_att", [2, 512]); GML = din("g_mlstm", [2, 512])
    BI = din("b_i", [2, 4]); BFG = din("b_f", [2, 4]); REL = din("rel_table", [2, 8, 513])
    WCV = din("w_conv", [2, 3, 2 * DFF]); BCV = din("b_conv", [2, 2 * DFF])
    CK = din("ck", [2, 8, 512, 64]); CV = din("cv", [2, 8, 512, 64])
    SC = din("sc", [2, 4, 128, 128]); SN = din("sn", [2, 4, 128]); SM = din("sm", [2, 4])
    CCV = din("cconv", [2, 2, 2 * DFF])
    IDN = din("ident", [128, 128]); TRIU = din("triu", [128, 128]); ONES = din("ones", [128, 128])
    CMASK = din("cmask", [12])

    YP = dout("yp", [TOK, D]); YS = dout("ys", [16, D])
    KP = dout("kp", [2, 8, 512, 64]); VP = dout("vp", [2, 8, 512, 64])
    CP = dout("cp", [2, 4, 128, 128]); NP = dout("np", [2, 4, 128]); MP = dout("mp", [2, 4])
    CVP = dout("cvp", [2, 2, 2 * DFF])
    KS = dout("ks", [2, 8, 16, 64]); VS = dout("vs", [2, 8, 16, 64])
    CS = dout("cs", [2, 4, 128, 128]); NS = dout("ns", [2, 4, 128]); MS = dout("ms", [2, 4])
    CVS = dout("cvs", [2, 2, 2 * DFF])

    XM = dscr("xm", [TOK, D]); X1 = dscr("x1", [TOK, D])
    XSM = dscr("xsm", [16, D]); XS1 = dscr("xs1", [16, D])
    DTO = dscr("dtoep", [2, 8, DSZ])
    EXK_S = dscr("exk_s", [512, 512], BF16); EXK_D = dscr("exk_d", [4 * 512, 512], BF16)
    EXV_S = dscr("exv_s", [512, 512], BF16); EXV_D = dscr("exv_d", [4 * 512, 512], BF16)
    EXS_S = dscr("exs_s", [XROWS, XCOLS]); EXS_D = dscr("exs_d", [4 * XROWS, XCOLS])
    EXX_S = dscr("exx_s", [2, D]); EXX_D = dscr("exx_d", [8, D])
    d_xm = Dep("xm"); d_x1 = Dep("x1"); d_xsm = Dep("xsm"); d_xs1 = Dep("xs1"); d_dto = Dep("dto")
    d_exk = [Dep("exk_s"), Dep("exk_d")]; d_exv = [Dep("exv_s"), Dep("exv_d")]
    d_exs = [Dep("exs_s"), Dep("exs_d")]; d_exx = [Dep("exx_s"), Dep("exx_d")]
    d_out = Dep("outs")
    sem_out = cx.shared_sem("out")
    sem_cc = cx.new_sem("cc", True)

    PB = [nc.alloc_psum_tensor(f"pb{i}", [128, 512], F32) for i in range(8)]
    PD = [Dep(f"pb{i}") for i in range(8)]
    MM0, MM1, TRB, S0, S1, O0, O1, ML = range(8)

    def sb(name, shape, dt=F32):
        return TD(nc.alloc_sbuf_tensor(nm(name), list(shape), dt), name)

    ident_f = sb("ident_f", [128, 128]); triu_f = sb("triu_f", [128, 128]); ones_f = sb("ones_f", [128, 128])
    ident_b = sb("ident_b", [128, 128], BF16); masku_b = sb("masku_b", [128, 128], BF16)
    cm = sb("cm", [128, 12])
    EALL1 = sb("eall", [128, 8, 5, 128], BF16)
    EALL = [EALL1, EALL1]
    gbc = sb("gbc", [128, 4, D]); gatt = sb("gatt", [128, 512]); gml = sb("gml", [128, 512])
    bif = sb("bif", [128, 8]); wcT = sb("wcT", [128, 3, 2 * NCH]); bcT = sb("bcT", [128, 2 * NCH])
    WB = [sb(f"wb{i}", [128, 8, 512], BF16) for i in range(3)]
    wg = sb("wg", [128, 8, 8], BF16)
    C32 = sb("c32", [128, 4, 129]); CBF = sb("cbf", [128, 4, 129], BF16)
    ngrun = sb("ngrun", [128, 4]); amaxr = sb("amaxr", [128, 4])
    uh_p = sb("uh_p", [128, 2 * NCH, 2]); uh_s = sb("uh_s", [128, 2 * NCH, 2])
    ME = [sb(f"mE{i}", [128, 8]) for i in range(2)]
    SMALL = [sb(f"sm{i}", [128, 64]) for i in range(6)]
    sm_i = [0]

    def small():
        sm_i[0] = (sm_i[0] + 1) % len(SMALL)
        return SMALL[sm_i[0]]

    xt_i = [0]


    class WS:
        def __init__(self):
            self.pending = []
            self.inflight = []
            self.k = 0

        def schedule(self, loaders):
            self.pending.extend(loaders)

        def prefetch(self):
            while len(self.inflight) < 3 and self.pending:
                slot = WB[self.k % 3]
                self.k += 1
                self.pending.pop(0)(slot)
                self.inflight.append(slot)

        def get(self):
            while len(self.inflight) < 3 and self.pending:
                slot = WB[self.k % 3]
                self.k += 1
                self.pending.pop(0)(slot)
                self.inflight.append(slot)
            return self.inflight.pop(0)

    ws = WS()

    def win_loader(l, c0, n=512):
        def f(slot):
            cx.dma("pool", slot.t[:, :, 0:n], WIN.ap()[l, :, c0:c0 + n].rearrange("(k p) n -> p k n", p=128),
                   writes=[slot.d])
        return f

    def wup_loader(l, j):
        def f(slot):
            cx.dma("pool", slot.t[:, :, 0:256], WUP.ap()[l, :, j * 256:(j + 1) * 256].rearrange("(k p) n -> p k n", p=128),
                   writes=[slot.d])
            cx.dma("pool", slot.t[:, :, 256:512],
                   WUP.ap()[l, :, DFF + j * 256:DFF + (j + 1) * 256].rearrange("(k p) n -> p k n", p=128),
                   writes=[slot.d])
        return f

    def mm(out, lhsT, rhs, start=True, stop=True):
        return lambda e: e.matmul(out, lhsT, rhs, start=start, stop=stop)

    def acc_group(out, pairs):
        n = len(pairs)
        return [mm(out, a, b, start=(i == 0), stop=(i == n - 1)) for i, (a, b) in enumerate(pairs)]

    def rstd_from_ss(st, nt, c_in, c_out, inv_n, width=1):
        cx.op("act", lambda e: e.activation(st.t[:nt, c_out:c_out + width], st.t[:nt, c_in:c_in + width], AF.Ln,
                                            scale=inv_n, bias=eps_t.t[:nt, 0:1]), reads=[st.d, eps_t.d], writes=[st.d])
        cx.op("act", lambda e: e.activation(st.t[:nt, c_out:c_out + width], st.t[:nt, c_out:c_out + width], AF.Exp,
                                            scale=-0.5), reads=[st.d], writes=[st.d])

    eps_t = sb("eps_t", [128, 4])

    cx.dma("sp", ident_f.t[:], IDN.ap()[:, :], writes=[ident_f.d])
    cx.dma("sp", triu_f.t[:], TRIU.ap()[:, :], writes=[triu_f.d])
    cx.dma("sp", ones_f.t[:], ONES.ap()[:, :], writes=[ones_f.d])
    cx.dma("sp", cm.t[:], CMASK.ap().partition_broadcast(128), writes=[cm.d])
    cx.op("dve", lambda e: e.tensor_copy(ident_b.t[:], ident_f.t[:]), reads=[ident_f.d], writes=[ident_b.d])
    cx.op("dve", lambda e: e.tensor_copy(masku_b.t[:], triu_f.t[:]), reads=[triu_f.d], writes=[masku_b.d])
    cx.op("dve", lambda e: e.memset(eps_t.t[:, 0:1], EPS), writes=[eps_t.d])
    cx.op("dve", lambda e: e.memset(eps_t.t[:, 1:2], LNK), writes=[eps_t.d])
    cx.op("dve", lambda e: e.memset(eps_t.t[:, 2:3], 1.0), writes=[eps_t.d])
    cx.op("dve", lambda e: e.memset(eps_t.t[:, 3:4], 0.0), writes=[eps_t.d])

    def build_E(l, es_):
        R = TD(es_.enter_context(nc.sbuf_tensor(nm("Rtoep"), [128, 8, 768], F32)), "Rtoep")
        cl = TD(es_.enter_context(nc.sbuf_tensor(nm("cl"), [128, 8, 1], F32)), "cl")
        EP = TD(es_.enter_context(nc.sbuf_tensor(nm("epre"), [128, 8, 5, 128], F32)), "epre")
        cx.dma("sp", R.t[:, :, 0:384], bass.AP(REL, l * 8 * 513 + 129, [[0, 128], [513, 8], [1, 384]]), writes=[R.d])
        cx.dma("sp", cl.t[:], bass.AP(REL, l * 8 * 513 + 512, [[0, 128], [513, 8], [1, 1]]), writes=[cl.d],
               allow_slow_non_contiguous=True)
        cx.op("dve", lambda e: e.tensor_copy(R.t[:, :, 384:768], cl.t[:].to_broadcast([128, 8, 384])),
              reads=[cl.d], writes=[R.d])
        cx.dma("sp", bass.AP(DTO, l * 8 * DSZ, [[769, 128], [DSZ, 8], [1, 768]]), R.t[:], reads=[R.d], writes=[d_dto])
        for j in range(5):
            cx.dma("sp", EP.t[:, :, j, :], bass.AP(DTO, l * 8 * DSZ + 127 + 128 * (4 - j), [[768, 128], [DSZ, 8], [1, 128]]),
                   reads=[d_dto], writes=[EP.d])
        cx.op("act", lambda e: e.activation(EALL1.t[:].rearrange("p h j q -> p (h j q)"),
                                            EP.t[:].rearrange("p h j q -> p (h j q)"), AF.Exp),
              reads=[EP.d], writes=[EALL1.d])
        cx.op("dve", lambda e: e.memset(EALL1.t[0:64, :, 0, 64:128], 0.0), writes=[EALL1.d])
        cx.op("dve", lambda e: e.memset(EALL1.t[64:128, :, 4, 0:64], 0.0), writes=[EALL1.d])

    def load_x(XT, src_ap, nt):
        xt_i[0] ^= 1
        xt = XT[xt_i[0]]
        cx.dma("sp", xt.t[:nt, :], src_ap, writes=[xt.d])
        return xt

    def norm_T(xt, nt, gidx, hT, col0, sq, xn):
        st = small()
        cx.op("act", lambda e: e.activation(sq.t[:nt, :], xt.t[:nt, :], AF.Square, accum_out=st.t[:nt, 0:1]),
              reads=[xt.d], writes=[sq.d, st.d])
        rstd_from_ss(st, nt, 0, 1, 1.0 / D)
        cx.op("dve", lambda e: e.scalar_tensor_tensor(xn.t[:nt, :], xt.t[:nt, :], st.t[:nt, 1:2], gbc.t[:nt, gidx, :],
                                                      ALU.mult, ALU.mult), reads=[xt.d, st.d, gbc.d], writes=[xn.d])
        transpose_to(xn, nt, 8, hT, col0)

    def transpose_to(src, nt, nk, dst, col0, eng="act"):
        trv = PB[TRB][:].bitcast(BF16).rearrange("p (k c) -> p k c", k=8)
        cx.op("pe", [(lambda e, k=k: e.transpose(trv[:, k, :nt], src.t[:nt, k * 128:(k + 1) * 128], ident_b.t[:nt, :nt]))
                     for k in range(nk)], reads=[src.d, ident_b.d], writes=[PD[TRB]])
        if eng == "act":
            cx.op("act", lambda e: e.activation(dst.t[:, 0:nk, col0:col0 + nt], trv[:, 0:nk, :nt], AF.Copy),
                  writes=[PD[TRB], dst.d])
        else:
            cx.op("dve", lambda e: e.tensor_copy(dst.t[:, 0:nk, col0:col0 + nt], trv[:, 0:nk, :nt]),
                  writes=[PD[TRB], dst.d])

    def norm_seq(items, gidx, hT, xns, junks):
        trv = PB[TRB][:].bitcast(BF16).rearrange("p (k c) -> p k c", k=8)
        prev = None
        for idx, (xt, nt, col0) in enumerate(items):
            xn = xns[idx % len(xns)]
            junk = junks[idx % len(junks)]
            st = small()
            cx.op("act", lambda e: e.activation(junk.t[:nt, :], xt.t[:nt, :], AF.Square, accum_out=st.t[:nt, 0:1]),
                  reads=[xt.d], writes=[junk.d, st.d])
            rstd_from_ss(st, nt, 0, 1, 1.0 / D)
            cx.op("dve", lambda e: e.scalar_tensor_tensor(xn.t[:nt, :], xt.t[:nt, :], st.t[:nt, 1:2], gbc.t[:nt, gidx, :],
                                                          ALU.mult, ALU.mult), reads=[xt.d, st.d, gbc.d], writes=[xn.d])
            if prev is not None:
                pnt, pcol = prev
                cx.op("dve", lambda e: e.tensor_copy(hT.t[:, 0:8, pcol:pcol + pnt], trv[:, 0:8, :pnt]), writes=[PD[TRB], hT.d])
            cx.op("pe", [(lambda e, k=k: e.transpose(trv[:, k, :nt], xn.t[:nt, k * 128:(k + 1) * 128], ident_b.t[:nt, :nt]))
                         for k in range(8)], reads=[xn.d, ident_b.d], writes=[PD[TRB]])
            prev = (nt, col0)
        pnt, pcol = prev
        cx.op("dve", lambda e: e.tensor_copy(hT.t[:, 0:8, pcol:pcol + pnt], trv[:, 0:8, :pnt]), writes=[PD[TRB], hT.d])

    mm_i = [0]
    mm_ring = [[MM0, MM1]]
    RING2 = [MM0, MM1]
    RING6 = [MM0, MM1, S0, S1, O0, O1]

    def next_mm():
        mm_i[0] += 1
        r = mm_ring[0]
        return r[mm_i[0] % len(r)]

    s_i = [0]

    def next_s():
        s_i[0] ^= 1
        return S0 if s_i[0] else S1

    ev_i = [0]

    def evac(out_ap, in_ap, reads_bank, out_dep, extra_reads=()):
        ev_i[0] ^= 1
        if ev_i[0]:
            cx.op("act", lambda e: e.activation(out_ap, in_ap, AF.Copy), reads=list(extra_reads),
                  writes=[PD[reads_bank], out_dep])
        else:
            cx.op("dve", lambda e: e.tensor_copy(out_ap, in_ap), reads=list(extra_reads), writes=[PD[reads_bank], out_dep])

    def fm_proj(wslot, cchunks, hT, ncols, dst, dst_k0, dst_col0):
        for i, cc in enumerate(cchunks):
            b = next_mm()
            cx.op("pe", acc_group(PB[b][:, 0:ncols], [(wslot.t[:, k, cc * 128:(cc + 1) * 128], hT.t[:, k, 0:ncols])
                                                      for k in range(8)]),
                  reads=[wslot.d, hT.d], writes=[PD[b]])
            evac(dst.t[:, dst_k0 + i, dst_col0:dst_col0 + ncols], PB[b][:, 0:ncols], b, dst.d)

    def tm_proj(w_ap_fn, w_dep, hT, col0, nt, ncols):
        b = next_mm()
        cx.op("pe", acc_group(PB[b][:nt, 0:ncols], [(hT.t[:, k, col0:col0 + nt], w_ap_fn(k)) for k in range(8)]),
              reads=[w_dep, hT.d], writes=[PD[b]])
        return b

    def gate_math(hT, tiles, gs):
        ntl = len(tiles)
        gp = PB[ML][:, 0:8 * ntl].rearrange("p (t c) -> p t c", c=8)
        for ti, (col0, nt) in enumerate(tiles):
            cx.op("pe", acc_group(PB[ML][:nt, 8 * ti:8 * ti + 8], [(hT.t[:, k, col0:col0 + nt], wg.t[:, k, :]) for k in range(8)]),
                  reads=[wg.d, hT.d], writes=[PD[ML]])
        nt = tiles[0][1]
        ig, sp = gs["ig"], gs["sp"]
        cx.op("dve", lambda e: e.tensor_tensor(ig.t[:nt, 0:ntl, :], gp[:nt, :, 0:4],
                                               bif.t[:nt, 0:4].unsqueeze(1).to_broadcast([nt, ntl, 4]), ALU.add),
              reads=[bif.d], writes=[PD[ML], ig.d])
        cx.op("dve", lambda e: e.tensor_tensor(sp.t[:nt, 0:ntl, :], gp[:nt, :, 4:8],
                                               bif.t[:nt, 4:8].unsqueeze(1).to_broadcast([nt, ntl, 4]), ALU.add),
              reads=[bif.d], writes=[PD[ML], sp.d])
        spf = sp.t[:nt, 0:ntl, :].rearrange("p t h -> p (t h)")
        cx.op("act", lambda e: e.activation(spf, spf, AF.Exp, scale=-1.0), reads=[sp.d], writes=[sp.d])
        cx.op("act", lambda e: e.activation(spf, spf, AF.Ln, bias=eps_t.t[:nt, 2:3]), reads=[sp.d, eps_t.d], writes=[sp.d])
        w = 4 * ntl
        cx.op("pe", [mm(PB[ML][:nt, 128:128 + w], triu_f.t[:nt, :nt], spf),
                     mm(PB[ML][:, 192:192 + w], ones_f.t[:nt, :], spf)],
              reads=[triu_f.d, ones_f.d, sp.d], writes=[PD[ML]])
        negb = PB[ML][:nt, 128:128 + w]
        nbt = PB[ML][:, 192:192 + w]

        def fl(x):
            return x.t[:nt, 0:ntl, :].rearrange("p t h -> p (t h)")

        def flA(x):
            return x.t[:, 0:ntl, :].rearrange("p t h -> p (t h)")
        a, nB, amb, wA, wB, eb, eB = (gs[k] for k in ("a", "nB", "amb", "wA", "wB", "eb", "eB"))
        cx.op("dve", lambda e: e.tensor_tensor(fl(a), fl(ig), negb, ALU.add), reads=[ig.d], writes=[PD[ML], a.d])
        cx.op("dve", lambda e: e.tensor_copy(flA(nB), nbt), writes=[PD[ML], nB.d])
        cx.op("act", lambda e: e.activation(fl(eb), negb, AF.Exp, scale=-1.0), writes=[PD[ML], eb.d])
        cx.op("dve", lambda e: e.tensor_tensor(fl(amb), fl(a), fl(nB), ALU.subtract), reads=[a.d, nB.d], writes=[amb.d])
        cx.op("act", lambda e: e.activation(fl(wA), fl(a), AF.Exp, bias=eps_t.t[:nt, 1:2]), reads=[a.d, eps_t.d], writes=[wA.d])
        cx.op("act", lambda e: e.activation(fl(wB), fl(amb), AF.Exp, bias=eps_t.t[:nt, 1:2]), reads=[amb.d, eps_t.d], writes=[wB.d])
        cx.op("act", lambda e: e.activation(flA(eB), flA(nB), AF.Exp, scale=-1.0), reads=[nB.d], writes=[eB.d])

    def m_track(gs, ti, nt):
        a, nB = gs["a"], gs["nB"]
        t = small()
        cx.op("dve", lambda e: e.tensor_tensor(t.t[:nt, 0:4], a.t[:nt, ti, :], ngrun.t[:nt, :], ALU.add),
              reads=[a.d, ngrun.d], writes=[t.d])
        cx.op("dve", lambda e: e.tensor_tensor(amaxr.t[:nt, :], amaxr.t[:nt, :], t.t[:nt, 0:4], ALU.max),
              reads=[t.d, amaxr.d], writes=[amaxr.d])
        cx.op("dve", lambda e: e.tensor_tensor(ngrun.t[:, :], ngrun.t[:, :], nB.t[:, ti, :], ALU.add),
              reads=[nB.d, ngrun.d], writes=[ngrun.d])

    def new_gs(es_, ntl):
        return {k: TD(es_.enter_context(nc.sbuf_tensor(nm("gs_" + k), [128, ntl, 4], F32)), "gs_" + k)
                for k in ("ig", "sp", "a", "nB", "amb", "wA", "wB", "eb", "eB")}

    def state_update(K3, VB1, ti, nt, gs, with_bf, C32=C32, CBF=CBF):
        for h in range(4):
            cx.op("pe", mm(PB[ML][:, 256:385], K3.t[:nt, ti, h, :], VB1.t[:nt, ti, h, :]),
                  reads=[K3.d, VB1.d], writes=[PD[ML]])
            cx.op("dve", lambda e, h=h: e.scalar_tensor_tensor(C32.t[:, h, :], C32.t[:, h, :], gs["eB"].t[:, ti, h:h + 1],
                                                              PB[ML][:, 256:385], ALU.mult, ALU.add),
                  reads=[gs["eB"].d], writes=[PD[ML], C32.d])
        if with_bf:
            cx.op("act", lambda e: e.activation(CBF.t[:].rearrange("p h e -> p (h e)"),
                                                C32.t[:].rearrange("p h e -> p (h e)"), AF.Copy),
                  reads=[C32.d], writes=[CBF.d])

    def evac_k3(hT, col0, nt, ti, wkb, gs, K3):
        b = tm_proj(lambda k: wkb.t[:, k, :], wkb.d, hT, col0, nt, 512)
        cx.op("dve", lambda e: e.tensor_tensor(K3.t[:nt, ti, :, :], PB[b][:nt, :].rearrange("p (h d) -> p h d", h=4),
                                               gs["wB"].t[:nt, ti, :].unsqueeze(2).to_broadcast([nt, 4, 128]), ALU.mult),
              reads=[gs["wB"].d], writes=[PD[b], K3.d])

    def evac_vb1(hT, col0, nt, ti, wvb, VB1):
        b = tm_proj(lambda k: wvb.t[:, k, :], wvb.d, hT, col0, nt, 512)
        cx.op("act", lambda e: e.activation(VB1.t[:nt, ti, :, 0:128], PB[b][:nt, :].rearrange("p (h d) -> p h d", h=4), AF.Copy),
              writes=[PD[b], VB1.d])
        cx.op("dve", lambda e: e.memset(VB1.t[:nt, ti, :, 128:129], 1.0), writes=[VB1.d])

    try:
      for l in range(2):
          stop_at(1 + 10 * l)
          xin = XP if l == 0 else X1
          xin_d = None if l == 0 else d_x1
          xout = X1 if l == 0 else YP
          xsin = XS if l == 0 else XS1
          xsout = XS1 if l == 0 else YS

          sem_par = cx.shared_sem("par")
          cx.dma("sp", gbc.t[:].rearrange("p a d -> p (a d)"), NG.ap()[l].rearrange("a d -> (a d)").partition_broadcast(128),
                 writes=[gbc.d], sem=sem_par)
          cx.dma("sp", gatt.t[:], GATT.ap()[l].partition_broadcast(128), writes=[gatt.d], sem=sem_par)
          cx.dma("sp", gml.t[:], GML.ap()[l].partition_broadcast(128), writes=[gml.d], sem=sem_par)
          cx.dma("sp", bif.t[:, 0:4], BI.ap()[l].partition_broadcast(128), writes=[bif.d], sem=sem_par)
          cx.dma("sp", bif.t[:, 4:8], BFG.ap()[l].partition_broadcast(128), writes=[bif.d], sem=sem_par)
          cx.dma("pool", wg.t[:], WIN.ap()[l, :, C_G:C_G + 8].rearrange("(k p) n -> p k n", p=128), writes=[wg.d])

          with contextlib.ExitStack() as es_:
              def lt(name, shape, dt=F32):
                  return TD(es_.enter_context(nc.sbuf_tensor(nm(name), list(shape), dt)), name)
              mm_ring[0] = [MM0, MM1, TRB]
              XT = [lt(f"a_xt{i}", [128, D]) for i in range(2)]
              hTA = lt("a_hTA", [128, 8, TOK], BF16)
              sqs = [lt(f"a_sq{i}", [128, D]) for i in range(2)]; xns = [lt(f"a_xn{i}", [128, D], BF16) for i in range(2)]
              wkb = lt("a_wkb", [128, 8, 512], BF16); wvb = lt("a_wvb", [128, 8, 512], BF16)
              K3L = [lt(f"a_k3{i}", [128, 4, 128], BF16) for i in range(2)]
              VB1L = [lt(f"a_vb1{i}", [128, 4, 129], BF16) for i in range(2)]
              khT = lt("a_khT", [128, 4, 512], BF16); vh = lt("a_vh", [128, 4, 512], BF16)
              gs = new_gs(es_, NTILE)
              pre = lt("a_pre", [128, NTILE, 4]); wC = lt("a_wC", [128, NTILE, 4])
              cx.dma("pool", wkb.t[:], WIN.ap()[l, :, C_KB:C_KB + 512].rearrange("(k p) n -> p k n", p=128), writes=[wkb.d])
              cx.dma("pool", wvb.t[:], WIN.ap()[l, :, C_VB:C_VB + 512].rearrange("(k p) n -> p k n", p=128), writes=[wvb.d])
              ws.schedule([win_loader(l, C_KA), win_loader(l, C_VA)])
              XT4 = XT + sqs
              jk = [lt(f"a_jk{i}", [128, D], BF16) for i in range(2)]
              items = []
              for t in range(NTILE):
                  xt = XT4[t % 4]
                  items.append((xt, 128, t * 128))
              for t0 in range(0, NTILE, 4):
                  for t in range(t0, t0 + 4):
                      cx.dma("sp", XT4[t % 4].t[:, :], xin.ap()[t * 128:(t + 1) * 128, :], writes=[XT4[t % 4].d])
                  norm_seq(items[t0:t0 + 4], 0, hTA, xns, jk)
              gate_math(hTA, [(t * 128, 128) for t in range(NTILE)], gs)
              cx.op("dve", lambda e: e.memset(pre.t[:, 0, :], 0.0), writes=[pre.d])
              for t in range(1, NTILE):
                  cx.op("dve", lambda e, t=t: e.tensor_tensor(pre.t[:, t, :], pre.t[:, t - 1, :], gs["nB"].t[:, t - 1, :], ALU.add),
                        reads=[gs["nB"].d], writes=[pre.d])
              cx.op("dve", lambda e: e.tensor_tensor(ngrun.t[:, :], pre.t[:, NTILE - 1, :], gs["nB"].t[:, NTILE - 1, :], ALU.add),
                    reads=[gs["nB"].d, pre.d], writes=[ngrun.d])
              cx.op("dve", lambda e: e.tensor_tensor(gs["amb"].t[:], gs["a"].t[:], pre.t[:], ALU.add),
                    reads=[gs["a"].d, pre.d], writes=[gs["amb"].d])
              cx.op("dve", lambda e: e.tensor_reduce(amaxr.t[:, :], gs["amb"].t[:].rearrange("p t h -> p h t"), AX.X, ALU.max),
                    reads=[gs["amb"].d], writes=[amaxr.d])
              cx.op("dve", lambda e: e.tensor_tensor(pre.t[:], pre.t[:], ngrun.t[:, :].unsqueeze(1).to_broadcast([128, NTILE, 4]), ALU.subtract),
                    reads=[ngrun.d], writes=[pre.d])
              cx.op("dve", lambda e: e.tensor_tensor(gs["amb"].t[:], gs["a"].t[:], pre.t[:], ALU.add),
                    reads=[gs["a"].d, pre.d], writes=[gs["amb"].d])
              cx.op("act", lambda e: e.activation(wC.t[:].rearrange("p t h -> p (t h)"), gs["amb"].t[:].rearrange("p t h -> p (t h)"),
                                                  AF.Exp, bias=eps_t.t[:, 1:2]), reads=[gs["amb"].d, eps_t.d], writes=[wC.d])
              CB = [S0, S1, O0, O1]
              for t in range(NTILE):
                  K3, VB1 = K3L[t % 2], VB1L[t % 2]
                  b = tm_proj(lambda k: wkb.t[:, k, :], wkb.d, hTA, t * 128, 128, 512)
                  cx.op("dve", lambda e, t=t, b=b, K3=K3: e.tensor_tensor(K3.t[:, :, :], PB[b][:, :].rearrange("p (h d) -> p h d", h=4),
                                                                       wC.t[:, t, :].unsqueeze(2).to_broadcast([128, 4, 128]), ALU.mult),
                        reads=[wC.d], writes=[PD[b], K3.d])
                  b = tm_proj(lambda k: wvb.t[:, k, :], wvb.d, hTA, t * 128, 128, 512)
                  cx.op("act", lambda e, b=b, VB1=VB1: e.activation(VB1.t[:, :, 0:128], PB[b][:, :].rearrange("p (h d) -> p h d", h=4), AF.Copy),
                        writes=[PD[b], VB1.d])
                  cx.op("dve", lambda e, VB1=VB1: e.memset(VB1.t[:, :, 128:129], 1.0), writes=[VB1.d])
                  for h in range(4):
                      cx.op("pe", mm(PB[CB[h]][:, 0:129], K3.t[:, h, :], VB1.t[:, h, :], t == 0, t == NTILE - 1),
                            reads=[K3.d, VB1.d], writes=[PD[CB[h]]])
              for h in range(4):
                  cx.op("dve" if h % 2 else "act",
                        (lambda e, h=h: e.tensor_copy(C32.t[:, h, :], PB[CB[h]][:, 0:129])) if h % 2 else
                        (lambda e, h=h: e.activation(C32.t[:, h, :], PB[CB[h]][:, 0:129], AF.Copy)),
                        writes=[PD[CB[h]], C32.d])
              build_E(l, es_)
              hT = TD(None, "hT_alias"); hT.t = hTA.t[:, :, TOK - 512:TOK]; hT.d = hTA.d
              wka = ws.get()
              fm_proj(wka, range(4), hT, 512, khT, 0, 0)
              cx.dma("sp", EXK_S.ap()[:, :].rearrange("(c p) n -> p c n", p=128), khT.t[:], reads=[khT.d],
                     writes=[d_exk[0]], sem=sem_out)
              wva = ws.get()
              for i in range(4):
                  b = tm_proj(lambda k: wva.t[:, k, :], wva.d, hT, i * 128, 128, 512)
                  evac(vh.t[:, i, :], PB[b][:, :], b, vh.d)
              cx.dma("sp", EXV_S.ap()[:, :].rearrange("(c p) n -> p c n", p=128), vh.t[:], reads=[vh.d],
                     writes=[d_exv[0]], sem=sem_out)
              cx.dma("sp", EXS_S.ap()[0:128, 0:516], C32.t[:].rearrange("p h e -> p (h e)"), reads=[C32.d],
                     writes=[d_exs[0]], sem=sem_out)
              srow = lt("a_srow", [128, XCOLS])
              cx.op("dve", lambda e: e.memset(srow.t[0:1, :], 0.0), writes=[srow.d])
              cx.op("dve", lambda e: e.tensor_copy(srow.t[0:1, 0:4], ngrun.t[0:1, :]), reads=[ngrun.d], writes=[srow.d])
              cx.op("pe", lambda e: e.transpose(PB[ML][0:4, 0:128], amaxr.t[:, 0:4], ident_f.t[:, :]),
                    reads=[amaxr.d, ident_f.d], writes=[PD[ML]])
              mu4 = small()
              cx.op("dve", lambda e: e.tensor_reduce(mu4.t[0:4, 0:1], PB[ML][0:4, 0:128], AX.X, ALU.max),
                    writes=[PD[ML], mu4.d])
              cx.op("pe", mm(PB[ML][0:1, 200:204], mu4.t[0:4, 0:1], ident_f.t[0:4, 0:4]), reads=[mu4.d, ident_f.d], writes=[PD[ML]])
              cx.op("dve", lambda e: e.tensor_copy(srow.t[0:1, 4:8], PB[ML][0:1, 200:204]), writes=[PD[ML], srow.d])
              cx.dma("sp", EXS_S.ap()[128:129, :], srow.t[0:1, :], reads=[srow.d], writes=[d_exs[0]], sem=sem_out)
              stop_at(2 + 10 * l)
              cx.allgather(EXK_S.ap().opt(), EXK_D.ap().opt(), [d_exk[0]], [d_exk[1]], sem_cc)
              cx.allgather(EXV_S.ap().opt(), EXV_D.ap().opt(), [d_exv[0]], [d_exv[1]], sem_cc)
              cx.allgather(EXS_S.ap().opt(), EXS_D.ap().opt(), [d_exs[0]], [d_exs[1]], sem_cc)
              stop_at(3 + 10 * l)
              cx.barrier(exclude=(sem_cc.key,))

          with contextlib.ExitStack() as es_:
              def lt(name, shape, dt=F32):
                  return TD(es_.enter_context(nc.sbuf_tensor(nm(name), list(shape), dt)), name)
              KTA = lt("b_kta", [128, 4, 1024], BF16)
              VA = lt("b_va", [128, 8, 8, 65], BF16)
              def consume_exchange():
                  kxs = PT.t[:, 0:16, :].rearrange("p (c a) q -> p c (a q)", c=4)
                  vxs = PT.t[:, 16:32, :].rearrange("p (c a) q -> p c (a q)", c=4)
                  cxs = yb.t[:, 0:516]
                  scbt = small()
                  scb = scbt.t[:, 0:32].rearrange("p (w r h) -> p w r h", w=2, r=4)
                  for w_ in range(2):
                      cx.dma("sp", scb[:, w_, :, :], bass.AP(EXS_D, 128 * XCOLS + 4 * w_, [[0, 128], [XROWS * XCOLS, 4], [1, 4]]),
                             reads=[d_exs[1]], writes=[scbt.d])
                  er = small()
                  cx.op("act", lambda e: e.activation(er.t[:, 0:16], scb[:, 0, :, :].rearrange("p r h -> p (r h)"), AF.Exp, scale=-1.0),
                        reads=[scbt.d], writes=[er.d])
                  cx.op("dve", lambda e: e.tensor_scalar(er.t[:, 0:16], er.t[:, 0:16], -1.0, None, ALU.add), reads=[er.d], writes=[er.d])
                  cx.op("dve", lambda e: e.tensor_tensor(er.t[:, 0:16].rearrange("p (r h) -> p r h", r=4),
                                                         er.t[:, 0:16].rearrange("p (r h) -> p r h", r=4),
                                                         cm.t[:, 4:8].unsqueeze(2).to_broadcast([128, 4, 4]), ALU.mult),
                        reads=[er.d, cm.d], writes=[er.d])
                  cx.op("dve", lambda e: e.tensor_scalar(er.t[:, 0:16], er.t[:, 0:16], 1.0, None, ALU.add), reads=[er.d], writes=[er.d])
                  cx.op("dve", lambda e: e.memset(C32.t[:].rearrange("p h e -> p (h e)"), 0.0), writes=[C32.d])
                  kdst = KTA.t[:, :, 512:1024]
                  for r in range(3):
                      cx.dma("sp", kxs, EXK_D.ap()[r * 512:(r + 1) * 512, :].rearrange("(c p) n -> p c n", p=128),
                             reads=[d_exk[1]], writes=[PT.d], sem=cx.shared_sem("xst"))
                      cx.dma("sp", vxs, EXV_D.ap()[r * 512:(r + 1) * 512, :].rearrange("(c p) n -> p c n", p=128),
                             reads=[d_exv[1]], writes=[PT.d], sem=cx.shared_sem("xst"))
                      if r < 3:
                          cx.dma("sp", cxs, EXS_D.ap()[r * XROWS:r * XROWS + 128, 0:516], reads=[d_exs[1]], writes=[yb.d],
                                 sem=cx.shared_sem("xst"))
                      if r == 0:
                          cx.op("dve", lambda e, r=r: e.tensor_scalar(kdst, kxs, cm.t[:, r:r + 1], None, ALU.mult),
                                reads=[PT.d, cm.d], writes=[KTA.d])
                      else:
                          cx.op("dve", lambda e, r=r: e.scalar_tensor_tensor(kdst, kxs, cm.t[:, r:r + 1], kdst, ALU.mult, ALU.add),
                                reads=[PT.d, cm.d], writes=[KTA.d])
                      for tt_ in range(4):
                          vsrc = vxs[:, tt_, :].rearrange("p (h d) -> p h d", h=8)
                          vdst = VA.t[:, 4 + tt_, :, 0:64]
                          if r == 0:
                              cx.op("dve", lambda e, r=r, vsrc=vsrc, vdst=vdst: e.tensor_scalar(vdst, vsrc, cm.t[:, r:r + 1], None, ALU.mult),
                                    reads=[PT.d, cm.d], writes=[VA.d])
                          else:
                              cx.op("dve", lambda e, r=r, vsrc=vsrc, vdst=vdst: e.scalar_tensor_tensor(
                                  vdst, vsrc, cm.t[:, r:r + 1], vdst, ALU.mult, ALU.add), reads=[PT.d, cm.d], writes=[VA.d])
                      if r < 3:
                          for h in range(4):
                              cx.op("dve", lambda e, r=r, h=h: e.tensor_scalar(C32.t[:, h, :], C32.t[:, h, :], er.t[:, r * 4 + h:r * 4 + h + 1],
                                                                              None, ALU.mult), reads=[er.d], writes=[C32.d])
                              cx.op("dve", lambda e, r=r, h=h: e.scalar_tensor_tensor(C32.t[:, h, :], cxs[:, h * 129:(h + 1) * 129],
                                                                                     cm.t[:, 4 + r:5 + r], C32.t[:, h, :], ALU.mult, ALU.add),
                                    reads=[yb.d, cm.d], writes=[C32.d])
                  fl_ = small()
                  cx.op("dve", lambda e: e.tensor_reduce(fl_.t[:, 0:1], cm.t[:, 0:4], AX.X, ALU.add), reads=[cm.d], writes=[fl_.d])
                  for tt_ in range(4):
                      cx.op("dve", lambda e, tt_=tt_: e.tensor_copy(VA.t[:, 4 + tt_, :, 64:65],
                                                                   fl_.t[:, 0:1].unsqueeze(1).to_broadcast([128, 8, 1])),
                            reads=[fl_.d], writes=[VA.d])
                  cx.op("act", lambda e: e.activation(CBF.t[:].rearrange("p h e -> p (h e)"),
                                                      C32.t[:].rearrange("p h e -> p (h e)"), AF.Copy), reads=[C32.d], writes=[CBF.d])
                  mt = small()
                  cx.op("dve", lambda e: e.memset(mt.t[:, 0:16], 0.0), writes=[mt.d])
                  for r in range(4):
                      cx.op("dve", lambda e, r=r: e.tensor_tensor(mt.t[:, 12:16], scb[:, 1, r, :], mt.t[:, 0:4], ALU.add),
                            reads=[scbt.d], writes=[mt.d])
                      cx.op("dve", lambda e, r=r: e.tensor_scalar(mt.t[:, 16:17], cm.t[:, 8 + r:9 + r], -1.0, -NEG, ALU.add, ALU.mult),
                            reads=[cm.d], writes=[mt.d])
                      cx.op("dve", lambda e: e.tensor_scalar(mt.t[:, 12:16], mt.t[:, 12:16], mt.t[:, 16:17], None, ALU.add),
                            writes=[mt.d])
                      cx.op("dve", lambda e: e.tensor_tensor(mt.t[:, 4:8], mt.t[:, 4:8], mt.t[:, 12:16], ALU.max), writes=[mt.d])
                      cx.op("dve", lambda e, r=r: e.tensor_tensor(mt.t[:, 0:4], mt.t[:, 0:4], scb[:, 0, r, :], ALU.add),
                            reads=[scbt.d], writes=[mt.d])
                      cx.op("dve", lambda e, r=r: e.scalar_tensor_tensor(mt.t[:, 8:12], scb[:, 0, r, :], cm.t[:, 8 + r:9 + r],
                                                                        mt.t[:, 8:12], ALU.mult, ALU.add),
                            reads=[scbt.d, cm.d], writes=[mt.d])
                  mE = ME[l]
                  cx.op("dve", lambda e: e.tensor_tensor(mE.t[:, 0:4], mt.t[:, 4:8], mt.t[:, 8:12], ALU.subtract),
                        reads=[mt.d], writes=[mE.d])
                  cx.op("act", lambda e: e.activation(mE.t[:, 4:8], mE.t[:, 0:4], AF.Exp, scale=-1.0), reads=[mE.d], writes=[mE.d])
                  cx.dma("sp", MP.ap()[l:l + 1, :], mE.t[0:1, 0:4], reads=[mE.d], writes=[d_out], sem=sem_out)

              hT = lt("b_hT", [128, 8, 512], BF16); sq = lt("b_sq", [128, D]); xn = lt("b_xn", [128, D], BF16)
              xn2 = lt("b_xn2", [128, D], BF16); jk0 = TD(None, "jk0"); jk0.t = sq.t[:, 0:512].bitcast(BF16); jk0.d = sq.d
              jk1 = TD(None, "jk1"); jk1.t = sq.t[:, 512:1024].bitcast(BF16); jk1.d = sq.d
              QTA = lt("b_qta", [128, 8, 512], BF16)
              QTB = lt("b_qtb", [128, 4, 512], BF16); KTB = lt("b_ktb", [128, 4, 512], BF16)
              K3 = lt("b_k3", [128, 4, 4, 128], BF16); VB1 = lt("b_vb1", [128, 4, 4, 129], BF16)
              OG = lt("b_og", [128, 4, 512])
              PT = lt("b_pt", [128, 40, 128], BF16)
              wout = lt("b_wout", [128, 8, D], BF16)
              att = lt("b_att", [128, 8, 64]); mix = lt("b_mix", [128, D], BF16); mix2 = lt("b_mix2", [128, D], BF16); mixT = lt("b_mixT", [128, 8, 128], BF16)
              yb = lt("b_y", [128, D]); hh = lt("b_hh", [128, 4, 128]); GT = lt("b_gt", [128, 4, 128], BF16)
              XK = [lt(f"b_xk{i}", [128, D]) for i in range(4)]
              gs = new_gs(es_, 4)
              cx.dma("pool", wout.t[:], WOUT.ap()[l].rearrange("(k p) n -> p k n", p=128), writes=[wout.d])
              cx.op("dve", lambda e: e.memset(QTA.t[:].rearrange("p h q -> p (h q)"), 0.0), writes=[QTA.d])
              ov = [PB[O0][:, 0:260].rearrange("p (h e) -> p h e", h=4), PB[O1][:, 0:260].rearrange("p (h e) -> p h e", h=4)]
              numv = [PB[O0][:, :].rearrange("p (h e) -> p h e", h=2), PB[O1][:, :].rearrange("p (h e) -> p h e", h=2)]
              trv = PB[TRB][:].bitcast(BF16).rearrange("p (k c) -> p k c", k=8)

              if do_sample:
                  hTs = lt("s_hT", [128, 8, 16], BF16)
                  QTAs = lt("s_qta", [128, 8, 16], BF16); KTAs = lt("s_kta", [128, 4, 16], BF16)
                  QTBs = lt("s_qtb", [128, 4, 16], BF16); KTBs = lt("s_ktb", [128, 4, 16], BF16)
                  K3s = lt("s_k3", [128, 1, 4, 128], BF16); VB1s = lt("s_vb1", [128, 1, 4, 129], BF16)
                  OGs = lt("s_og", [128, 1, 512]); VAs = lt("s_va", [128, 8, 65], BF16)
                  KcT = lt("s_kct", [128, 4, 512], BF16); Vc = lt("s_vc", [128, 4, 8, 65], BF16)
                  ckb = TD(None, "ckb_alias"); ckb.t = None; ckb.d = None
                  PTs = lt("s_pt", [128, 8, 5, 16], BF16)
                  C32s = lt("s_c32", [128, 4, 129]); CBFs = lt("s_cbf", [128, 4, 129], BF16)
                  m0b = lt("s_m0", [128, 16]); XKs = lt("s_xk", [128, D])
                  gss = new_gs(es_, 1)
                  cx.op("dve", lambda e: e.memset(QTAs.t[:].rearrange("p h q -> p (h q)"), 0.0), writes=[QTAs.d])
                  def sample_setup():
                      ckb.t = yb.t[:, :].bitcast(BF16).rearrange("p (j q) -> p j q", j=4)
                      ckb.d = yb.d
                      for j in range(4):
                          cx.dma("pool", ckb.t[:, j, :].rearrange("p (h d) -> p h d", h=8),
                                 CK.ap()[l, :, j * 128:(j + 1) * 128, :].rearrange("h t d -> t h d"), writes=[ckb.d])
                          cx.dma("pool", Vc.t[:, j, :, 0:64], CV.ap()[l, :, j * 128:(j + 1) * 128, :].rearrange("h t d -> t h d"),
                                 writes=[Vc.d])
                          cx.op("dve", lambda e, j=j: e.memset(Vc.t[:, j, :, 64:65], 1.0), writes=[Vc.d])
                      for jj in range(2):
                          cx.op("pe", [(lambda e, a_=a_, hp_=hp_: e.transpose(trv[:, a_ * 4 + hp_, :],
                                                                               ckb.t[:, 2 * jj + a_, hp_ * 128:(hp_ + 1) * 128], ident_b.t[:, :]))
                                       for a_ in range(2) for hp_ in range(4)], reads=[ckb.d, ident_b.d], writes=[PD[TRB]])
                          for a_ in range(2):
                              j = 2 * jj + a_
                              cx.op("act", lambda e, a_=a_, j=j: e.activation(KcT.t[:, :, j * 128:(j + 1) * 128], trv[:, a_ * 4:(a_ + 1) * 4, :], AF.Copy),
                                    writes=[PD[TRB], KcT.d])
                      cx.dma("sp", C32s.t[:, :, 0:128], SC.ap()[l].rearrange("h d e -> d h e"), writes=[C32s.d])
                      with nc.allow_non_contiguous_dma(reason="n state column"):
                          cx.dma("sp", C32s.t[:, :, 128], SN.ap()[l].rearrange("h d -> d h"), writes=[C32s.d])
                      cx.dma("sp", m0b.t[:, 0:4], SM.ap()[l].partition_broadcast(128), writes=[m0b.d])
                      cx.op("act", lambda e: e.activation(m0b.t[:, 4:8], m0b.t[:, 0:4], AF.Exp), reads=[m0b.d], writes=[m0b.d])
                      for h in range(4):
                          cx.op("dve", lambda e, h=h: e.tensor_scalar(C32s.t[:, h, :], C32s.t[:, h, :], m0b.t[:, 4 + h:5 + h], None, ALU.mult),
                                reads=[m0b.d], writes=[C32s.d])
                      cx.op("act", lambda e: e.activation(CBFs.t[:].rearrange("p h e -> p (h e)"),
                                                          C32s.t[:].rearrange("p h e -> p (h e)"), AF.Copy), reads=[C32s.d], writes=[CBFs.d])

              PTD = [Dep(f"pt{n_}") for n_ in range(10)]
              GTD = [Dep(f"gt{h_}") for h_ in range(4)]

              def attn_prompt(tg, c0, slices):
                  def S(n):
                      sbk = S0 if n % 2 == 0 else S1
                      fns = []
                      for q in range(4):
                          blk = n * 4 + q
                          h, j = blk // 5, blk % 5
                          kt = (tg - 4 + j) % 8
                          fns.append(mm(PB[sbk][:, q * 128:(q + 1) * 128], KTA.t[:, h // 2, kt * 128:(kt + 1) * 128],
                                        QTA.t[:, h, c0:c0 + 128]))
                      cx.op("pe", fns, reads=[KTA.d, QTA.d], writes=[PD[sbk]])
                      return sbk

                  def pv(h):
                      ob = O0 if h < 4 else O1
                      pairs = []
                      for j in range(5):
                          kt = (tg - 4 + j) % 8
                          pairs.append((PT.t[:, h * 5 + j, :], VA.t[:, kt, h, :]))
                      n0, n1 = (5 * h) // 4, (5 * h + 4) // 4
                      cx.op("pe", acc_group(ov[h // 4][:, h % 4, :], pairs), reads=[PTD[n0], PTD[n1], VA.d], writes=[PD[ob]])
                  done_h = 0
                  sbk = S(0)
                  for n in range(10):
                      nxt = S(n + 1) if n + 1 < 10 else None
                      ptv = PT.t[:, n * 4:(n + 1) * 4, :].rearrange("p b q -> p (b q)")
                      cx.op("act", lambda e, sbk=sbk, ptv=ptv: e.activation(ptv, PB[sbk][:, :], AF.Exp, scale=0.125),
                            writes=[PD[sbk], PTD[n]])
                      ev = EALL[l].t[:].rearrange("p h j q -> p (h j) q")[:, n * 4:(n + 1) * 4, :].rearrange("p b q -> p (b q)")
                      cx.op("dve", lambda e, ptv=ptv, ev=ev: e.tensor_tensor(ptv, ptv, ev, ALU.mult), reads=[EALL[l].d], writes=[PTD[n]])
                      while (done_h + 1) * 5 <= (n + 1) * 4:
                          pv(done_h)
                          done_h += 1
                      if slices:
                          slices.pop(0)()
                      sbk = nxt
                  while slices:
                      slices.pop(0)()

              def attn_sample():
                  for hb, sbk in ((0, S0), (1, S1)):
                      fns = []
                      for hq in range(4):
                          h = hb * 4 + hq
                          for j in range(4):
                              fns.append(mm(PB[sbk][:, hq * 80 + j * 16:hq * 80 + j * 16 + 16], KcT.t[:, h // 2, j * 128:(j + 1) * 128],
                                            QTAs.t[:, h, 0:16]))
                          fns.append(mm(PB[sbk][0:16, hq * 80 + 64:hq * 80 + 80], KTAs.t[:, h // 2, 0:16], QTAs.t[:, h, 0:16]))
                      cx.op("pe", fns, reads=[KcT.d, KTAs.d, QTAs.d], writes=[PD[sbk]])
                      ptv = PTs.t[:, hb * 4:(hb + 1) * 4, :, :].rearrange("p h j q -> p (h j q)")
                      cx.op("act", lambda e, sbk=sbk, ptv=ptv: e.activation(ptv, PB[sbk][:, 0:320], AF.Exp, scale=0.125),
                            writes=[PD[sbk], PTs.d])
                  cx.op("dve", lambda e: e.tensor_tensor(PTs.t[:, :, :, :].rearrange("p h j q -> p (h j) q"),
                                                         PTs.t[:, :, :, :].rearrange("p h j q -> p (h j) q"),
                                                         EALL[l].t[:].rearrange("p h j q -> p (h j) q")[:, :, 0:16], ALU.mult),
                        reads=[EALL[l].d], writes=[PTs.d])
                  for h in range(8):
                      ob = O0 if h < 4 else O1
                      pairs = [(PTs.t[:, h, j, :], Vc.t[:, j, h, :]) for j in range(4)]
                      pairs.append((PTs.t[0:16, h, 4, :], VAs.t[0:16, h, :]))
                      cx.op("pe", acc_group(ov[h // 4][0:16, h % 4, :], pairs), reads=[PTs.d, Vc.d, VAs.d], writes=[PD[ob]])

              def attn_epi(nt, mix):
                  rd = small()
                  for hb, ob in ((0, O0), (1, O1)):
                      cx.op("dve", lambda e, hb=hb: e.reciprocal(rd.t[:nt, hb * 4:hb * 4 + 4], ov[hb][:nt, :, 64]),
                            writes=[PD[ob], rd.d])
                      cx.op("dve", lambda e, hb=hb: e.tensor_tensor(att.t[:nt, hb * 4:hb * 4 + 4, :], ov[hb][:nt, :, 0:64],
                                                                    rd.t[:nt, hb * 4:hb * 4 + 4].unsqueeze(2).to_broadcast([nt, 4, 64]),
                                                                    ALU.mult), reads=[rd.d], writes=[PD[ob], att.d])
                  attf = att.t[:nt].rearrange("p h d -> p (h d)")
                  st = small()
                  cx.op("act", lambda e: e.activation(sq.t[:nt, 0:512], attf, AF.Square, accum_out=st.t[:nt, 0:1]),
                        reads=[att.d], writes=[sq.d, st.d])
                  rstd_from_ss(st, nt, 0, 1, 1.0 / 512)
                  cx.op("dve", lambda e: e.scalar_tensor_tensor(mix.t[:nt, 0:512], attf, st.t[:nt, 1:2], gatt.t[:nt, :], ALU.mult, ALU.mult),
                        reads=[att.d, st.d, gatt.d], writes=[mix.d])

              numv = [PB[MM0][:, :].rearrange("p (h e) -> p h e", h=2), PB[MM1][:, :].rearrange("p (h e) -> p h e", h=2)]
              NB_ = (MM0, MM1)

              def make_slices(nt, c0, ti, QTBx, KTBx, K3x, VB1x, OGx, ogd, gsx, C32x, CBFx, track, xk, after, mix):
                  sl = []

                  def head(h):
                      def f():
                          cx.op("pe", mm(PB[ML][:nt, 0:nt], KTBx.t[:, h, c0:c0 + nt], QTBx.t[:, h, c0:c0 + nt]),
                                reads=[KTBx.d, QTBx.d], writes=[PD[ML]])
                          cx.op("dve", lambda e: e.scalar_tensor_tensor(GT.t[:nt, h, :nt], PB[ML][:nt, 0:nt],
                                                                        gsx["wA"].t[:nt, ti, h:h + 1], masku_b.t[:nt, :nt],
                                                                        ALU.mult, ALU.mult),
                                reads=[gsx["wA"].d, masku_b.d], writes=[PD[ML], GTD[h]])
                          cx.op("pe", [mm(numv[h // 2][:nt, h % 2, 0:129], GT.t[:nt, h, :nt], VB1x.t[:nt, ti, h, :], True, False),
                                       mm(numv[h // 2][:nt, h % 2, 0:129], QTBx.t[:, h, c0:c0 + nt], CBFx.t[:, h, :], False, True)],
                                reads=[GTD[h], VB1x.d, QTBx.d, CBFx.d], writes=[PD[NB_[h // 2]]])
                      return f
                  for h in range(4):
                      sl.append(head(h))

                  def st_upd():
                      if track:
                          m_track(gsx, ti, nt)
                      state_update(K3x, VB1x, ti, nt, gsx, True, C32x, CBFx)
                  sl.append(st_upd)

                  def hnorm():
                      dn = small()
                      for hb in range(2):
                          cx.op("dve", lambda e, hb=hb: e.tensor_tensor(dn.t[:nt, hb * 2:hb * 2 + 2], numv[hb][:nt, :, 128],
                                                                        gsx["eb"].t[:nt, ti, hb * 2:hb * 2 + 2], ALU.mult),
                                reads=[gsx["eb"].d], writes=[PD[NB_[hb]], dn.d])
                      cx.op("act", lambda e: e.activation(dn.t[:nt, 0:4], dn.t[:nt, 0:4], AF.Abs), writes=[dn.d])
                      cx.op("dve", lambda e: e.tensor_scalar(dn.t[:nt, 0:4], dn.t[:nt, 0:4], 1.0, None, ALU.max), writes=[dn.d])
                      cx.op("dve", lambda e: e.reciprocal(dn.t[:nt, 0:4], dn.t[:nt, 0:4]), writes=[dn.d])
                      cx.op("dve", lambda e: e.tensor_tensor(dn.t[:nt, 4:8], dn.t[:nt, 0:4], gsx["eb"].t[:nt, ti, :], ALU.mult),
                            reads=[gsx["eb"].d], writes=[dn.d])
                      for hb in range(2):
                          cx.op("dve", lambda e, hb=hb: e.tensor_tensor(hh.t[:nt, hb * 2:hb * 2 + 2, :], numv[hb][:nt, :, 0:128],
                                                                        dn.t[:nt, 4 + hb * 2:6 + hb * 2].unsqueeze(2).to_broadcast([nt, 2, 128]),
                                                                        ALU.mult), reads=[dn.d], writes=[PD[NB_[hb]], hh.d])
                      hhf = hh.t[:nt].rearrange("p h d -> p (h d)")
                      cx.op("act", lambda e: e.activation(sq.t[:nt, 0:512], hhf, AF.Square), reads=[hh.d], writes=[sq.d])
                      s4 = small()
                      cx.op("dve", lambda e: e.tensor_reduce(s4.t[:nt, 0:4], sq.t[:nt, 0:512].rearrange("p (h d) -> p h d", h=4), AX.X, ALU.add),
                            reads=[sq.d], writes=[s4.d])
                      rstd_from_ss(s4, nt, 0, 4, 1.0 / 128, width=4)
                      cx.op("dve", lambda e: e.tensor_tensor(hh.t[:nt], hh.t[:nt], s4.t[:nt, 4:8].unsqueeze(2).to_broadcast([nt, 4, 128]), ALU.mult),
                            reads=[s4.d], writes=[hh.d])
                      cx.op("dve", lambda e: e.tensor_tensor(mix.t[:nt, 512:1024], hhf, OGx, ALU.mult),
                            reads=[hh.d, ogd], writes=[mix.d])
                  sl.append(hnorm)

                  def tr():
                      transpose_to(mix, nt, 8, mixT, 0, eng="dve")
                  sl.append(tr)

                  def wo(half):
                      def f():
                          b = TRB
                          cx.op("pe", acc_group(PB[b][:nt, :], [(mixT.t[:, k, 0:nt], wout.t[:, k, half * 512:(half + 1) * 512]) for k in range(8)]),
                                reads=[mixT.d, wout.d], writes=[PD[b]])
                          evac(yb.t[:nt, half * 512:(half + 1) * 512], PB[b][:nt, :], b, yb.d)
                      return f
                  sl.append(wo(0))
                  sl.append(wo(1))

                  def res():
                      st = small()
                      cx.op("act", lambda e: e.activation(sq.t[:nt, :], yb.t[:nt, :], AF.Square, accum_out=st.t[:nt, 0:1]),
                            reads=[yb.d], writes=[sq.d, st.d])
                      rstd_from_ss(st, nt, 0, 1, 1.0 / D)
                      cx.op("dve", lambda e: e.scalar_tensor_tensor(yb.t[:nt, :], yb.t[:nt, :], st.t[:nt, 1:2], gbc.t[:nt, 1, :], ALU.mult, ALU.mult),
                            reads=[st.d, gbc.d], writes=[yb.d])
                      cx.op("dve", lambda e: e.tensor_tensor(xk.t[:nt, :], xk.t[:nt, :], yb.t[:nt, :], ALU.add),
                            reads=[yb.d], writes=[xk.d])
                      after()
                  sl.append(res)
                  return sl[0:6], sl[6:10]

              stop_at(4 + 10 * l)
              blocks = []
              for g in range(4):
                  blocks += [win_loader(l, C_QA), win_loader(l, C_KA), win_loader(l, C_QB), win_loader(l, C_KB),
                             win_loader(l, C_VB), win_loader(l, C_OB), win_loader(l, C_VA)]
              ws.schedule(blocks)

              for g in range(4):
                  smp = do_sample and g == 3
                  mm_ring[0] = RING6
                  for i in range(4):
                      r0 = (g * 4 + i) * 128
                      if xin_d is not None:
                          cx.dma("sp", XK[i].t[:, :], xin.ap()[r0:r0 + 128, :], reads=[xin_d], writes=[XK[i].d])
                      else:
                          cx.dma("sp", XK[i].t[:, :], xin.ap()[r0:r0 + 128, :], writes=[XK[i].d])
                  norm_seq([(XK[i], 128, i * 128) for i in range(4)], 0, hT, [xn, xn2], [jk0, jk1])
                  tiles = [(i * 128, 128) for i in range(4)]
                  gate_math(hT, tiles, gs)
                  if g == 0:
                      with nc.allow_non_contiguous_dma(reason="tiny conv params"):
                          cx.dma("sp", wcT.t[:], WCV.ap()[l].rearrange("j (c p) -> p j c", p=128), writes=[wcT.d], sem=cx.shared_sem("parc"))
                          cx.dma("sp", bcT.t[:], BCV.ap()[l].rearrange("(c p) -> p c", p=128), writes=[bcT.d], sem=cx.shared_sem("parc"))
                  if smp:
                      sample_setup()
                      if l == 0:
                          cx.dma("sp", XKs.t[0:16, :], xsin.ap()[:, :], writes=[XKs.d])
                      else:
                          cx.dma("sp", XKs.t[0:16, :], xsin.ap()[:, :], reads=[d_xs1], writes=[XKs.d])
                      norm_T(XKs, 16, 0, hTs, 0, sq, xn)
                      gate_math(hTs, [(0, 16)], gss)
                  slot_g = (g % 2) * 4

                  def q_masked(w, hsrc, ncols, dst):
                      for hp_ in range(4):
                          b = next_mm()
                          cx.op("pe", acc_group(PB[b][:, 0:ncols], [(w.t[:, k, hp_ * 128:(hp_ + 1) * 128], hsrc.t[:, k, 0:ncols]) for k in range(8)]),
                                reads=[w.d, hsrc.d], writes=[PD[b]])
                          cx.op("act", lambda e, hp_=hp_, b=b: e.activation(dst.t[0:64, 2 * hp_, 0:ncols], PB[b][0:64, 0:ncols], AF.Copy),
                                writes=[PD[b], dst.d])
                          cx.op("dve", lambda e, hp_=hp_, b=b: e.tensor_copy(dst.t[64:128, 2 * hp_ + 1, 0:ncols], PB[b][64:128, 0:ncols]),
                                writes=[PD[b], dst.d])
                  w = ws.get()
                  q_masked(w, hT, 512, QTA)
                  if smp:
                      q_masked(w, hTs, 16, QTAs)
                  w = ws.get(); fm_proj(w, range(4), hT, 512, KTA, 0, slot_g * 128)
                  if g == 3:
                      ko = TD(yb.t, "ko_alias"); ko.d = yb.d
                      for i in range(4):
                          b = tm_proj(lambda k: w.t[:, k, :], w.d, hT, i * 128, 128, 512)
                          evac(yb.t[:, 0:512], PB[b][:, :], b, yb.d)
                          cx.dma("sp", KP.ap()[l, :, i * 128:(i + 1) * 128, :].rearrange("h t d -> t h d"),
                                 yb.t[:, 0:512].rearrange("p (h d) -> p h d", h=8), reads=[yb.d], writes=[d_out], sem=sem_out)
                  if smp:
                      fm_proj(w, range(4), hTs, 16, KTAs, 0, 0)
                      b = tm_proj(lambda k: w.t[:, k, :], w.d, hTs, 0, 16, 512)
                      evac(yb.t[0:16, 0:512], PB[b][0:16, :], b, yb.d)
                      cx.dma("sp", KS.ap()[l].rearrange("h t d -> t h d"), yb.t[0:16, 0:512].rearrange("p (h d) -> p h d", h=8),
                             reads=[yb.d], writes=[d_out], sem=sem_out)
                  w = ws.get(); fm_proj(w, range(4), hT, 512, QTB, 0, 0)
                  if smp:
                      fm_proj(w, range(4), hTs, 16, QTBs, 0, 0)
                  w = ws.get(); fm_proj(w, range(4), hT, 512, KTB, 0, 0)
                  for i in range(4):
                      evac_k3(hT, i * 128, 128, i, w, gs, K3)
                  if smp:
                      fm_proj(w, range(4), hTs, 16, KTBs, 0, 0)
                      evac_k3(hTs, 0, 16, 0, w, gss, K3s)
                  w = ws.get()
                  for i in range(4):
                      evac_vb1(hT, i * 128, 128, i, w, VB1)
                  if smp:
                      evac_vb1(hTs, 0, 16, 0, w, VB1s)

                  def og_tile(hsrc, col0, nt, dst_ap, dd):
                      b = tm_proj(lambda k: w.t[:, k, :], w.d, hsrc, col0, nt, 512)
                      cx.op("act", lambda e: e.activation(dst_ap, PB[b][:nt, :], AF.Exp, scale=-1.0), writes=[PD[b], dd])
                      cx.op("dve", lambda e: e.tensor_scalar(dst_ap, dst_ap, 1.0, None, ALU.add), writes=[dd])
                      cx.op("dve", lambda e: e.reciprocal(dst_ap, dst_ap), writes=[dd])
                      cx.op("dve", lambda e: e.tensor_tensor(dst_ap, dst_ap, gml.t[:nt, :], ALU.mult), reads=[gml.d], writes=[dd])
                  w = ws.get()
                  for i in range(4):
                      og_tile(hT, i * 128, 128, OG.t[:, i, :], OG.d)
                  if smp:
                      og_tile(hTs, 0, 16, OGs.t[0:16, 0, :], OGs.d)
                  w = ws.get()
                  for i in range(4):
                      b = tm_proj(lambda k: w.t[:, k, :], w.d, hT, i * 128, 128, 512)
                      cx.op("act", lambda e, i=i, b=b: e.activation(VA.t[:, slot_g + i, :, 0:64],
                                                                    PB[b][:, :].rearrange("p (h d) -> p h d", h=8), AF.Copy),
                            writes=[PD[b], VA.d])
                      cx.op("dve", lambda e, i=i: e.memset(VA.t[:, slot_g + i, :, 64:65], 1.0), writes=[VA.d])
                      if g == 3:
                          cx.op("dve", lambda e, b=b: e.tensor_copy(yb.t[:, 512:1024], PB[b][:, :]), writes=[PD[b], yb.d])
                          cx.dma("sp", VP.ap()[l, :, i * 128:(i + 1) * 128, :].rearrange("h t d -> t h d"),
                                 yb.t[:, 512:1024].rearrange("p (h d) -> p h d", h=8), reads=[yb.d], writes=[d_out], sem=sem_out)
                  if smp:
                      b = tm_proj(lambda k: w.t[:, k, :], w.d, hTs, 0, 16, 512)
                      cx.op("act", lambda e, b=b: e.activation(VAs.t[0:16, :, 0:64], PB[b][0:16, :].rearrange("p (h d) -> p h d", h=8), AF.Copy),
                            writes=[PD[b], VAs.d])
                      cx.op("dve", lambda e: e.memset(VAs.t[0:16, :, 64:65], 1.0), writes=[VAs.d])
                      cx.op("dve", lambda e, b=b: e.tensor_copy(yb.t[0:16, 512:1024], PB[b][0:16, :]), writes=[PD[b], yb.d])
                      cx.dma("sp", VS.ap()[l].rearrange("h t d -> t h d"), yb.t[0:16, 512:1024].rearrange("p (h d) -> p h d", h=8),
                             reads=[yb.d], writes=[d_out], sem=sem_out)

                  stop_at(41 + 100 * l)
                  if g == 3:
                      stop_at(45 + 100 * l)
                  if g == 0:
                      consume_exchange()
                  mm_ring[0] = RING2
                  back_prev = []
                  mixL = [mix, mix2]
                  for i in range(4):
                      tg = g * 4 + i
                      mixx = mixL[i % 2]

                      def after(i=i, tg=tg):
                          cx.dma("sp", XM.ap()[tg * 128:(tg + 1) * 128, :], XK[i].t[:, :], reads=[XK[i].d], writes=[d_xm], sem=sem_out)
                          if tg == NTILE - 1:
                              cx.dma("sp", EXX_S.ap()[:, :], XK[i].t[126:128, :], reads=[XK[i].d], writes=[d_exx[0]], sem=sem_out)
                      front, back = make_slices(128, i * 128, i, QTB, KTB, K3, VB1, OG.t[:, i, :], OG.d, gs, C32, CBF, True, XK[i], after, mixx)
                      order = []
                      bp, fr = list(back_prev), list(front)
                      while bp or fr:
                          if bp:
                              order.append(bp.pop(0))
                          if fr:
                              order.append(fr.pop(0))
                      attn_prompt(tg, i * 128, order)
                      attn_epi(128, mixx)
                      back_prev = back
                  for f_ in back_prev:
                      f_()
                  if smp:
                      def after_s():
                          cx.dma("sp", XSM.ap()[:, :], XKs.t[0:16, :], reads=[XKs.d], writes=[d_xsm], sem=sem_out)
                      front, back = make_slices(16, 0, 0, QTBs, KTBs, K3s, VB1s, OGs.t[0:16, 0, :], OGs.d, gss, C32s, CBFs, False, XKs, after_s, mix)
                      attn_sample()
                      attn_epi(16, mix)
                      for f_ in front + back:
                          f_()
                      cx.op("pe", lambda e: e.transpose(PB[ML][0:4, 0:16], gss["a"].t[0:16, 0, :], ident_f.t[0:16, 0:16]),
                            reads=[gss["a"].d, ident_f.d], writes=[PD[ML]])
                      mu4 = small()
                      cx.op("dve", lambda e: e.tensor_reduce(mu4.t[0:4, 0:1], PB[ML][0:4, 0:16], AX.X, ALU.max), writes=[PD[ML], mu4.d])
                      cx.op("pe", mm(PB[ML][0:1, 200:204], mu4.t[0:4, 0:1], ident_f.t[0:4, 0:4]), reads=[mu4.d, ident_f.d], writes=[PD[ML]])
                      cx.op("dve", lambda e: e.tensor_copy(mu4.t[0:1, 4:8], PB[ML][0:1, 200:204]), writes=[PD[ML], mu4.d])
                      cx.op("pe", mm(PB[ML][:, 208:212], ones_f.t[0:1, :], mu4.t[0:1, 4:8]), reads=[mu4.d, ones_f.d], writes=[PD[ML]])
                      cx.op("dve", lambda e: e.tensor_tensor(m0b.t[:, 8:12], m0b.t[:, 0:4], PB[ML][:, 208:212], ALU.max),
                            writes=[PD[ML], m0b.d])
                      cx.op("dve", lambda e: e.tensor_tensor(m0b.t[:, 8:12], m0b.t[:, 8:12], gss["nB"].t[:, 0, :], ALU.subtract),
                            reads=[gss["nB"].d], writes=[m0b.d])
                      cx.op("act", lambda e: e.activation(m0b.t[:, 12:16], m0b.t[:, 8:12], AF.Exp, scale=-1.0), writes=[m0b.d])
                      cos_ = hh
                      for h in range(4):
                          cx.op("dve", lambda e, h=h: e.tensor_scalar(sq.t[:, h * 129:(h + 1) * 129], C32s.t[:, h, :], m0b.t[:, 12 + h:13 + h], None, ALU.mult),
                                reads=[C32s.d, m0b.d], writes=[sq.d])
                      sqv = sq.t[:, 0:516].rearrange("p (h e) -> p h e", h=4)
                      cx.dma("sp", CS.ap()[l].rearrange("h d e -> d h e"), sqv[:, :, 0:128], reads=[sq.d], writes=[d_out], sem=sem_out)
                      with nc.allow_non_contiguous_dma(reason="n state column"):
                          cx.dma("sp", NS.ap()[l].rearrange("h d -> d h"), sqv[:, :, 128], reads=[sq.d], writes=[d_out], sem=sem_out)
                      cx.dma("sp", MS.ap()[l:l + 1, :], m0b.t[0:1, 8:12], reads=[m0b.d], writes=[d_out], sem=sem_out)

              mE = ME[l]
              co = TD(None, "co_alias"); co.t = sq.t[:, 0:516].rearrange("p (h e) -> p h e", h=4); co.d = sq.d
              for h in range(4):
                  cx.op("dve", lambda e, h=h: e.tensor_scalar(co.t[:, h, :], C32.t[:, h, :], mE.t[:, 4 + h:5 + h], None, ALU.mult),
                        reads=[C32.d, mE.d], writes=[co.d])
              cx.dma("sp", CP.ap()[l].rearrange("h d e -> d h e"), co.t[:, :, 0:128], reads=[co.d], writes=[d_out], sem=sem_out)
              with nc.allow_non_contiguous_dma(reason="n state column"):
                  cx.dma("sp", NP.ap()[l].rearrange("h d -> d h"), co.t[:, :, 128], reads=[co.d], writes=[d_out], sem=sem_out)
              stop_at(5 + 10 * l)
              cx.allgather(EXX_S.ap().opt(), EXX_D.ap().opt(), [d_exx[0]], [d_exx[1]], sem_cc)
              cx.barrier(exclude=(sem_cc.key,))

          with contextlib.ExitStack() as es_:
              def lt(name, shape, dt=F32):
                  return TD(es_.enter_context(nc.sbuf_tensor(nm(name), list(shape), dt)), name)
              hT = lt("c_hT", [128, 8, 512], BF16); sq = lt("c_sq", [128, D]); xn = lt("c_xn", [128, D], BF16)
              xn2 = lt("c_xn2", [128, D], BF16); jk0 = lt("c_jk0", [128, D], BF16); jk1 = lt("c_jk1", [128, D], BF16)
              actT = lt("c_actT", [128, NCH, 512], BF16)
              wd = lt("c_wd", [128, NCH, D], BF16)
              XK = [lt(f"c_xk{i}", [128, D]) for i in range(4)]
              yg = [lt(f"c_yg{i}", [128, 512]) for i in range(2)]
              yu = [lt(f"c_yu{i}", [128, 512]) for i in range(2)]
              yb = lt("c_y", [128, D])
              hTh = lt("c_hTh", [128, 8, 2], BF16); xh = yb
              blocks = []
              for g in range(4):
                  blocks += [wup_loader(l, j) for j in range(11)]
                  if g == 0:
                      blocks += [wup_loader(l, j) for j in range(3)]
              ws.schedule(blocks)
              ws.prefetch()
              for hf in range(2):
                  cx.dma("pool", wd.t[:, hf * 11:(hf + 1) * 11, :],
                         WDN.ap()[l, hf * 11 * 128:(hf + 1) * 11 * 128, :].rearrange("(c p) n -> p c n", p=128), writes=[wd.d])

              UC_RING = [S0, S1, O0, O1, ML, MM0, MM1]
              uc_i = [0]
              mm_ring[0] = RING2

              def edge_src(bk, ncols):
                  a_ = PB[bk][:, 0:2]
                  return bass.AP(a_.tensor, a_.offset, [list(a_.ap[0]), [ncols - 2, 2], [1, 2]])

              HS = 3

              def halo_block(w, j, hsrc, uh):
                  for pr in range(2):
                      for typ in range(2):
                          cidx = 2 * j + pr + typ * NCH
                          uc_i[0] += 1
                          bk = UC_RING[uc_i[0] % len(UC_RING)]
                          cx.op("pe", acc_group(PB[bk][:, 0:2], [(w.t[:, k, typ * 256 + pr * 128:typ * 256 + (pr + 1) * 128],
                                                                  hsrc.t[:, k, 0:2]) for k in range(8)]),
                                reads=[w.d, hsrc.d], writes=[PD[bk]])
                          cx.op("act", lambda e, bk=bk, cidx=cidx: e.activation(uh.t[:, cidx, :], PB[bk][:, 0:2], AF.Copy),
                                writes=[PD[bk], uh.d])

              def up_main(streams, halo=None):
                  for j in range(11):
                      w = ws.get()
                      if halo is not None and j == HS:
                          halo_prep()
                      if halo is not None and j >= HS:
                          halo_block(w, j, halo[0], halo[1])
                      for (hsrc, ncols, edge, act_dst) in streams:
                          for pr in range(2):
                              ch = 2 * j + pr
                              for typ in range(2):
                                  cidx = ch + typ * NCH
                                  uc_i[0] += 1
                                  bk = UC_RING[uc_i[0] % len(UC_RING)]
                                  cx.op("pe", acc_group(PB[bk][:, 0:ncols], [(w.t[:, k, typ * 256 + pr * 128:typ * 256 + (pr + 1) * 128],
                                                                              hsrc.t[:, k, 0:ncols]) for k in range(8)]),
                                        reads=[w.d, hsrc.d], writes=[PD[bk]])
                                  y = (yg if typ == 0 else yu)[pr]
                                  cx.op("act", lambda e, bk=bk, cidx=cidx, y=y: e.activation(y.t[:, 0:ncols], PB[bk][:, 0:ncols], AF.Identity,
                                                                                            scale=wcT.t[:, 2, cidx:cidx + 1], bias=bcT.t[:, cidx:cidx + 1]),
                                        reads=[wcT.d, bcT.d], writes=[PD[bk], y.d])
                                  cx.op("act", lambda e, bk=bk, cidx=cidx: e.activation(edge.t[:, cidx, :].rearrange("p (a b) -> p a b", a=2),
                                                                                       edge_src(bk, ncols), AF.Copy),
                                        writes=[PD[bk], edge.d])
                                  cx.op("dve", lambda e, bk=bk, cidx=cidx, y=y: e.scalar_tensor_tensor(
                                      y.t[:, 1:ncols], PB[bk][:, 0:ncols - 1], wcT.t[:, 1, cidx:cidx + 1], y.t[:, 1:ncols], ALU.mult, ALU.add),
                                      reads=[wcT.d], writes=[PD[bk], y.d])
                                  cx.op("dve", lambda e, bk=bk, cidx=cidx, y=y: e.scalar_tensor_tensor(
                                      y.t[:, 2:ncols], PB[bk][:, 0:ncols - 2], wcT.t[:, 0, cidx:cidx + 1], y.t[:, 2:ncols], ALU.mult, ALU.add),
                                      reads=[wcT.d], writes=[PD[bk], y.d])
                              cx.op("act", lambda e, pr=pr: e.activation(yg[pr].t[:, 0:ncols], yg[pr].t[:, 0:ncols], AF.Gelu_apprx_tanh),
                                    writes=[yg[pr].d])
                              cx.op("dve", lambda e, pr=pr, ch=ch: e.tensor_tensor(act_dst.t[:, ch, 0:ncols], yg[pr].t[:, 0:ncols],
                                                                                  yu[pr].t[:, 0:ncols], ALU.mult),
                                    reads=[yg[pr].d, yu[pr].d], writes=[act_dst.d])

              def halo_pass(hsrc, uh):
                  for j in range(HS):
                      w = ws.get()
                      halo_block(w, j, hsrc, uh)

              fy = lt("c_fy", [128, 2, 2 * NCH]); ft = lt("c_ft", [128, 2 * NCH])

              def fix_boundary(edge, halo, halo_dep, act_dst):
                  u0, u1 = edge.t[:, :, 0], edge.t[:, :, 1]
                  h0, h1 = halo[:, :, 0], halo[:, :, 1]
                  W0, W1, W2 = wcT.t[:, 0, :], wcT.t[:, 1, :], wcT.t[:, 2, :]
                  rd = [edge.d, halo_dep, wcT.d, bcT.d]
                  for col, (ua, ta, tb) in enumerate(((u0, h1, h0), (u1, u0, h1))):
                      yv = fy.t[:, col, :]
                      cx.op("dve", lambda e, yv=yv, ua=ua: e.tensor_tensor(yv, ua, W2, ALU.mult), reads=rd, writes=[fy.d])
                      cx.op("dve", lambda e, yv=yv: e.tensor_tensor(yv, yv, bcT.t[:, :], ALU.add), reads=rd, writes=[fy.d])
                      cx.op("dve", lambda e, ta=ta: e.tensor_tensor(ft.t[:, :], ta, W1, ALU.mult), reads=rd, writes=[ft.d])
                      cx.op("dve", lambda e, yv=yv: e.tensor_tensor(yv, yv, ft.t[:, :], ALU.add), reads=[ft.d], writes=[fy.d])
                      cx.op("dve", lambda e, tb=tb: e.tensor_tensor(ft.t[:, :], tb, W0, ALU.mult), reads=rd, writes=[ft.d])
                      cx.op("dve", lambda e, yv=yv: e.tensor_tensor(yv, yv, ft.t[:, :], ALU.add), reads=[ft.d], writes=[fy.d])
                  cx.op("act", lambda e: e.activation(fy.t[:, :, 0:NCH], fy.t[:, :, 0:NCH], AF.Gelu_apprx_tanh), writes=[fy.d])
                  cx.op("dve", lambda e: e.tensor_tensor(act_dst.t[:, :, 0:2].rearrange("p c t -> p t c"), fy.t[:, :, 0:NCH], fy.t[:, :, NCH:2 * NCH],
                                                         ALU.mult), reads=[fy.d], writes=[act_dst.d])

              def down_res(nt, act_src, c0, xk):
                  for half in range(2):
                      b = next_mm()
                      cx.op("pe", acc_group(PB[b][:nt, :], [(act_src.t[:, c, c0:c0 + nt], wd.t[:, c, half * 512:(half + 1) * 512])
                                                            for c in range(NCH)]),
                            reads=[act_src.d, wd.d], writes=[PD[b]])
                      evac(yb.t[:nt, half * 512:(half + 1) * 512], PB[b][:nt, :], b, yb.d)
                  st = small()
                  cx.op("act", lambda e: e.activation(sq.t[:nt, :], yb.t[:nt, :], AF.Square, accum_out=st.t[:nt, 0:1]),
                        reads=[yb.d], writes=[sq.d, st.d])
                  rstd_from_ss(st, nt, 0, 1, 1.0 / D)
                  cx.op("dve", lambda e: e.scalar_tensor_tensor(yb.t[:nt, :], yb.t[:nt, :], st.t[:nt, 1:2], gbc.t[:nt, 3, :], ALU.mult, ALU.mult),
                        reads=[st.d, gbc.d], writes=[yb.d])
                  cx.op("dve", lambda e: e.tensor_tensor(xk.t[:nt, :], xk.t[:nt, :], yb.t[:nt, :], ALU.add),
                        reads=[yb.d], writes=[xk.d])

              EDGE = [lt(f"c_edge{i}", [128, 2 * NCH, 4]) for i in range(2)]
              edge_s = lt("c_edge_s", [128, 2 * NCH, 4])
              if do_sample:
                  XKs = lt("c_xks", [128, D]); hTs = lt("c_hTs", [128, 8, 16], BF16); actTs = lt("c_actTs", [128, NCH, 16], BF16)
                  with nc.allow_non_contiguous_dma(reason="conv cache rows"):
                      for r_ in range(2):
                          cx.dma("sp", uh_s.t[:, :, r_], CCV.ap()[l, r_].rearrange("(c p) -> p c", p=128), writes=[uh_s.d])

              def halo_prep():
                  for r in range(4):
                      cx.dma("sp", sq.t[0:2, :], EXX_D.ap()[2 * r:2 * r + 2, :], reads=[d_exx[1]], writes=[sq.d], sem=cx.shared_sem("xst"))
                      if r == 0:
                          cx.op("dve", lambda e: e.tensor_scalar(xh.t[0:2, :], sq.t[0:2, :], cm.t[0:2, 0:1], None, ALU.mult),
                                reads=[sq.d, cm.d], writes=[xh.d])
                      else:
                          cx.op("dve", lambda e, r=r: e.scalar_tensor_tensor(xh.t[0:2, :], sq.t[0:2, :], cm.t[0:2, r:r + 1], xh.t[0:2, :],
                                                                            ALU.mult, ALU.add), reads=[sq.d, cm.d], writes=[xh.d])
                  norm_T(xh, 2, 2, hTh, 0, sq, xn)

              hT2 = lt("c_hT2", [128, 8, 512], BF16)
              XN = [lt(f"c_xnin{i}", [128, D]) for i in range(2)]
              hTL = [hT, hT2]

              def norm_group(g_):
                  for i2 in range(0, 4, 2):
                      items = []
                      for i in (i2, i2 + 1):
                          r0 = (g_ * 4 + i) * 128
                          cx.dma("sp", XN[i % 2].t[:, :], XM.ap()[r0:r0 + 128, :], reads=[d_xm], writes=[XN[i % 2].d])
                          items.append((XN[i % 2], 128, i * 128))
                      norm_seq(items, 2, hTL[g_ % 2], [xn, xn2], [jk0, jk1])

              norm_group(0)
              for g in range(4):
                  smp = do_sample and g == 3
                  hT = hTL[g % 2]
                  edge = EDGE[g % 2]
                  streams = [(hT, 512, edge, actT)]
                  if smp:
                      cx.dma("sp", XKs.t[0:16, :], XSM.ap()[:, :], reads=[d_xsm], writes=[XKs.d])
                      norm_T(XKs, 16, 2, hTs, 0, sq, xn)
                      streams.append((hTs, 16, edge_s, actTs))
                  up_main(streams, halo=(hTh, uh_p) if g == 0 else None)
                  if g == 0:
                      halo_pass(hTh, uh_p)
                      fix_boundary(edge, uh_p.t[:, :, :], uh_p.d, actT)
                  else:
                      fix_boundary(edge, EDGE[(g - 1) % 2].t[:, :, 2:4], EDGE[(g - 1) % 2].d, actT)
                  if smp:
                      fix_boundary(edge_s, uh_s.t[:, :, :], uh_s.d, actTs)
                  for i in range(4):
                      r0 = (g * 4 + i) * 128
                      cx.dma("sp", XK[i].t[:, :], XM.ap()[r0:r0 + 128, :], reads=[d_xm], writes=[XK[i].d])
                  for i in range(4):
                      tg = g * 4 + i
                      down_res(128, actT, i * 128, XK[i])
                      cx.dma("sp", xout.ap()[tg * 128:(tg + 1) * 128, :], XK[i].t[:, :], reads=[XK[i].d],
                             writes=[d_x1 if l == 0 else d_out], sem=sem_out)
                      if g + 1 < 4:
                          r0n = ((g + 1) * 4 + i) * 128
                          cx.dma("sp", XN[i % 2].t[:, :], XM.ap()[r0n:r0n + 128, :], reads=[d_xm], writes=[XN[i % 2].d])
                          norm_seq([(XN[i % 2], 128, i * 128)], 2, hTL[(g + 1) % 2], [[xn, xn2][i % 2]], [[jk0, jk1][i % 2]])
                  if smp:
                      down_res(16, actTs, 0, XKs)
                      cx.dma("sp", xsout.ap()[:, :], XKs.t[0:16, :], reads=[XKs.d], writes=[d_xs1 if l == 0 else d_out], sem=sem_out)
                      with nc.allow_non_contiguous_dma(reason="conv state rows"):
                          for r_ in range(2):
                              cx.dma("sp", CVS.ap()[l, r_].rearrange("(c p) -> p c", p=128), edge_s.t[:, :, 2 + r_], reads=[edge_s.d],
                                     writes=[d_out], sem=sem_out)
              with nc.allow_non_contiguous_dma(reason="conv state rows (2 x 5632 elements)"):
                  for r_ in range(2):
                      cx.dma("sp", CVP.ap()[l, r_].rearrange("(c p) -> p c", p=128), EDGE[1].t[:, :, 2 + r_], reads=[EDGE[1].d],
                             writes=[d_out], sem=sem_out)
              cx.barrier()

    except _Stop:
        pass
    cx.finish()
    build.stats = (cx.n_inst, cx.n_wait, len(cx.sems))
    return nc


_NC_CACHE = {}


def kernel(x_prompt, x_sample, cache_k_att, cache_v_att, state_mlstm_c, state_mlstm_n, state_mlstm_m,
           cache_ffn_conv, norm_g, w_in, b_i, b_f, rel_table, g_att, g_mlstm, w_out, w_up, w_conv,
           b_conv, w_down):
    f = lambda a: np.ascontiguousarray(np.asarray(a), dtype=np.float32)
    x_prompt, x_sample = f(x_prompt), f(x_sample)
    shared = {
        "w_in": f(w_in), "w_out": f(w_out), "w_up": f(w_up), "w_down": f(w_down), "norm_g": f(norm_g),
        "g_att": f(g_att), "g_mlstm": f(g_mlstm), "b_i": f(b_i), "b_f": f(b_f), "rel_table": f(rel_table),
        "w_conv": f(w_conv), "b_conv": f(b_conv),
        "ident": np.eye(128, dtype=np.float32), "triu": np.triu(np.ones((128, 128), np.float32)),
        "ones": np.ones((128, 128), np.float32),
    }
    ck, cv = f(cache_k_att), f(cache_v_att)
    sc, sn, sm, ccv = f(state_mlstm_c), f(state_mlstm_n), f(state_mlstm_m), f(cache_ffn_conv)
    in_maps = []
    for c in range(8):
        b, j = c // 4, c % 4
        cmask = np.zeros(12, np.float32)
        for r in range(4):
            cmask[r] = 1.0 if r == j - 1 else 0.0
            cmask[4 + r] = 1.0 if r < j else 0.0
            cmask[8 + r] = 1.0 if r <= j else 0.0
        m = dict(shared)
        m.update({
            "xp": np.ascontiguousarray(x_prompt[b, j * TOK:(j + 1) * TOK]), "xs": np.ascontiguousarray(x_sample[c]),
            "ck": np.ascontiguousarray(ck[:, c]), "cv": np.ascontiguousarray(cv[:, c]),
            "sc": np.ascontiguousarray(sc[:, c]), "sn": np.ascontiguousarray(sn[:, c]),
            "sm": np.ascontiguousarray(sm[:, c]), "cconv": np.ascontiguousarray(ccv[:, c]),
            "cmask": cmask,
        })
        in_maps.append(m)
    if "nc" not in _NC_CACHE:
        _NC_CACHE["nc"] = build()
    res = run_bass_kernel_spmd(_NC_CACHE["nc"], in_maps, core_ids=list(range(8)))
    R = res.results
    yp = np.stack([np.concatenate([R[b * 4 + j]["yp"] for j in range(4)], 0) for b in range(2)], 0)
    ys = np.stack([R[c]["ys"] for c in range(8)], 0)
    last = [3, 7]
    pick = lambda k: np.stack([R[c][k] for c in last], 1)
    allc = lambda k: np.stack([R[c][k] for c in range(8)], 1)
    outs = (yp, ys, pick("kp"), pick("vp"), pick("cp"), pick("np"), pick("mp"), pick("cvp"),
            allc("ks"), allc("vs"), allc("cs"), allc("ns"), allc("ms"), allc("cvs"))
    return tuple(np.ascontiguousarray(o, dtype=np.float32) for o in outs)
```

```python
import contextlib
import math
import numpy as np
import concourse.bass as bass
import concourse.mybir as mybir
from concourse.bass_utils import run_bass_kernel_spmd

F32 = mybir.dt.float32
BF16 = mybir.dt.bfloat16
AF = mybir.ActivationFunctionType
ALU = mybir.AluOpType
AX = mybir.AxisListType

D = 1024
TOK = 2048
NTILE = 16
DIN = 3592
DFF = 2816
NCH = 22
EPS = 1e-6
C_QA, C_KA, C_VA, C_QB, C_KB, C_VB, C_OB, C_G = 0, 512, 1024, 1536, 2048, 2560, 3072, 3584
LNK = -0.5 * math.log(128.0)
NEG = -1.0e30
DSZ = 128 * 769 + 768
XROWS = 129
XCOLS = 516


class Dep:
    __slots__ = ("name", "w", "r", "dsem")

    def __init__(self, name):
        self.name = name
        self.w = None
        self.r = {}
        self.dsem = None


class Sem:
    __slots__ = ("key", "h", "count", "is_dma")

    def __init__(self, key, h, is_dma):
        self.key = key
        self.h = h
        self.count = 0
        self.is_dma = is_dma


class Eng:
    __slots__ = ("name", "eng", "sem", "seen", "same_sync")

    def __init__(self, name, eng, sem, same_sync):
        self.name = name
        self.eng = eng
        self.sem = sem
        self.seen = {}
        self.same_sync = same_sync


class Ctx:
    def __init__(self, nc, same_sync=True):
        self.nc = nc
        self.sems = {}
        self.engs = {}
        for name, eng, ss in (("pe", nc.tensor, False), ("act", nc.scalar, same_sync),
                              ("dve", nc.vector, same_sync), ("pool", nc.gpsimd, same_sync),
                              ("sp", nc.sync, False)):
            s = self.new_sem("e_" + name, False)
            self.engs[name] = Eng(name, eng, s, ss)
        self.n_wait = 0
        self.n_inst = 0
        self.shared = {}

    def new_sem(self, key, is_dma=True):
        h = self.nc.alloc_semaphore(key)
        s = Sem(key, h, is_dma)
        self.sems[key] = s
        return s

    def shared_sem(self, key):
        if key not in self.shared:
            self.shared[key] = self.new_sem("sh_" + key)
        return self.shared[key]

    def _waits(self, es, reads, writes):
        need = {}
        for d in reads:
            if d.w is not None:
                k, v = d.w
                if need.get(k, 0) < v:
                    need[k] = v
        for d in writes:
            if d.w is not None:
                k, v = d.w
                if need.get(k, 0) < v:
                    need[k] = v
            for k, v in d.r.items():
                if need.get(k, 0) < v:
                    need[k] = v
        for k, v in need.items():
            if k == es.sem.key and not es.same_sync:
                continue
            s = self.sems[k]
            if s.is_dma:
                v = s.count
            if es.seen.get(k, 0) < v:
                es.eng.wait_ge(s.h, v)
                es.seen[k] = v
                self.n_wait += 1

    def op(self, E, fns, reads=(), writes=()):
        es = self.engs[E]
        self._waits(es, reads, writes)
        if not isinstance(fns, (list, tuple)):
            fns = [fns]
        inst = None
        for f in fns:
            inst = f(es.eng)
            self.n_inst += 1
        es.sem.count += 1
        inst.then_inc(es.sem.h, 1)
        k, c = es.sem.key, es.sem.count
        for d in reads:
            d.r[k] = c
        for d in writes:
            d.w = (k, c)
            d.r = {}
        return inst

    def _sem_for(self, reads, writes, sem):
        if sem is not None:
            return sem
        d0 = writes[0] if writes else reads[0]
        if d0.dsem is None:
            key = "d_" + d0.name
            if key not in self.sems:
                self.new_sem(key)
            d0.dsem = key
        return self.sems[d0.dsem]

    def dma(self, Q, out, in_, reads=(), writes=(), sem=None, **kw):
        es = self.engs[Q]
        self._waits(es, reads, writes)
        sem = self._sem_for(reads, writes, sem)
        inst = es.eng.dma_start(out=out, in_=in_, **kw)
        self.n_inst += 1
        sem.count += 16
        inst.then_inc(sem.h, 16)
        for d in reads:
            d.r[sem.key] = sem.count
        for d in writes:
            d.w = (sem.key, sem.count)
            d.r = {}
        return inst

    def allgather(self, src_ap, dst_ap, reads, writes, sem):
        es = self.engs["pool"]
        self._waits(es, reads, writes)
        inst = es.eng.collective_compute("AllGather", ALU.bypass, replica_groups=[[0, 1, 2, 3], [4, 5, 6, 7]],
                                         ins=[src_ap], outs=[dst_ap])
        self.n_inst += 1
        sem.count += 1
        inst.then_inc(sem.h, 1)
        for d in reads:
            d.r[sem.key] = sem.count
        for d in writes:
            d.w = (sem.key, sem.count)
            d.r = {}

    def barrier(self, exclude=()):
        for es in self.engs.values():
            for k, s in self.sems.items():
                if k in exclude:
                    continue
                if s.count > 0 and k != es.sem.key and es.seen.get(k, 0) < s.count:
                    es.eng.wait_ge(s.h, s.count)
                    es.seen[k] = s.count
                    self.n_wait += 1

    def finish(self, E="sp"):
        es = self.engs[E]
        for k, s in self.sems.items():
            if s.count > 0 and k != es.sem.key and es.seen.get(k, 0) < s.count:
                es.eng.wait_ge(s.h, s.count)
                es.seen[k] = s.count


class TD:
    __slots__ = ("t", "d")

    def __init__(self, t, name):
        self.t = t
        self.d = Dep(name)


class _Stop(Exception):
    pass


def build(do_sample=True):
    import os
    kstop = int(os.environ.get("KSTOP", "99"))

    def stop_at(n):
        if kstop == n:
            raise _Stop()
    nc = bass.Bass("TRN2", target_bir_lowering=False)
    cx = Ctx(nc)
    uid = [0]

    def nm(s):
        uid[0] += 1
        return f"{s}_{uid[0]}"

    def din(name, shape):
        return nc.dram_tensor(name, list(shape), F32, kind="ExternalInput")

    def dout(name, shape):
        return nc.dram_tensor(name, list(shape), F32, kind="ExternalOutput")

    def dscr(name, shape, dt=F32):
        return nc.dram_tensor(name, list(shape), dt)

    XP = din("xp", [TOK, D]); XS = din("xs", [16, D])
    WIN = din("w_in", [2, D, DIN]); WOUT = din("w_out", [2, D, D])
    WUP = din("w_up", [2, D, 2 * DFF]); WDN = din("w_down", [2, DFF, D])
    NG = din("norm_g", [2, 4, D]); GATT = din("g_att", [2, 512]); GML = din("g_mlstm", [2, 512])
    BI = din("b_i", [2, 4]); BFG = din("b_f", [2, 4]); REL = din("rel_table", [2, 8, 513])
    WCV = din("w_conv", [2, 3, 2 * DFF]); BCV = din("b_conv", [2, 2 * DFF])
    CK = din("ck", [2, 8, 512, 64]); CV = din("cv", [2, 8, 512, 64])
    SC = din("sc", [2, 4, 128, 128]); SN = din("sn", [2, 4, 128]); SM = din("sm", [2, 4])
    CCV = din("cconv", [2, 2, 2 * DFF])
    IDN = din("ident", [128, 128]); TRIU = din("triu", [128, 128]); ONES = din("ones", [128, 128])
    CMASK = din("cmask", [12])

    YP = dout("yp", [TOK, D]); YS = dout("ys", [16, D])
    KP = dout("kp", [2, 8, 512, 64]); VP = dout("vp", [2, 8, 512, 64])
    CP = dout("cp", [2, 4, 128, 128]); NP = dout("np", [2, 4, 128]); MP = dout("mp", [2, 4])
    CVP = dout("cvp", [2, 2, 2 * DFF])
    KS = dout("ks", [2, 8, 16, 64]); VS = dout("vs", [2, 8, 16, 64])
    CS = dout("cs", [2, 4, 128, 128]); NS = dout("ns", [2, 4, 128]); MS = dout("ms", [2, 4])
    CVS = dout("cvs", [2, 2, 2 * DFF])

    XM = dscr("xm", [TOK, D]); X1 = dscr("x1", [TOK, D])
    XSM = dscr("xsm", [16, D]); XS1 = dscr("xs1", [16, D])
    DTO = dscr("dtoep", [2, 8, DSZ])
    EXK_S = dscr("exk_s", [512, 512], BF16); EXK_D = dscr("exk_d", [4 * 512, 512], BF16)
    EXV_S = dscr("exv_s", [512, 512], BF16); EXV_D = dscr("exv_d", [4 * 512, 512], BF16)
    EXS_S = dscr("exs_s", [XROWS, XCOLS]); EXS_D = dscr("exs_d", [4 * XROWS, XCOLS])
    EXX_S = dscr("exx_s", [2, D]); EXX_D = dscr("exx_d", [8, D])
    d_xm = Dep("xm"); d_x1 = Dep("x1"); d_xsm = Dep("xsm"); d_xs1 = Dep("xs1"); d_dto = Dep("dto")
    d_exk = [Dep("exk_s"), Dep("exk_d")]; d_exv = [Dep("exv_s"), Dep("exv_d")]
    d_exs = [Dep("exs_s"), Dep("exs_d")]; d_exx = [Dep("exx_s"), Dep("exx_d")]
    d_out = Dep("outs")
    sem_out = cx.shared_sem("out")
    sem_cc = cx.new_sem("cc", True)

    PB = [nc.alloc_psum_tensor(f"pb{i}", [128, 512], F32) for i in range(8)]
    PD = [Dep(f"pb{i}") for i in range(8)]
    MM0, MM1, TRB, S0, S1, O0, O1, ML = range(8)

    def sb(name, shape, dt=F32):
        return TD(nc.alloc_sbuf_tensor(nm(name), list(shape), dt), name)

    ident_f = sb("ident_f", [128, 128]); triu_f = sb("triu_f", [128, 128]); ones_f = sb("ones_f", [128, 128])
    ident_b = sb("ident_b", [128, 128], BF16); masku_b = sb("masku_b", [128, 128], BF16)
    cm = sb("cm", [128, 12])
    EALL1 = sb("eall", [128, 8, 5, 128], BF16)
    EALL = [EALL1, EALL1]
    gbc = sb("gbc", [128, 4, D]); gatt = sb("gatt", [128, 512]); gml = sb("gml", [128, 512])
    bif = sb("bif", [128, 8]); wcT = sb("wcT", [128, 3, 2 * NCH]); bcT = sb("bcT", [128, 2 * NCH])
    WB = [sb(f"wb{i}", [128, 8, 512], BF16) for i in range(3)]
    wg = sb("wg", [128, 8, 8], BF16)
    C32 = sb("c32", [128, 4, 129]); CBF = sb("cbf", [128, 4, 129], BF16)
    ngrun = sb("ngrun", [128, 4]); amaxr = sb("amaxr", [128, 4])
    uh_p = sb("uh_p", [128, 2 * NCH, 2]); uh_s = sb("uh_s", [128, 2 * NCH, 2])
    ME = [sb(f"mE{i}", [128, 8]) for i in range(2)]
    SMALL = [sb(f"sm{i}", [128, 64]) for i in range(6)]
    sm_i = [0]

    def small():
        sm_i[0] = (sm_i[0] + 1) % len(SMALL)
        return SMALL[sm_i[0]]

    xt_i = [0]


    class WS:
        def __init__(self):
            self.pending = []
            self.inflight = []
            self.k = 0

        def schedule(self, loaders):
            self.pending.extend(loaders)

        def prefetch(self):
            while len(self.inflight) < 3 and self.pending:
                slot = WB[self.k % 3]
                self.k += 1
                self.pending.pop(0)(slot)
                self.inflight.append(slot)

        def get(self):
            while len(self.inflight) < 3 and self.pending:
                slot = WB[self.k % 3]
                self.k += 1
                self.pending.pop(0)(slot)
                self.inflight.append(slot)
            return self.inflight.pop(0)

    ws = WS()

    def win_loader(l, c0, n=512):
        def f(slot):
            cx.dma("pool", slot.t[:, :, 0:n], WIN.ap()[l, :, c0:c0 + n].rearrange("(k p) n -> p k n", p=128),
                   writes=[slot.d])
        return f

    def wup_loader(l, j):
        def f(slot):
            cx.dma("pool", slot.t[:, :, 0:256], WUP.ap()[l, :, j * 256:(j + 1) * 256].rearrange("(k p) n -> p k n", p=128),
                   writes=[slot.d])
            cx.dma("pool", slot.t[:, :, 256:512],
                   WUP.ap()[l, :, DFF + j * 256:DFF + (j + 1) * 256].rearrange("(k p) n -> p k n", p=128),
                   writes=[slot.d])
        return f

    def mm(out, lhsT, rhs, start=True, stop=True):
        return lambda e: e.matmul(out, lhsT, rhs, start=start, stop=stop)

    def acc_group(out, pairs):
        n = len(pairs)
        return [mm(out, a, b, start=(i == 0), stop=(i == n - 1)) for i, (a, b) in enumerate(pairs)]

    def rstd_from_ss(st, nt, c_in, c_out, inv_n, width=1):
        cx.op("act", lambda e: e.activation(st.t[:nt, c_out:c_out + width], st.t[:nt, c_in:c_in + width], AF.Ln,
                                            scale=inv_n, bias=eps_t.t[:nt, 0:1]), reads=[st.d, eps_t.d], writes=[st.d])
        cx.op("act", lambda e: e.activation(st.t[:nt, c_out:c_out + width], st.t[:nt, c_out:c_out + width], AF.Exp,
                                            scale=-0.5), reads=[st.d], writes=[st.d])

    eps_t = sb("eps_t", [128, 4])

    cx.dma("sp", ident_f.t[:], IDN.ap()[:, :], writes=[ident_f.d])
    cx.dma("sp", triu_f.t[:], TRIU.ap()[:, :], writes=[triu_f.d])
    cx.dma("sp", ones_f.t[:], ONES.ap()[:, :], writes=[ones_f.d])
    cx.dma("sp", cm.t[:], CMASK.ap().partition_broadcast(128), writes=[cm.d])
    cx.op("dve", lambda e: e.tensor_copy(ident_b.t[:], ident_f.t[:]), reads=[ident_f.d], writes=[ident_b.d])
    cx.op("dve", lambda e: e.tensor_copy(masku_b.t[:], triu_f.t[:]), reads=[triu_f.d], writes=[masku_b.d])
    cx.op("dve", lambda e: e.memset(eps_t.t[:, 0:1], EPS), writes=[eps_t.d])
    cx.op("dve", lambda e: e.memset(eps_t.t[:, 1:2], LNK), writes=[eps_t.d])
    cx.op("dve", lambda e: e.memset(eps_t.t[:, 2:3], 1.0), writes=[eps_t.d])
    cx.op("dve", lambda e: e.memset(eps_t.t[:, 3:4], 0.0), writes=[eps_t.d])

    def build_E(l, es_):
        R = TD(es_.enter_context(nc.sbuf_tensor(nm("Rtoep"), [128, 8, 768], F32)), "Rtoep")
        cl = TD(es_.enter_context(nc.sbuf_tensor(nm("cl"), [128, 8, 1], F32)), "cl")
        EP = TD(es_.enter_context(nc.sbuf_tensor(nm("epre"), [128, 8, 5, 128], F32)), "epre")
        cx.dma("sp", R.t[:, :, 0:384], bass.AP(REL, l * 8 * 513 + 129, [[0, 128], [513, 8], [1, 384]]), writes=[R.d])
        cx.dma("sp", cl.t[:], bass.AP(REL, l * 8 * 513 + 512, [[0, 128], [513, 8], [1, 1]]), writes=[cl.d],
               allow_slow_non_contiguous=True)
        cx.op("dve", lambda e: e.tensor_copy(R.t[:, :, 384:768], cl.t[:].to_broadcast([128, 8, 384])),
              reads=[cl.d], writes=[R.d])
        cx.dma("sp", bass.AP(DTO, l * 8 * DSZ, [[769, 128], [DSZ, 8], [1, 768]]), R.t[:], reads=[R.d], writes=[d_dto])
        for j in range(5):
            cx.dma("sp", EP.t[:, :, j, :], bass.AP(DTO, l * 8 * DSZ + 127 + 128 * (4 - j), [[768, 128], [DSZ, 8], [1, 128]]),
                   reads=[d_dto], writes=[EP.d])
        cx.op("act", lambda e: e.activation(EALL1.t[:].rearrange("p h j q -> p (h j q)"),
                                            EP.t[:].rearrange("p h j q -> p (h j q)"), AF.Exp),
              reads=[EP.d], writes=[EALL1.d])
        cx.op("dve", lambda e: e.memset(EALL1.t[0:64, :, 0, 64:128], 0.0), writes=[EALL1.d])
        cx.op("dve", lambda e: e.memset(EALL1.t[64:128, :, 4, 0:64], 0.0), writes=[EALL1.d])

    def load_x(XT, src_ap, nt):
        xt_i[0] ^= 1
        xt = XT[xt_i[0]]
        cx.dma("sp", xt.t[:nt, :], src_ap, writes=[xt.d])
        return xt

    def norm_T(xt, nt, gidx, hT, col0, sq, xn):
        st = small()
        cx.op("act", lambda e: e.activation(sq.t[:nt, :], xt.t[:nt, :], AF.Square, accum_out=st.t[:nt, 0:1]),
              reads=[xt.d], writes=[sq.d, st.d])
        rstd_from_ss(st, nt, 0, 1, 1.0 / D)
        cx.op("dve", lambda e: e.scalar_tensor_tensor(xn.t[:nt, :], xt.t[:nt, :], st.t[:nt, 1:2], gbc.t[:nt, gidx, :],
                                                      ALU.mult, ALU.mult), reads=[xt.d, st.d, gbc.d], writes=[xn.d])
        transpose_to(xn, nt, 8, hT, col0)

    def transpose_to(src, nt, nk, dst, col0, eng="act"):
        trv = PB[TRB][:].bitcast(BF16).rearrange("p (k c) -> p k c", k=8)
        cx.op("pe", [(lambda e, k=k: e.transpose(trv[:, k, :nt], src.t[:nt, k * 128:(k + 1) * 128], ident_b.t[:nt, :nt]))
                     for k in range(nk)], reads=[src.d, ident_b.d], writes=[PD[TRB]])
        if eng == "act":
            cx.op("act", lambda e: e.activation(dst.t[:, 0:nk, col0:col0 + nt], trv[:, 0:nk, :nt], AF.Copy),
                  writes=[PD[TRB], dst.d])
        else:
            cx.op("dve", lambda e: e.tensor_copy(dst.t[:, 0:nk, col0:col0 + nt], trv[:, 0:nk, :nt]),
                  writes=[PD[TRB], dst.d])

    def norm_seq(items, gidx, hT, xns, junks):
        trv = PB[TRB][:].bitcast(BF16).rearrange("p (k c) -> p k c", k=8)
        prev = None
        for idx, (xt, nt, col0) in enumerate(items):
            xn = xns[idx % len(xns)]
            junk = junks[idx % len(junks)]
            st = small()
            cx.op("act", lambda e: e.activation(junk.t[:nt, :], xt.t[:nt, :], AF.Square, accum_out=st.t[:nt, 0:1]),
                  reads=[xt.d], writes=[junk.d, st.d])
            rstd_from_ss(st, nt, 0, 1, 1.0 / D)
            cx.op("dve", lambda e: e.scalar_tensor_tensor(xn.t[:nt, :], xt.t[:nt, :], st.t[:nt, 1:2], gbc.t[:nt, gidx, :],
                                                          ALU.mult, ALU.mult), reads=[xt.d, st.d, gbc.d], writes=[xn.d])
            if prev is not None:
                pnt, pcol = prev
                cx.op("dve", lambda e: e.tensor_copy(hT.t[:, 0:8, pcol:pcol + pnt], trv[:, 0:8, :pnt]), writes=[PD[TRB], hT.d])
            cx.op("pe", [(lambda e, k=k: e.transpose(trv[:, k, :nt], xn.t[:nt, k * 128:(k + 1) * 128], ident_b.t[:nt, :nt]))
                         for k in range(8)], reads=[xn.d, ident_b.d], writes=[PD[TRB]])
            prev = (nt, col0)
        pnt, pcol = prev
        cx.op("dve", lambda e: e.tensor_copy(hT.t[:, 0:8, pcol:pcol + pnt], trv[:, 0:8, :pnt]), writes=[PD[TRB], hT.d])

    mm_i = [0]
    mm_ring = [[MM0, MM1]]
    RING2 = [MM0, MM1]
    RING6 = [MM0, MM1, S0, S1, O0, O1]

    def next_mm():
        mm_i[0] += 1
        r = mm_ring[0]
        return r[mm_i[0] % len(r)]

    s_i = [0]

    def next_s():
        s_i[0] ^= 1
        return S0 if s_i[0] else S1

    ev_i = [0]

    def evac(out_ap, in_ap, reads_bank, out_dep, extra_reads=()):
        ev_i[0] ^= 1
        if ev_i[0]:
            cx.op("act", lambda e: e.activation(out_ap, in_ap, AF.Copy), reads=list(extra_reads),
                  writes=[PD[reads_bank], out_dep])
        else:
            cx.op("dve", lambda e: e.tensor_copy(out_ap, in_ap), reads=list(extra_reads), writes=[PD[reads_bank], out_dep])

    def fm_proj(wslot, cchunks, hT, ncols, dst, dst_k0, dst_col0):
        for i, cc in enumerate(cchunks):
            b = next_mm()
            cx.op("pe", acc_group(PB[b][:, 0:ncols], [(wslot.t[:, k, cc * 128:(cc + 1) * 128], hT.t[:, k, 0:ncols])
                                                      for k in range(8)]),
                  reads=[wslot.d, hT.d], writes=[PD[b]])
            evac(dst.t[:, dst_k0 + i, dst_col0:dst_col0 + ncols], PB[b][:, 0:ncols], b, dst.d)

    def tm_proj(w_ap_fn, w_dep, hT, col0, nt, ncols):
        b = next_mm()
        cx.op("pe", acc_group(PB[b][:nt, 0:ncols], [(hT.t[:, k, col0:col0 + nt], w_ap_fn(k)) for k in range(8)]),
              reads=[w_dep, hT.d], writes=[PD[b]])
        return b

    def gate_math(hT, tiles, gs):
        ntl = len(tiles)
        gp = PB[ML][:, 0:8 * ntl].rearrange("p (t c) -> p t c", c=8)
        for ti, (col0, nt) in enumerate(tiles):
            cx.op("pe", acc_group(PB[ML][:nt, 8 * ti:8 * ti + 8], [(hT.t[:, k, col0:col0 + nt], wg.t[:, k, :]) for k in range(8)]),
                  reads=[wg.d, hT.d], writes=[PD[ML]])
        nt = tiles[0][1]
        ig, sp = gs["ig"], gs["sp"]
        cx.op("dve", lambda e: e.tensor_tensor(ig.t[:nt, 0:ntl, :], gp[:nt, :, 0:4],
                                               bif.t[:nt, 0:4].unsqueeze(1).to_broadcast([nt, ntl, 4]), ALU.add),
              reads=[bif.d], writes=[PD[ML], ig.d])
        cx.op("dve", lambda e: e.tensor_tensor(sp.t[:nt, 0:ntl, :], gp[:nt, :, 4:8],
                                               bif.t[:nt, 4:8].unsqueeze(1).to_broadcast([nt, ntl, 4]), ALU.add),
              reads=[bif.d], writes=[PD[ML], sp.d])
        spf = sp.t[:nt, 0:ntl, :].rearrange("p t h -> p (t h)")
        cx.op("act", lambda e: e.activation(spf, spf, AF.Exp, scale=-1.0), reads=[sp.d], writes=[sp.d])
        cx.op("act", lambda e: e.activation(spf, spf, AF.Ln, bias=eps_t.t[:nt, 2:3]), reads=[sp.d, eps_t.d], writes=[sp.d])
        w = 4 * ntl
        cx.op("pe", [mm(PB[ML][:nt, 128:128 + w], triu_f.t[:nt, :nt], spf),
                     mm(PB[ML][:, 192:192 + w], ones_f.t[:nt, :], spf)],
              reads=[triu_f.d, ones_f.d, sp.d], writes=[PD[ML]])
        negb = PB[ML][:nt, 128:128 + w]
        nbt = PB[ML][:, 192:192 + w]

        def fl(x):
            return x.t[:nt, 0:ntl, :].rearrange("p t h -> p (t h)")

        def flA(x):
            return x.t[:, 0:ntl, :].rearrange("p t h -> p (t h)")
        a, nB, amb, wA, wB, eb, eB = (gs[k] for k in ("a", "nB", "amb", "wA", "wB", "eb", "eB"))
        cx.op("dve", lambda e: e.tensor_tensor(fl(a), fl(ig), negb, ALU.add), reads=[ig.d], writes=[PD[ML], a.d])
        cx.op("dve", lambda e: e.tensor_copy(flA(nB), nbt), writes=[PD[ML], nB.d])
        cx.op("act", lambda e: e.activation(fl(eb), negb, AF.Exp, scale=-1.0), writes=[PD[ML], eb.d])
        cx.op("dve", lambda e: e.tensor_tensor(fl(amb), fl(a), fl(nB), ALU.subtract), reads=[a.d, nB.d], writes=[amb.d])
        cx.op("act", lambda e: e.activation(fl(wA), fl(a), AF.Exp, bias=eps_t.t[:nt, 1:2]), reads=[a.d, eps_t.d], writes=[wA.d])
        cx.op("act", lambda e: e.activation(fl(wB), fl(amb), AF.Exp, bias=eps_t.t[:nt, 1:2]), reads=[amb.d, eps_t.d], writes=[wB.d])
        cx.op("act", lambda e: e.activation(flA(eB), flA(nB), AF.Exp, scale=-1.0), reads=[nB.d], writes=[eB.d])

    def m_track(gs, ti, nt):
        a, nB = gs["a"], gs["nB"]
        t = small()
        cx.op("dve", lambda e: e.tensor_tensor(t.t[:nt, 0:4], a.t[:nt, ti, :], ngrun.t[:nt, :], ALU.add),
              reads=[a.d, ngrun.d], writes=[t.d])
        cx.op("dve", lambda e: e.tensor_tensor(amaxr.t[:nt, :], amaxr.t[:nt, :], t.t[:nt, 0:4], ALU.max),
              reads=[t.d, amaxr.d], writes=[amaxr.d])
        cx.op("dve", lambda e: e.tensor_tensor(ngrun.t[:, :], ngrun.t[:, :], nB.t[:, ti, :], ALU.add),
              reads=[nB.d, ngrun.d], writes=[ngrun.d])

    def new_gs(es_, ntl):
        return {k: TD(es_.enter_context(nc.sbuf_tensor(nm("gs_" + k), [128, ntl, 4], F32)), "gs_" + k)
                for k in ("ig", "sp", "a", "nB", "amb", "wA", "wB", "eb", "eB")}

    def state_update(K3, VB1, ti, nt, gs, with_bf, C32=C32, CBF=CBF):
        for h in range(4):
            cx.op("pe", mm(PB[ML][:, 256:385], K3.t[:nt, ti, h, :], VB1.t[:nt, ti, h, :]),
                  reads=[K3.d, VB1.d], writes=[PD[ML]])
            cx.op("dve", lambda e, h=h: e.scalar_tensor_tensor(C32.t[:, h, :], C32.t[:, h, :], gs["eB"].t[:, ti, h:h + 1],
                                                              PB[ML][:, 256:385], ALU.mult, ALU.add),
                  reads=[gs["eB"].d], writes=[PD[ML], C32.d])
        if with_bf:
            cx.op("act", lambda e: e.activation(CBF.t[:].rearrange("p h e -> p (h e)"),
                                                C32.t[:].rearrange("p h e -> p (h e)"), AF.Copy),
                  reads=[C32.d], writes=[CBF.d])

    def evac_k3(hT, col0, nt, ti, wkb, gs, K3):
        b = tm_proj(lambda k: wkb.t[:, k, :], wkb.d, hT, col0, nt, 512)
        cx.op("dve", lambda e: e.tensor_tensor(K3.t[:nt, ti, :, :], PB[b][:nt, :].rearrange("p (h d) -> p h d", h=4),
                                               gs["wB"].t[:nt, ti, :].unsqueeze(2).to_broadcast([nt, 4, 128]), ALU.mult),
              reads=[gs["wB"].d], writes=[PD[b], K3.d])

    def evac_vb1(hT, col0, nt, ti, wvb, VB1):
        b = tm_proj(lambda k: wvb.t[:, k, :], wvb.d, hT, col0, nt, 512)
        cx.op("act", lambda e: e.activation(VB1.t[:nt, ti, :, 0:128], PB[b][:nt, :].rearrange("p (h d) -> p h d", h=4), AF.Copy),
              writes=[PD[b], VB1.d])
        cx.op("dve", lambda e: e.memset(VB1.t[:nt, ti, :, 128:129], 1.0), writes=[VB1.d])

    try:
      for l in range(2):
          stop_at(1 + 10 * l)
          xin = XP if l == 0 else X1
          xin_d = None if l == 0 else d_x1
          xout = X1 if l == 0 else YP
          xsin = XS if l == 0 else XS1
          xsout = XS1 if l == 0 else YS

          sem_par = cx.shared_sem("par")
          cx.dma("sp", gbc.t[:].rearrange("p a d -> p (a d)"), NG.ap()[l].rearrange("a d -> (a d)").partition_broadcast(128),
                 writes=[gbc.d], sem=sem_par)
          cx.dma("sp", gatt.t[:], GATT.ap()[l].partition_broadcast(128), writes=[gatt.d], sem=sem_par)
          cx.dma("sp", gml.t[:], GML.ap()[l].partition_broadcast(128), writes=[gml.d], sem=sem_par)
          cx.dma("sp", bif.t[:, 0:4], BI.ap()[l].partition_broadcast(128), writes=[bif.d], sem=sem_par)
          cx.dma("sp", bif.t[:, 4:8], BFG.ap()[l].partition_broadcast(128), writes=[bif.d], sem=sem_par)
          cx.dma("pool", wg.t[:], WIN.ap()[l, :, C_G:C_G + 8].rearrange("(k p) n -> p k n", p=128), writes=[wg.d])

          with contextlib.ExitStack() as es_:
              def lt(name, shape, dt=F32):
                  return TD(es_.enter_context(nc.sbuf_tensor(nm(name), list(shape), dt)), name)
              mm_ring[0] = [MM0, MM1, TRB]
              XT = [lt(f"a_xt{i}", [128, D]) for i in range(2)]
              hTA = lt("a_hTA", [128, 8, TOK], BF16)
              sqs = [lt(f"a_sq{i}", [128, D]) for i in range(2)]; xns = [lt(f"a_xn{i}", [128, D], BF16) for i in range(2)]
              wkb = lt("a_wkb", [128, 8, 512], BF16); wvb = lt("a_wvb", [128, 8, 512], BF16)
              K3L = [lt(f"a_k3{i}", [128, 4, 128], BF16) for i in range(2)]
              VB1L = [lt(f"a_vb1{i}", [128, 4, 129], BF16) for i in range(2)]
              khT = lt("a_khT", [128, 4, 512], BF16); vh = lt("a_vh", [128, 4, 512], BF16)
              gs = new_gs(es_, NTILE)
              pre = lt("a_pre", [128, NTILE, 4]); wC = lt("a_wC", [128, NTILE, 4])
              cx.dma("pool", wkb.t[:], WIN.ap()[l, :, C_KB:C_KB + 512].rearrange("(k p) n -> p k n", p=128), writes=[wkb.d])
              cx.dma("pool", wvb.t[:], WIN.ap()[l, :, C_VB:C_VB + 512].rearrange("(k p) n -> p k n", p=128), writes=[wvb.d])
              ws.schedule([win_loader(l, C_KA), win_loader(l, C_VA)])
              XT4 = XT + sqs
              jk = [lt(f"a_jk{i}", [128, D], BF16) for i in range(2)]
              items = []
              for t in range(NTILE):
                  xt = XT4[t % 4]
                  items.append((xt, 128, t * 128))
              for t0 in range(0, NTILE, 4):
                  for t in range(t0, t0 + 4):
                      cx.dma("sp", XT4[t % 4].t[:, :], xin.ap()[t * 128:(t + 1) * 128, :], writes=[XT4[t % 4].d])
                  norm_seq(items[t0:t0 + 4], 0, hTA, xns, jk)
              gate_math(hTA, [(t * 128, 128) for t in range(NTILE)], gs)
              cx.op("dve", lambda e: e.memset(pre.t[:, 0, :], 0.0), writes=[pre.d])
              for t in range(1, NTILE):
                  cx.op("dve", lambda e, t=t: e.tensor_tensor(pre.t[:, t, :], pre.t[:, t - 1, :], gs["nB"].t[:, t - 1, :], ALU.add),
                        reads=[gs["nB"].d], writes=[pre.d])
              cx.op("dve", lambda e: e.tensor_tensor(ngrun.t[:, :], pre.t[:, NTILE - 1, :], gs["nB"].t[:, NTILE - 1, :], ALU.add),
                    reads=[gs["nB"].d, pre.d], writes=[ngrun.d])
              cx.op("dve", lambda e: e.tensor_tensor(gs["amb"].t[:], gs["a"].t[:], pre.t[:], ALU.add),
                    reads=[gs["a"].d, pre.d], writes=[gs["amb"].d])
              cx.op("dve", lambda e: e.tensor_reduce(amaxr.t[:, :], gs["amb"].t[:].rearrange("p t h -> p h t"), AX.X, ALU.max),
                    reads=[gs["amb"].d], writes=[amaxr.d])
              cx.op("dve", lambda e: e.tensor_tensor(pre.t[:], pre.t[:], ngrun.t[:, :].unsqueeze(1).to_broadcast([128, NTILE, 4]), ALU.subtract),
                    reads=[ngrun.d], writes=[pre.d])
              cx.op("dve", lambda e: e.tensor_tensor(gs["amb"].t[:], gs["a"].t[:], pre.t[:], ALU.add),
                    reads=[gs["a"].d, pre.d], writes=[gs["amb"].d])
              cx.op("act", lambda e: e.activation(wC.t[:].rearrange("p t h -> p (t h)"), gs["amb"].t[:].rearrange("p t h -> p (t h)"),
                                                  AF.Exp, bias=eps_t.t[:, 1:2]), reads=[gs["amb"].d, eps_t.d], writes=[wC.d])
              CB = [S0, S1, O0, O1]
              for t in range(NTILE):
                  K3, VB1 = K3L[t % 2], VB1L[t % 2]
                  b = tm_proj(lambda k: wkb.t[:, k, :], wkb.d, hTA, t * 128, 128, 512)
                  cx.op("dve", lambda e, t=t, b=b, K3=K3: e.tensor_tensor(K3.t[:, :, :], PB[b][:, :].rearrange("p (h d) -> p h d", h=4),
                                                                       wC.t[:, t, :].unsqueeze(2).to_broadcast([128, 4, 128]), ALU.mult),
                        reads=[wC.d], writes=[PD[b], K3.d])
                  b = tm_proj(lambda k: wvb.t[:, k, :], wvb.d, hTA, t * 128, 128, 512)
                  cx.op("act", lambda e, b=b, VB1=VB1: e.activation(VB1.t[:, :, 0:128], PB[b][:, :].rearrange("p (h d) -> p h d", h=4), AF.Copy),
                        writes=[PD[b], VB1.d])
                  cx.op("dve", lambda e, VB1=VB1: e.memset(VB1.t[:, :, 128:129], 1.0), writes=[VB1.d])
                  for h in range(4):
                      cx.op("pe", mm(PB[CB[h]][:, 0:129], K3.t[:, h, :], VB1.t[:, h, :], t == 0, t == NTILE - 1),
                            reads=[K3.d, VB1.d], writes=[PD[CB[h]]])
              for h in range(4):
                  cx.op("dve" if h % 2 else "act",
                        (lambda e, h=h: e.tensor_copy(C32.t[:, h, :], PB[CB[h]][:, 0:129])) if h % 2 else
                        (lambda e, h=h: e.activation(C32.t[:, h, :], PB[CB[h]][:, 0:129], AF.Copy)),
                        writes=[PD[CB[h]], C32.d])
              build_E(l, es_)
              hT = TD(None, "hT_alias"); hT.t = hTA.t[:, :, TOK - 512:TOK]; hT.d = hTA.d
              wka = ws.get()
              fm_proj(wka, range(4), hT, 512, khT, 0, 0)
              cx.dma("sp", EXK_S.ap()[:, :].rearrange("(c p) n -> p c n", p=128), khT.t[:], reads=[khT.d],
                     writes=[d_exk[0]], sem=sem_out)
              wva = ws.get()
              for i in range(4):
                  b = tm_proj(lambda k: wva.t[:, k, :], wva.d, hT, i * 128, 128, 512)
                  evac(vh.t[:, i, :], PB[b][:, :], b, vh.d)
              cx.dma("sp", EXV_S.ap()[:, :].rearrange("(c p) n -> p c n", p=128), vh.t[:], reads=[vh.d],
                     writes=[d_exv[0]], sem=sem_out)
              cx.dma("sp", EXS_S.ap()[0:128, 0:516], C32.t[:].rearrange("p h e -> p (h e)"), reads=[C32.d],
                     writes=[d_exs[0]], sem=sem_out)
              srow = lt("a_srow", [128, XCOLS])
              cx.op("dve", lambda e: e.memset(srow.t[0:1, :], 0.0), writes=[srow.d])
              cx.op("dve", lambda e: e.tensor_copy(srow.t[0:1, 0:4], ngrun.t[0:1, :]), reads=[ngrun.d], writes=[srow.d])
              cx.op("pe", lambda e: e.transpose(PB[ML][0:4, 0:128], amaxr.t[:, 0:4], ident_f.t[:, :]),
                    reads=[amaxr.d, ident_f.d], writes=[PD[ML]])
              mu4 = small()
              cx.op("dve", lambda e: e.tensor_reduce(mu4.t[0:4, 0:1], PB[ML][0:4, 0:128], AX.X, ALU.max),
                    writes=[PD[ML], mu4.d])
              cx.op("pe", mm(PB[ML][0:1, 200:204], mu4.t[0:4, 0:1], ident_f.t[0:4, 0:4]), reads=[mu4.d, ident_f.d], writes=[PD[ML]])
              cx.op("dve", lambda e: e.tensor_copy(srow.t[0:1, 4:8], PB[ML][0:1, 200:204]), writes=[PD[ML], srow.d])
              cx.dma("sp", EXS_S.ap()[128:129, :], srow.t[0:1, :], reads=[srow.d], writes=[d_exs[0]], sem=sem_out)
              stop_at(2 + 10 * l)
              cx.allgather(EXK_S.ap().opt(), EXK_D.ap().opt(), [d_exk[0]], [d_exk[1]], sem_cc)
              cx.allgather(EXV_S.ap().opt(), EXV_D.ap().opt(), [d_exv[0]], [d_exv[1]], sem_cc)
              cx.allgather(EXS_S.ap().opt(), EXS_D.ap().opt(), [d_exs[0]], [d_exs[1]], sem_cc)
              stop_at(3 + 10 * l)
              cx.barrier(exclude=(sem_cc.key,))

          with contextlib.ExitStack() as es_:
              def lt(name, shape, dt=F32):
                  return TD(es_.enter_context(nc.sbuf_tensor(nm(name), list(shape), dt)), name)
              KTA = lt("b_kta", [128, 4, 1024], BF16)
              VA = lt("b_va", [128, 8, 8, 65], BF16)
              def consume_exchange():
                  kxs = PT.t[:, 0:16, :].rearrange("p (c a) q -> p c (a q)", c=4)
                  vxs = PT.t[:, 16:32, :].rearrange("p (c a) q -> p c (a q)", c=4)
                  cxs = yb.t[:, 0:516]
                  scbt = small()
                  scb = scbt.t[:, 0:32].rearrange("p (w r h) -> p w r h", w=2, r=4)
                  for w_ in range(2):
                      cx.dma("sp", scb[:, w_, :, :], bass.AP(EXS_D, 128 * XCOLS + 4 * w_, [[0, 128], [XROWS * XCOLS, 4], [1, 4]]),
                             reads=[d_exs[1]], writes=[scbt.d])
                  er = small()
                  cx.op("act", lambda e: e.activation(er.t[:, 0:16], scb[:, 0, :, :].rearrange("p r h -> p (r h)"), AF.Exp, scale=-1.0),
                        reads=[scbt.d], writes=[er.d])
                  cx.op("dve", lambda e: e.tensor_scalar(er.t[:, 0:16], er.t[:, 0:16], -1.0, None, ALU.add), reads=[er.d], writes=[er.d])
                  cx.op("dve", lambda e: e.tensor_tensor(er.t[:, 0:16].rearrange("p (r h) -> p r h", r=4),
                                                         er.t[:, 0:16].rearrange("p (r h) -> p r h", r=4),
                                                         cm.t[:, 4:8].unsqueeze(2).to_broadcast([128, 4, 4]), ALU.mult),
                        reads=[er.d, cm.d], writes=[er.d])
                  cx.op("dve", lambda e: e.tensor_scalar(er.t[:, 0:16], er.t[:, 0:16], 1.0, None, ALU.add), reads=[er.d], writes=[er.d])
                  cx.op("dve", lambda e: e.memset(C32.t[:].rearrange("p h e -> p (h e)"), 0.0), writes=[C32.d])
                  kdst = KTA.t[:, :, 512:1024]
                  for r in range(3):
                      cx.dma("sp", kxs, EXK_D.ap()[r * 512:(r + 1) * 512, :].rearrange("(c p) n -> p c n", p=128),
                             reads=[d_exk[1]], writes=[PT.d], sem=cx.shared_sem("xst"))
                      cx.dma("sp", vxs, EXV_D.ap()[r * 512:(r + 1) * 512, :].rearrange("(c p) n -> p c n", p=128),
                             reads=[d_exv[1]], writes=[PT.d], sem=cx.shared_sem("xst"))
                      if r < 3:
                          cx.dma("sp", cxs, EXS_D.ap()[r * XROWS:r * XROWS + 128, 0:516], reads=[d_exs[1]], writes=[yb.d],
                                 sem=cx.shared_sem("xst"))
                      if r == 0:
                          cx.op("dve", lambda e, r=r: e.tensor_scalar(kdst, kxs, cm.t[:, r:r + 1], None, ALU.mult),
                                reads=[PT.d, cm.d], writes=[KTA.d])
                      else:
                          cx.op("dve", lambda e, r=r: e.scalar_tensor_tensor(kdst, kxs, cm.t[:, r:r + 1], kdst, ALU.mult, ALU.add),
                                reads=[PT.d, cm.d], writes=[KTA.d])
                      for tt_ in range(4):
                          vsrc = vxs[:, tt_, :].rearrange("p (h d) -> p h d", h=8)
                          vdst = VA.t[:, 4 + tt_, :, 0:64]
                          if r == 0:
                              cx.op("dve", lambda e, r=r, vsrc=vsrc, vdst=vdst: e.tensor_scalar(vdst, vsrc, cm.t[:, r:r + 1], None, ALU.mult),
                                    reads=[PT.d, cm.d], writes=[VA.d])
                          else:
                              cx.op("dve", lambda e, r=r, vsrc=vsrc, vdst=vdst: e.scalar_tensor_tensor(
                                  vdst, vsrc, cm.t[:, r:r + 1], vdst, ALU.mult, ALU.add), reads=[PT.d, cm.d], writes=[VA.d])
                      if r < 3:
                          for h in range(4):
                              cx.op("dve", lambda e, r=r, h=h: e.tensor_scalar(C32.t[:, h, :], C32.t[:, h, :], er.t[:, r * 4 + h:r * 4 + h + 1],
                                                                              None, ALU.mult), reads=[er.d], writes=[C32.d])
                              cx.op("dve", lambda e, r=r, h=h: e.scalar_tensor_tensor(C32.t[:, h, :], cxs[:, h * 129:(h + 1) * 129],
                                                                                     cm.t[:, 4 + r:5 + r], C32.t[:, h, :], ALU.mult, ALU.add),
                                    reads=[yb.d, cm.d], writes=[C32.d])
                  fl_ = small()
                  cx.op("dve", lambda e: e.tensor_reduce(fl_.t[:, 0:1], cm.t[:, 0:4], AX.X, ALU.add), reads=[cm.d], writes=[fl_.d])
                  for tt_ in range(4):
                      cx.op("dve", lambda e, tt_=tt_: e.tensor_copy(VA.t[:, 4 + tt_, :, 64:65],
                                                                   fl_.t[:, 0:1].unsqueeze(1).to_broadcast([128, 8, 1])),
                            reads=[fl_.d], writes=[VA.d])
                  cx.op("act", lambda e: e.activation(CBF.t[:].rearrange("p h e -> p (h e)"),
                                                      C32.t[:].rearrange("p h e -> p (h e)"), AF.Copy), reads=[C32.d], writes=[CBF.d])
                  mt = small()
                  cx.op("dve", lambda e: e.memset(mt.t[:, 0:16], 0.0), writes=[mt.d])
                  for r in range(4):
                      cx.op("dve", lambda e, r=r: e.tensor_tensor(mt.t[:, 12:16], scb[:, 1, r, :], mt.t[:, 0:4], ALU.add),
                            reads=[scbt.d], writes=[mt.d])
                      cx.op("dve", lambda e, r=r: e.tensor_scalar(mt.t[:, 16:17], cm.t[:, 8 + r:9 + r], -1.0, -NEG, ALU.add, ALU.mult),
                            reads=[cm.d], writes=[mt.d])
                      cx.op("dve", lambda e: e.tensor_scalar(mt.t[:, 12:16], mt.t[:, 12:16], mt.t[:, 16:17], None, ALU.add),
                            writes=[mt.d])
                      cx.op("dve", lambda e: e.tensor_tensor(mt.t[:, 4:8], mt.t[:, 4:8], mt.t[:, 12:16], ALU.max), writes=[mt.d])
                      cx.op("dve", lambda e, r=r: e.tensor_tensor(mt.t[:, 0:4], mt.t[:, 0:4], scb[:, 0, r, :], ALU.add),
                            reads=[scbt.d], writes=[mt.d])
                      cx.op("dve", lambda e, r=r: e.scalar_tensor_tensor(mt.t[:, 8:12], scb[:, 0, r, :], cm.t[:, 8 + r:9 + r],
                                                                        mt.t[:, 8:12], ALU.mult, ALU.add),
                            reads=[scbt.d, cm.d], writes=[mt.d])
                  mE = ME[l]
                  cx.op("dve", lambda e: e.tensor_tensor(mE.t[:, 0:4], mt.t[:, 4:8], mt.t[:, 8:12], ALU.subtract),
                        reads=[mt.d], writes=[mE.d])
                  cx.op("act", lambda e: e.activation(mE.t[:, 4:8], mE.t[:, 0:4], AF.Exp, scale=-1.0), reads=[mE.d], writes=[mE.d])
                  cx.dma("sp", MP.ap()[l:l + 1, :], mE.t[0:1, 0:4], reads=[mE.d], writes=[d_out], sem=sem_out)

              hT = lt("b_hT", [128, 8, 512], BF16); sq = lt("b_sq", [128, D]); xn = lt("b_xn", [128, D], BF16)
              xn2 = lt("b_xn2", [128, D], BF16); jk0 = TD(None, "jk0"); jk0.t = sq.t[:, 0:512].bitcast(BF16); jk0.d = sq.d
              jk1 = TD(None, "jk1"); jk1.t = sq.t[:, 512:1024].bitcast(BF16); jk1.d = sq.d
              QTA = lt("b_qta", [128, 8, 512], BF16)
              QTB = lt("b_qtb", [128, 4, 512], BF16); KTB = lt("b_ktb", [128, 4, 512], BF16)
              K3 = lt("b_k3", [128, 4, 4, 128], BF16); VB1 = lt("b_vb1", [128, 4, 4, 129], BF16)
              OG = lt("b_og", [128, 4, 512])
              PT = lt("b_pt", [128, 40, 128], BF16)
              wout = lt("b_wout", [128, 8, D], BF16)
              att = lt("b_att", [128, 8, 64]); mix = lt("b_mix", [128, D], BF16); mix2 = lt("b_mix2", [128, D], BF16); mixT = lt("b_mixT", [128, 8, 128], BF16)
              yb = lt("b_y", [128, D]); hh = lt("b_hh", [128, 4, 128]); GT = lt("b_gt", [128, 4, 128], BF16)
              XK = [lt(f"b_xk{i}", [128, D]) for i in range(4)]
              gs = new_gs(es_, 4)
              cx.dma("pool", wout.t[:], WOUT.ap()[l].rearrange("(k p) n -> p k n", p=128), writes=[wout.d])
              cx.op("dve", lambda e: e.memset(QTA.t[:].rearrange("p h q -> p (h q)"), 0.0), writes=[QTA.d])
              ov = [PB[O0][:, 0:260].rearrange("p (h e) -> p h e", h=4), PB[O1][:, 0:260].rearrange("p (h e) -> p h e", h=4)]
              numv = [PB[O0][:, :].rearrange("p (h e) -> p h e", h=2), PB[O1][:, :].rearrange("p (h e) -> p h e", h=2)]
              trv = PB[TRB][:].bitcast(BF16).rearrange("p (k c) -> p k c", k=8)

              if do_sample:
                  hTs = lt("s_hT", [128, 8, 16], BF16)
                  QTAs = lt("s_qta", [128, 8, 16], BF16); KTAs = lt("s_kta", [128, 4, 16], BF16)
                  QTBs = lt("s_qtb", [128, 4, 16], BF16); KTBs = lt("s_ktb", [128, 4, 16], BF16)
                  K3s = lt("s_k3", [128, 1, 4, 128], BF16); VB1s = lt("s_vb1", [128, 1, 4, 129], BF16)
                  OGs = lt("s_og", [128, 1, 512]); VAs = lt("s_va", [128, 8, 65], BF16)
                  KcT = lt("s_kct", [128, 4, 512], BF16); Vc = lt("s_vc", [128, 4, 8, 65], BF16)
                  ckb = TD(None, "ckb_alias"); ckb.t = None; ckb.d = None
                  PTs = lt("s_pt", [128, 8, 5, 16], BF16)
                  C32s = lt("s_c32", [128, 4, 129]); CBFs = lt("s_cbf", [128, 4, 129], BF16)
                  m0b = lt("s_m0", [128, 16]); XKs = lt("s_xk", [128, D])
                  gss = new_gs(es_, 1)
                  cx.op("dve", lambda e: e.memset(QTAs.t[:].rearrange("p h q -> p (h q)"), 0.0), writes=[QTAs.d])
                  def sample_setup():
                      ckb.t = yb.t[:, :].bitcast(BF16).rearrange("p (j q) -> p j q", j=4)
                      ckb.d = yb.d
                      for j in range(4):
                          cx.dma("pool", ckb.t[:, j, :].rearrange("p (h d) -> p h d", h=8),
                                 CK.ap()[l, :, j * 128:(j + 1) * 128, :].rearrange("h t d -> t h d"), writes=[ckb.d])
                          cx.dma("pool", Vc.t[:, j, :, 0:64], CV.ap()[l, :, j * 128:(j + 1) * 128, :].rearrange("h t d -> t h d"),
                                 writes=[Vc.d])
                          cx.op("dve", lambda e, j=j: e.memset(Vc.t[:, j, :, 64:65], 1.0), writes=[Vc.d])
                      for jj in range(2):
                          cx.op("pe", [(lambda e, a_=a_, hp_=hp_: e.transpose(trv[:, a_ * 4 + hp_, :],
                                                                               ckb.t[:, 2 * jj + a_, hp_ * 128:(hp_ + 1) * 128], ident_b.t[:, :]))
                                       for a_ in range(2) for hp_ in range(4)], reads=[ckb.d, ident_b.d], writes=[PD[TRB]])
                          for a_ in range(2):
                              j = 2 * jj + a_
                              cx.op("act", lambda e, a_=a_, j=j: e.activation(KcT.t[:, :, j * 128:(j + 1) * 128], trv[:, a_ * 4:(a_ + 1) * 4, :], AF.Copy),
                                    writes=[PD[TRB], KcT.d])
                      cx.dma("sp", C32s.t[:, :, 0:128], SC.ap()[l].rearrange("h d e -> d h e"), writes=[C32s.d])
                      with nc.allow_non_contiguous_dma(reason="n state column"):
                          cx.dma("sp", C32s.t[:, :, 128], SN.ap()[l].rearrange("h d -> d h"), writes=[C32s.d])
                      cx.dma("sp", m0b.t[:, 0:4], SM.ap()[l].partition_broadcast(128), writes=[m0b.d])
                      cx.op("act", lambda e: e.activation(m0b.t[:, 4:8], m0b.t[:, 0:4], AF.Exp), reads=[m0b.d], writes=[m0b.d])
                      for h in range(4):
                          cx.op("dve", lambda e, h=h: e.tensor_scalar(C32s.t[:, h, :], C32s.t[:, h, :], m0b.t[:, 4 + h:5 + h], None, ALU.mult),
                                reads=[m0b.d], writes=[C32s.d])
                      cx.op("act", lambda e: e.activation(CBFs.t[:].rearrange("p h e -> p (h e)"),
                                                          C32s.t[:].rearrange("p h e -> p (h e)"), AF.Copy), reads=[C32s.d], writes=[CBFs.d])

              PTD = [Dep(f"pt{n_}") for n_ in range(10)]
              GTD = [Dep(f"gt{h_}") for h_ in range(4)]

              def attn_prompt(tg, c0, slices):
                  def S(n):
                      sbk = S0 if n % 2 == 0 else S1
                      fns = []
                      for q in range(4):
                          blk = n * 4 + q
                          h, j = blk // 5, blk % 5
                          kt = (tg - 4 + j) % 8
                          fns.append(mm(PB[sbk][:, q * 128:(q + 1) * 128], KTA.t[:, h // 2, kt * 128:(kt + 1) * 128],
                                        QTA.t[:, h, c0:c0 + 128]))
                      cx.op("pe", fns, reads=[KTA.d, QTA.d], writes=[PD[sbk]])
                      return sbk

                  def pv(h):
                      ob = O0 if h < 4 else O1
                      pairs = []
                      for j in range(5):
                          kt = (tg - 4 + j) % 8
                          pairs.append((PT.t[:, h * 5 + j, :], VA.t[:, kt, h, :]))
                      n0, n1 = (5 * h) // 4, (5 * h + 4) // 4
                      cx.op("pe", acc_group(ov[h // 4][:, h % 4, :], pairs), reads=[PTD[n0], PTD[n1], VA.d], writes=[PD[ob]])
                  done_h = 0
                  sbk = S(0)
                  for n in range(10):
                      nxt = S(n + 1) if n + 1 < 10 else None
                      ptv = PT.t[:, n * 4:(n + 1) * 4, :].rearrange("p b q -> p (b q)")
                      cx.op("act", lambda e, sbk=sbk, ptv=ptv: e.activation(ptv, PB[sbk][:, :], AF.Exp, scale=0.125),
                            writes=[PD[sbk], PTD[n]])
                      ev = EALL[l].t[:].rearrange("p h j q -> p (h j) q")[:, n * 4:(n + 1) * 4, :].rearrange("p b q -> p (b q)")
                      cx.op("dve", lambda e, ptv=ptv, ev=ev: e.tensor_tensor(ptv, ptv, ev, ALU.mult), reads=[EALL[l].d], writes=[PTD[n]])
                      while (done_h + 1) * 5 <= (n + 1) * 4:
                          pv(done_h)
                          done_h += 1
                      if slices:
                          slices.pop(0)()
                      sbk = nxt
                  while slices:
                      slices.pop(0)()

              def attn_sample():
                  for hb, sbk in ((0, S0), (1, S1)):
                      fns = []
                      for hq in range(4):
                          h = hb * 4 + hq
                          for j in range(4):
                              fns.append(mm(PB[sbk][:, hq * 80 + j * 16:hq * 80 + j * 16 + 16], KcT.t[:, h // 2, j * 128:(j + 1) * 128],
                                            QTAs.t[:, h, 0:16]))
                          fns.append(mm(PB[sbk][0:16, hq * 80 + 64:hq * 80 + 80], KTAs.t[:, h // 2, 0:16], QTAs.t[:, h, 0:16]))
                      cx.op("pe", fns, reads=[KcT.d, KTAs.d, QTAs.d], writes=[PD[sbk]])
                      ptv = PTs.t[:, hb * 4:(hb + 1) * 4, :, :].rearrange("p h j q -> p (h j q)")
                      cx.op("act", lambda e, sbk=sbk, ptv=ptv: e.activation(ptv, PB[sbk][:, 0:320], AF.Exp, scale=0.125),
                            writes=[PD[sbk], PTs.d])
                  cx.op("dve", lambda e: e.tensor_tensor(PTs.t[:, :, :, :].rearrange("p h j q -> p (h j) q"),
                                                         PTs.t[:, :, :, :].rearrange("p h j q -> p (h j) q"),
                                                         EALL[l].t[:].rearrange("p h j q -> p (h j) q")[:, :, 0:16], ALU.mult),
                        reads=[EALL[l].d], writes=[PTs.d])
                  for h in range(8):
                      ob = O0 if h < 4 else O1
                      pairs = [(PTs.t[:, h, j, :], Vc.t[:, j, h, :]) for j in range(4)]
                      pairs.append((PTs.t[0:16, h, 4, :], VAs.t[0:16, h, :]))
                      cx.op("pe", acc_group(ov[h // 4][0:16, h % 4, :], pairs), reads=[PTs.d, Vc.d, VAs.d], writes=[PD[ob]])

              def attn_epi(nt, mix):
                  rd = small()
                  for hb, ob in ((0, O0), (1, O1)):
                      cx.op("dve", lambda e, hb=hb: e.reciprocal(rd.t[:nt, hb * 4:hb * 4 + 4], ov[hb][:nt, :, 64]),
                            writes=[PD[ob], rd.d])
                      cx.op("dve", lambda e, hb=hb: e.tensor_tensor(att.t[:nt, hb * 4:hb * 4 + 4, :], ov[hb][:nt, :, 0:64],
                                                                    rd.t[:nt, hb * 4:hb * 4 + 4].unsqueeze(2).to_broadcast([nt, 4, 64]),
                                                                    ALU.mult), reads=[rd.d], writes=[PD[ob], att.d])
                  attf = att.t[:nt].rearrange("p h d -> p (h d)")
                  st = small()
                  cx.op("act", lambda e: e.activation(sq.t[:nt, 0:512], attf, AF.Square, accum_out=st.t[:nt, 0:1]),
                        reads=[att.d], writes=[sq.d, st.d])
                  rstd_from_ss(st, nt, 0, 1, 1.0 / 512)
                  cx.op("dve", lambda e: e.scalar_tensor_tensor(mix.t[:nt, 0:512], attf, st.t[:nt, 1:2], gatt.t[:nt, :], ALU.mult, ALU.mult),
                        reads=[att.d, st.d, gatt.d], writes=[mix.d])

              numv = [PB[MM0][:, :].rearrange("p (h e) -> p h e", h=2), PB[MM1][:, :].rearrange("p (h e) -> p h e", h=2)]
              NB_ = (MM0, MM1)

              def make_slices(nt, c0, ti, QTBx, KTBx, K3x, VB1x, OGx, ogd, gsx, C32x, CBFx, track, xk, after, mix):
                  sl = []

                  def head(h):
                      def f():
                          cx.op("pe", mm(PB[ML][:nt, 0:nt], KTBx.t[:, h, c0:c0 + nt], QTBx.t[:, h, c0:c0 + nt]),
                                reads=[KTBx.d, QTBx.d], writes=[PD[ML]])
                          cx.op("dve", lambda e: e.scalar_tensor_tensor(GT.t[:nt, h, :nt], PB[ML][:nt, 0:nt],
                                                                        gsx["wA"].t[:nt, ti, h:h + 1], masku_b.t[:nt, :nt],
                                                                        ALU.mult, ALU.mult),
                                reads=[gsx["wA"].d, masku_b.d], writes=[PD[ML], GTD[h]])
                          cx.op("pe", [mm(numv[h // 2][:nt, h % 2, 0:129], GT.t[:nt, h, :nt], VB1x.t[:nt, ti, h, :], True, False),
                                       mm(numv[h // 2][:nt, h % 2, 0:129], QTBx.t[:, h, c0:c0 + nt], CBFx.t[:, h, :], False, True)],
                                reads=[GTD[h], VB1x.d, QTBx.d, CBFx.d], writes=[PD[NB_[h // 2]]])
                      return f
                  for h in range(4):
                      sl.append(head(h))

                  def st_upd():
                      if track:
                          m_track(gsx, ti, nt)
                      state_update(K3x, VB1x, ti, nt, gsx, True, C32x, CBFx)
                  sl.append(st_upd)

                  def hnorm():
                      dn = small()
                      for hb in range(2):
                          cx.op("dve", lambda e, hb=hb: e.tensor_tensor(dn.t[:nt, hb * 2:hb * 2 + 2], numv[hb][:nt, :, 128],
                                                                        gsx["eb"].t[:nt, ti, hb * 2:hb * 2 + 2], ALU.mult),
                                reads=[gsx["eb"].d], writes=[PD[NB_[hb]], dn.d])
                      cx.op("act", lambda e: e.activation(dn.t[:nt, 0:4], dn.t[:nt, 0:4], AF.Abs), writes=[dn.d])
                      cx.op("dve", lambda e: e.tensor_scalar(dn.t[:nt, 0:4], dn.t[:nt, 0:4], 1.0, None, ALU.max), writes=[dn.d])
                      cx.op("dve", lambda e: e.reciprocal(dn.t[:nt, 0:4], dn.t[:nt, 0:4]), writes=[dn.d])
                      cx.op("dve", lambda e: e.tensor_tensor(dn.t[:nt, 4:8], dn.t[:nt, 0:4], gsx["eb"].t[:nt, ti, :], ALU.mult),
                            reads=[gsx["eb"].d], writes=[dn.d])
                      for hb in range(2):
                          cx.op("dve", lambda e, hb=hb: e.tensor_tensor(hh.t[:nt, hb * 2:hb * 2 + 2, :], numv[hb][:nt, :, 0:128],
                                                                        dn.t[:nt, 4 + hb * 2:6 + hb * 2].unsqueeze(2).to_broadcast([nt, 2, 128]),
                                                                        ALU.mult), reads=[dn.d], writes=[PD[NB_[hb]], hh.d])
                      hhf = hh.t[:nt].rearrange("p h d -> p (h d)")
                      cx.op("act", lambda e: e.activation(sq.t[:nt, 0:512], hhf, AF.Square), reads=[hh.d], writes=[sq.d])
                      s4 = small()
                      cx.op("dve", lambda e: e.tensor_reduce(s4.t[:nt, 0:4], sq.t[:nt, 0:512].rearrange("p (h d) -> p h d", h=4), AX.X, ALU.add),
                            reads=[sq.d], writes=[s4.d])
                      rstd_from_ss(s4, nt, 0, 4, 1.0 / 128, width=4)
                      cx.op("dve", lambda e: e.tensor_tensor(hh.t[:nt], hh.t[:nt], s4.t[:nt, 4:8].unsqueeze(2).to_broadcast([nt, 4, 128]), ALU.mult),
                            reads=[s4.d], writes=[hh.d])
                      cx.op("dve", lambda e: e.tensor_tensor(mix.t[:nt, 512:1024], hhf, OGx, ALU.mult),
                            reads=[hh.d, ogd], writes=[mix.d])
                  sl.append(hnorm)

                  def tr():
                      transpose_to(mix, nt, 8, mixT, 0, eng="dve")
                  sl.append(tr)

                  def wo(half):
                      def f():
                          b = TRB
                          cx.op("pe", acc_group(PB[b][:nt, :], [(mixT.t[:, k, 0:nt], wout.t[:, k, half * 512:(half + 1) * 512]) for k in range(8)]),
                                reads=[mixT.d, wout.d], writes=[PD[b]])
                          evac(yb.t[:nt, half * 512:(half + 1) * 512], PB[b][:nt, :], b, yb.d)
                      return f
                  sl.append(wo(0))
                  sl.append(wo(1))

                  def res():
                      st = small()
                      cx.op("act", lambda e: e.activation(sq.t[:nt, :], yb.t[:nt, :], AF.Square, accum_out=st.t[:nt, 0:1]),
                            reads=[yb.d], writes=[sq.d, st.d])
                      rstd_from_ss(st, nt, 0, 1, 1.0 / D)
                      cx.op("dve", lambda e: e.scalar_tensor_tensor(yb.t[:nt, :], yb.t[:nt, :], st.t[:nt, 1:2], gbc.t[:nt, 1, :], ALU.mult, ALU.mult),
                            reads=[st.d, gbc.d], writes=[yb.d])
                      cx.op("dve", lambda e: e.tensor_tensor(xk.t[:nt, :], xk.t[:nt, :], yb.t[:nt, :], ALU.add),
                            reads=[yb.d], writes=[xk.d])
                      after()
                  sl.append(res)
                  return sl[0:6], sl[6:10]

              stop_at(4 + 10 * l)
              blocks = []
              for g in range(4):
                  blocks += [win_loader(l, C_QA), win_loader(l, C_KA), win_loader(l, C_QB), win_loader(l, C_KB),
                             win_loader(l, C_VB), win_loader(l, C_OB), win_loader(l, C_VA)]
              ws.schedule(blocks)

              for g in range(4):
                  smp = do_sample and g == 3
                  mm_ring[0] = RING6
                  for i in range(4):
                      r0 = (g * 4 + i) * 128
                      if xin_d is not None:
                          cx.dma("sp", XK[i].t[:, :], xin.ap()[r0:r0 + 128, :], reads=[xin_d], writes=[XK[i].d])
                      else:
                          cx.dma("sp", XK[i].t[:, :], xin.ap()[r0:r0 + 128, :], writes=[XK[i].d])
                  norm_seq([(XK[i], 128, i * 128) for i in range(4)], 0, hT, [xn, xn2], [jk0, jk1])
                  tiles = [(i * 128, 128) for i in range(4)]
                  gate_math(hT, tiles, gs)
                  if g == 0:
                      with nc.allow_non_contiguous_dma(reason="tiny conv params"):
                          cx.dma("sp", wcT.t[:], WCV.ap()[l].rearrange("j (c p) -> p j c", p=128), writes=[wcT.d], sem=cx.shared_sem("parc"))
                          cx.dma("sp", bcT.t[:], BCV.ap()[l].rearrange("(c p) -> p c", p=128), writes=[bcT.d], sem=cx.shared_sem("parc"))
                  if smp:
                      sample_setup()
                      if l == 0:
                          cx.dma("sp", XKs.t[0:16, :], xsin.ap()[:, :], writes=[XKs.d])
                      else:
                          cx.dma("sp", XKs.t[0:16, :], xsin.ap()[:, :], reads=[d_xs1], writes=[XKs.d])
                      norm_T(XKs, 16, 0, hTs, 0, sq, xn)
                      gate_math(hTs, [(0, 16)], gss)
                  slot_g = (g % 2) * 4

                  def q_masked(w, hsrc, ncols, dst):
                      for hp_ in range(4):
                          b = next_mm()
                          cx.op("pe", acc_group(PB[b][:, 0:ncols], [(w.t[:, k, hp_ * 128:(hp_ + 1) * 128], hsrc.t[:, k, 0:ncols]) for k in range(8)]),
                                reads=[w.d, hsrc.d], writes=[PD[b]])
                          cx.op("act", lambda e, hp_=hp_, b=b: e.activation(dst.t[0:64, 2 * hp_, 0:ncols], PB[b][0:64, 0:ncols], AF.Copy),
                                writes=[PD[b], dst.d])
                          cx.op("dve", lambda e, hp_=hp_, b=b: e.tensor_copy(dst.t[64:128, 2 * hp_ + 1, 0:ncols], PB[b][64:128, 0:ncols]),
                                writes=[PD[b], dst.d])
                  w = ws.get()
                  q_masked(w, hT, 512, QTA)
                  if smp:
                      q_masked(w, hTs, 16, QTAs)
                  w = ws.get(); fm_proj(w, range(4), hT, 512, KTA, 0, slot_g * 128)
                  if g == 3:
                      ko = TD(yb.t, "ko_alias"); ko.d = yb.d
                      for i in range(4):
                          b = tm_proj(lambda k: w.t[:, k, :], w.d, hT, i * 128, 128, 512)
                          evac(yb.t[:, 0:512], PB[b][:, :], b, yb.d)
                          cx.dma("sp", KP.ap()[l, :, i * 128:(i + 1) * 128, :].rearrange("h t d -> t h d"),
                                 yb.t[:, 0:512].rearrange("p (h d) -> p h d", h=8), reads=[yb.d], writes=[d_out], sem=sem_out)
                  if smp:
                      fm_proj(w, range(4), hTs, 16, KTAs, 0, 0)
                      b = tm_proj(lambda k: w.t[:, k, :], w.d, hTs, 0, 16, 512)
                      evac(yb.t[0:16, 0:512], PB[b][0:16, :], b, yb.d)
                      cx.dma("sp", KS.ap()[l].rearrange("h t d -> t h d"), yb.t[0:16, 0:512].rearrange("p (h d) -> p h d", h=8),
                             reads=[yb.d], writes=[d_out], sem=sem_out)
                  w = ws.get(); fm_proj(w, range(4), hT, 512, QTB, 0, 0)
                  if smp:
                      fm_proj(w, range(4), hTs, 16, QTBs, 0, 0)
                  w = ws.get(); fm_proj(w, range(4), hT, 512, KTB, 0, 0)
                  for i in range(4):
                      evac_k3(hT, i * 128, 128, i, w, gs, K3)
                  if smp:
                      fm_proj(w, range(4), hTs, 16, KTBs, 0, 0)
                      evac_k3(hTs, 0, 16, 0, w, gss, K3s)
                  w = ws.get()
                  for i in range(4):
                      evac_vb1(hT, i * 128, 128, i, w, VB1)
                  if smp:
                      evac_vb1(hTs, 0, 16, 0, w, VB1s)

                  def og_tile(hsrc, col0, nt, dst_ap, dd):
                      b = tm_proj(lambda k: w.t[:, k, :], w.d, hsrc, col0, nt, 512)
                      cx.op("act", lambda e: e.activation(dst_ap, PB[b][:nt, :], AF.Exp, scale=-1.0), writes=[PD[b], dd])
                      cx.op("dve", lambda e: e.tensor_scalar(dst_ap, dst_ap, 1.0, None, ALU.add), writes=[dd])
                      cx.op("dve", lambda e: e.reciprocal(dst_ap, dst_ap), writes=[dd])
                      cx.op("dve", lambda e: e.tensor_tensor(dst_ap, dst_ap, gml.t[:nt, :], ALU.mult), reads=[gml.d], writes=[dd])
                  w = ws.get()
                  for i in range(4):
                      og_tile(hT, i * 128, 128, OG.t[:, i, :], OG.d)
                  if smp:
                      og_tile(hTs, 0, 16, OGs.t[0:16, 0, :], OGs.d)
                  w = ws.get()
                  for i in range(4):
                      b = tm_proj(lambda k: w.t[:, k, :], w.d, hT, i * 128, 128, 512)
                      cx.op("act", lambda e, i=i, b=b: e.activation(VA.t[:, slot_g + i, :, 0:64],
                                                                    PB[b][:, :].rearrange("p (h d) -> p h d", h=8), AF.Copy),
                            writes=[PD[b], VA.d])
                      cx.op("dve", lambda e, i=i: e.memset(VA.t[:, slot_g + i, :, 64:65], 1.0), writes=[VA.d])
                      if g == 3:
                          cx.op("dve", lambda e, b=b: e.tensor_copy(yb.t[:, 512:1024], PB[b][:, :]), writes=[PD[b], yb.d])
                          cx.dma("sp", VP.ap()[l, :, i * 128:(i + 1) * 128, :].rearrange("h t d -> t h d"),
                                 yb.t[:, 512:1024].rearrange("p (h d) -> p h d", h=8), reads=[yb.d], writes=[d_out], sem=sem_out)
                  if smp:
                      b = tm_proj(lambda k: w.t[:, k, :], w.d, hTs, 0, 16, 512)
                      cx.op("act", lambda e, b=b: e.activation(VAs.t[0:16, :, 0:64], PB[b][0:16, :].rearrange("p (h d) -> p h d", h=8), AF.Copy),
                            writes=[PD[b], VAs.d])
                      cx.op("dve", lambda e: e.memset(VAs.t[0:16, :, 64:65], 1.0), writes=[VAs.d])
                      cx.op("dve", lambda e, b=b: e.tensor_copy(yb.t[0:16, 512:1024], PB[b][0:16, :]), writes=[PD[b], yb.d])
                      cx.dma("sp", VS.ap()[l].rearrange("h t d -> t h d"), yb.t[0:16, 512:1024].rearrange("p (h d) -> p h d", h=8),
                             reads=[yb.d], writes=[d_out], sem=sem_out)

                  stop_at(41 + 100 * l)
                  if g == 3:
                      stop_at(45 + 100 * l)
                  if g == 0:
                      consume_exchange()
                  mm_ring[0] = RING2
                  back_prev = []
                  mixL = [mix, mix2]
                  for i in range(4):
                      tg = g * 4 + i
                      mixx = mixL[i % 2]

                      def after(i=i, tg=tg):
                          cx.dma("sp", XM.ap()[tg * 128:(tg + 1) * 128, :], XK[i].t[:, :], reads=[XK[i].d], writes=[d_xm], sem=sem_out)
                          if tg == NTILE - 1:
                              cx.dma("sp", EXX_S.ap()[:, :], XK[i].t[126:128, :], reads=[XK[i].d], writes=[d_exx[0]], sem=sem_out)
                      front, back = make_slices(128, i * 128, i, QTB, KTB, K3, VB1, OG.t[:, i, :], OG.d, gs, C32, CBF, True, XK[i], after, mixx)
                      order = []
                      bp, fr = list(back_prev), list(front)
                      while bp or fr:
                          if fr:
                              order.append(fr.pop(0))
                          if bp:
                              order.append(bp.pop(0))
                      attn_prompt(tg, i * 128, order)
                      attn_epi(128, mixx)
                      back_prev = back
                  for f_ in back_prev:
                      f_()
                  if smp:
                      def after_s():
                          cx.dma("sp", XSM.ap()[:, :], XKs.t[0:16, :], reads=[XKs.d], writes=[d_xsm], sem=sem_out)
                      front, back = make_slices(16, 0, 0, QTBs, KTBs, K3s, VB1s, OGs.t[0:16, 0, :], OGs.d, gss, C32s, CBFs, False, XKs, after_s, mix)
                      attn_sample()
                      attn_epi(16, mix)
                      for f_ in front + back:
                          f_()
                      cx.op("pe", lambda e: e.transpose(PB[ML][0:4, 0:16], gss["a"].t[0:16, 0, :], ident_f.t[0:16, 0:16]),
                            reads=[gss["a"].d, ident_f.d], writes=[PD[ML]])
                      mu4 = small()
                      cx.op("dve", lambda e: e.tensor_reduce(mu4.t[0:4, 0:1], PB[ML][0:4, 0:16], AX.X, ALU.max), writes=[PD[ML], mu4.d])
                      cx.op("pe", mm(PB[ML][0:1, 200:204], mu4.t[0:4, 0:1], ident_f.t[0:4, 0:4]), reads=[mu4.d, ident_f.d], writes=[PD[ML]])
                      cx.op("dve", lambda e: e.tensor_copy(mu4.t[0:1, 4:8], PB[ML][0:1, 200:204]), writes=[PD[ML], mu4.d])
                      cx.op("pe", mm(PB[ML][:, 208:212], ones_f.t[0:1, :], mu4.t[0:1, 4:8]), reads=[mu4.d, ones_f.d], writes=[PD[ML]])
                      cx.op("dve", lambda e: e.tensor_tensor(m0b.t[:, 8:12], m0b.t[:, 0:4], PB[ML][:, 208:212], ALU.max),
                            writes=[PD[ML], m0b.d])
                      cx.op("dve", lambda e: e.tensor_tensor(m0b.t[:, 8:12], m0b.t[:, 8:12], gss["nB"].t[:, 0, :], ALU.subtract),
                            reads=[gss["nB"].d], writes=[m0b.d])
                      cx.op("act", lambda e: e.activation(m0b.t[:, 12:16], m0b.t[:, 8:12], AF.Exp, scale=-1.0), writes=[m0b.d])
                      cos_ = hh
                      for h in range(4):
                          cx.op("dve", lambda e, h=h: e.tensor_scalar(sq.t[:, h * 129:(h + 1) * 129], C32s.t[:, h, :], m0b.t[:, 12 + h:13 + h], None, ALU.mult),
                                reads=[C32s.d, m0b.d], writes=[sq.d])
                      sqv = sq.t[:, 0:516].rearrange("p (h e) -> p h e", h=4)
                      cx.dma("sp", CS.ap()[l].rearrange("h d e -> d h e"), sqv[:, :, 0:128], reads=[sq.d], writes=[d_out], sem=sem_out)
                      with nc.allow_non_contiguous_dma(reason="n state column"):
                          cx.dma("sp", NS.ap()[l].rearrange("h d -> d h"), sqv[:, :, 128], reads=[sq.d], writes=[d_out], sem=sem_out)
                      cx.dma("sp", MS.ap()[l:l + 1, :], m0b.t[0:1, 8:12], reads=[m0b.d], writes=[d_out], sem=sem_out)

              mE = ME[l]
              co = TD(None, "co_alias"); co.t = sq.t[:, 0:516].rearrange("p (h e) -> p h e", h=4); co.d = sq.d
              for h in range(4):
                  cx.op("dve", lambda e, h=h: e.tensor_scalar(co.t[:, h, :], C32.t[:, h, :], mE.t[:, 4 + h:5 + h], None, ALU.mult),
                        reads=[C32.d, mE.d], writes=[co.d])
              cx.dma("sp", CP.ap()[l].rearrange("h d e -> d h e"), co.t[:, :, 0:128], reads=[co.d], writes=[d_out], sem=sem_out)
              with nc.allow_non_contiguous_dma(reason="n state column"):
                  cx.dma("sp", NP.ap()[l].rearrange("h d -> d h"), co.t[:, :, 128], reads=[co.d], writes=[d_out], sem=sem_out)
              stop_at(5 + 10 * l)
              cx.allgather(EXX_S.ap().opt(), EXX_D.ap().opt(), [d_exx[0]], [d_exx[1]], sem_cc)
              cx.barrier(exclude=(sem_cc.key,))

          with contextlib.ExitStack() as es_:
              def lt(name, shape, dt=F32):
                  return TD(es_.enter_context(nc.sbuf_tensor(nm(name), list(shape), dt)), name)
              hT = lt("c_hT", [128, 8, 512], BF16); sq = lt("c_sq", [128, D]); xn = lt("c_xn", [128, D], BF16)
              xn2 = lt("c_xn2", [128, D], BF16); jk0 = lt("c_jk0", [128, D], BF16); jk1 = lt("c_jk1", [128, D], BF16)
              actT = lt("c_actT", [128, NCH, 512], BF16)
              wd = lt("c_wd", [128, NCH, D], BF16)
              XK = [lt(f"c_xk{i}", [128, D]) for i in range(4)]
              yg = [lt(f"c_yg{i}", [128, 512]) for i in range(2)]
              yu = [lt(f"c_yu{i}", [128, 512]) for i in range(2)]
              yb = lt("c_y", [128, D])
              hTh = lt("c_hTh", [128, 8, 2], BF16); xh = yb
              blocks = []
              for g in range(4):
                  blocks += [wup_loader(l, j) for j in range(11)]
                  if g == 0:
                      blocks += [wup_loader(l, j) for j in range(3)]
              ws.schedule(blocks)
              ws.prefetch()
              for hf in range(2):
                  cx.dma("pool", wd.t[:, hf * 11:(hf + 1) * 11, :],
                         WDN.ap()[l, hf * 11 * 128:(hf + 1) * 11 * 128, :].rearrange("(c p) n -> p c n", p=128), writes=[wd.d])

              UC_RING = [S0, S1, O0, O1, ML, MM0, MM1]
              uc_i = [0]
              mm_ring[0] = RING2

              def edge_src(bk, ncols):
                  a_ = PB[bk][:, 0:2]
                  return bass.AP(a_.tensor, a_.offset, [list(a_.ap[0]), [ncols - 2, 2], [1, 2]])

              HS = 3

              def halo_block(w, j, hsrc, uh):
                  for pr in range(2):
                      for typ in range(2):
                          cidx = 2 * j + pr + typ * NCH
                          uc_i[0] += 1
                          bk = UC_RING[uc_i[0] % len(UC_RING)]
                          cx.op("pe", acc_group(PB[bk][:, 0:2], [(w.t[:, k, typ * 256 + pr * 128:typ * 256 + (pr + 1) * 128],
                                                                  hsrc.t[:, k, 0:2]) for k in range(8)]),
                                reads=[w.d, hsrc.d], writes=[PD[bk]])
                          cx.op("act", lambda e, bk=bk, cidx=cidx: e.activation(uh.t[:, cidx, :], PB[bk][:, 0:2], AF.Copy),
                                writes=[PD[bk], uh.d])

              def up_main(streams, halo=None):
                  for j in range(11):
                      w = ws.get()
                      if halo is not None and j == HS:
                          halo_prep()
                      if halo is not None and j >= HS:
                          halo_block(w, j, halo[0], halo[1])
                      for (hsrc, ncols, edge, act_dst) in streams:
                          for pr in range(2):
                              ch = 2 * j + pr
                              for typ in range(2):
                                  cidx = ch + typ * NCH
                                  uc_i[0] += 1
                                  bk = UC_RING[uc_i[0] % len(UC_RING)]
                                  cx.op("pe", acc_group(PB[bk][:, 0:ncols], [(w.t[:, k, typ * 256 + pr * 128:typ * 256 + (pr + 1) * 128],
                                                                              hsrc.t[:, k, 0:ncols]) for k in range(8)]),
                                        reads=[w.d, hsrc.d], writes=[PD[bk]])
                                  y = (yg if typ == 0 else yu)[pr]
                                  cx.op("act", lambda e, bk=bk, cidx=cidx, y=y: e.activation(y.t[:, 0:ncols], PB[bk][:, 0:ncols], AF.Identity,
                                                                                            scale=wcT.t[:, 2, cidx:cidx + 1], bias=bcT.t[:, cidx:cidx + 1]),
                                        reads=[wcT.d, bcT.d], writes=[PD[bk], y.d])
                                  cx.op("act", lambda e, bk=bk, cidx=cidx: e.activation(edge.t[:, cidx, :].rearrange("p (a b) -> p a b", a=2),
                                                                                       edge_src(bk, ncols), AF.Copy),
                                        writes=[PD[bk], edge.d])
                                  cx.op("dve", lambda e, bk=bk, cidx=cidx, y=y: e.scalar_tensor_tensor(
                                      y.t[:, 1:ncols], PB[bk][:, 0:ncols - 1], wcT.t[:, 1, cidx:cidx + 1], y.t[:, 1:ncols], ALU.mult, ALU.add),
                                      reads=[wcT.d], writes=[PD[bk], y.d])
                                  cx.op("dve", lambda e, bk=bk, cidx=cidx, y=y: e.scalar_tensor_tensor(
                                      y.t[:, 2:ncols], PB[bk][:, 0:ncols - 2], wcT.t[:, 0, cidx:cidx + 1], y.t[:, 2:ncols], ALU.mult, ALU.add),
                                      reads=[wcT.d], writes=[PD[bk], y.d])
                              cx.op("act", lambda e, pr=pr: e.activation(yg[pr].t[:, 0:ncols], yg[pr].t[:, 0:ncols], AF.Gelu_apprx_tanh),
                                    writes=[yg[pr].d])
                              cx.op("dve", lambda e, pr=pr, ch=ch: e.tensor_tensor(act_dst.t[:, ch, 0:ncols], yg[pr].t[:, 0:ncols],
                                                                                  yu[pr].t[:, 0:ncols], ALU.mult),
                                    reads=[yg[pr].d, yu[pr].d], writes=[act_dst.d])

              def halo_pass(hsrc, uh):
                  for j in range(HS):
                      w = ws.get()
                      halo_block(w, j, hsrc, uh)

              fy = lt("c_fy", [128, 2, 2 * NCH]); ft = lt("c_ft", [128, 2 * NCH])

              def fix_boundary(edge, halo, halo_dep, act_dst):
                  u0, u1 = edge.t[:, :, 0], edge.t[:, :, 1]
                  h0, h1 = halo[:, :, 0], halo[:, :, 1]
                  W0, W1, W2 = wcT.t[:, 0, :], wcT.t[:, 1, :], wcT.t[:, 2, :]
                  rd = [edge.d, halo_dep, wcT.d, bcT.d]
                  for col, (ua, ta, tb) in enumerate(((u0, h1, h0), (u1, u0, h1))):
                      yv = fy.t[:, col, :]
                      cx.op("dve", lambda e, yv=yv, ua=ua: e.tensor_tensor(yv, ua, W2, ALU.mult), reads=rd, writes=[fy.d])
                      cx.op("dve", lambda e, yv=yv: e.tensor_tensor(yv, yv, bcT.t[:, :], ALU.add), reads=rd, writes=[fy.d])
                      cx.op("dve", lambda e, ta=ta: e.tensor_tensor(ft.t[:, :], ta, W1, ALU.mult), reads=rd, writes=[ft.d])
                      cx.op("dve", lambda e, yv=yv: e.tensor_tensor(yv, yv, ft.t[:, :], ALU.add), reads=[ft.d], writes=[fy.d])
                      cx.op("dve", lambda e, tb=tb: e.tensor_tensor(ft.t[:, :], tb, W0, ALU.mult), reads=rd, writes=[ft.d])
                      cx.op("dve", lambda e, yv=yv: e.tensor_tensor(yv, yv, ft.t[:, :], ALU.add), reads=[ft.d], writes=[fy.d])
                  cx.op("act", lambda e: e.activation(fy.t[:, :, 0:NCH], fy.t[:, :, 0:NCH], AF.Gelu_apprx_tanh), writes=[fy.d])
                  cx.op("dve", lambda e: e.tensor_tensor(act_dst.t[:, :, 0:2].rearrange("p c t -> p t c"), fy.t[:, :, 0:NCH], fy.t[:, :, NCH:2 * NCH],
                                                         ALU.mult), reads=[fy.d], writes=[act_dst.d])

              def down_res(nt, act_src, c0, xk):
                  for half in range(2):
                      b = next_mm()
                      cx.op("pe", acc_group(PB[b][:nt, :], [(act_src.t[:, c, c0:c0 + nt], wd.t[:, c, half * 512:(half + 1) * 512])
                                                            for c in range(NCH)]),
                            reads=[act_src.d, wd.d], writes=[PD[b]])
                      evac(yb.t[:nt, half * 512:(half + 1) * 512], PB[b][:nt, :], b, yb.d)
                  st = small()
                  cx.op("act", lambda e: e.activation(sq.t[:nt, :], yb.t[:nt, :], AF.Square, accum_out=st.t[:nt, 0:1]),
                        reads=[yb.d], writes=[sq.d, st.d])
                  rstd_from_ss(st, nt, 0, 1, 1.0 / D)
                  cx.op("dve", lambda e: e.scalar_tensor_tensor(yb.t[:nt, :], yb.t[:nt, :], st.t[:nt, 1:2], gbc.t[:nt, 3, :], ALU.mult, ALU.mult),
                        reads=[st.d, gbc.d], writes=[yb.d])
                  cx.op("dve", lambda e: e.tensor_tensor(xk.t[:nt, :], xk.t[:nt, :], yb.t[:nt, :], ALU.add),
                        reads=[yb.d], writes=[xk.d])

              EDGE = [lt(f"c_edge{i}", [128, 2 * NCH, 4]) for i in range(2)]
              edge_s = lt("c_edge_s", [128, 2 * NCH, 4])
              if do_sample:
                  XKs = lt("c_xks", [128, D]); hTs = lt("c_hTs", [128, 8, 16], BF16); actTs = lt("c_actTs", [128, NCH, 16], BF16)
                  with nc.allow_non_contiguous_dma(reason="conv cache rows"):
                      for r_ in range(2):
                          cx.dma("sp", uh_s.t[:, :, r_], CCV.ap()[l, r_].rearrange("(c p) -> p c", p=128), writes=[uh_s.d])

              def halo_prep():
                  for r in range(4):
                      cx.dma("sp", sq.t[0:2, :], EXX_D.ap()[2 * r:2 * r + 2, :], reads=[d_exx[1]], writes=[sq.d], sem=cx.shared_sem("xst"))
                      if r == 0:
                          cx.op("dve", lambda e: e.tensor_scalar(xh.t[0:2, :], sq.t[0:2, :], cm.t[0:2, 0:1], None, ALU.mult),
                                reads=[sq.d, cm.d], writes=[xh.d])
                      else:
                          cx.op("dve", lambda e, r=r: e.scalar_tensor_tensor(xh.t[0:2, :], sq.t[0:2, :], cm.t[0:2, r:r + 1], xh.t[0:2, :],
                                                                            ALU.mult, ALU.add), reads=[sq.d, cm.d], writes=[xh.d])
                  norm_T(xh, 2, 2, hTh, 0, sq, xn)

              for g in range(4):
                  smp = do_sample and g == 3
                  for i in range(4):
                      r0 = (g * 4 + i) * 128
                      cx.dma("sp", XK[i].t[:, :], XM.ap()[r0:r0 + 128, :], reads=[d_xm], writes=[XK[i].d])
                  norm_seq([(XK[i], 128, i * 128) for i in range(4)], 2, hT, [xn, xn2], [jk0, jk1])
                  edge = EDGE[g % 2]
                  streams = [(hT, 512, edge, actT)]
                  if smp:
                      cx.dma("sp", XKs.t[0:16, :], XSM.ap()[:, :], reads=[d_xsm], writes=[XKs.d])
                      norm_T(XKs, 16, 2, hTs, 0, sq, xn)
                      streams.append((hTs, 16, edge_s, actTs))
                  up_main(streams, halo=(hTh, uh_p) if g == 0 else None)
                  if g == 0:
                      halo_pass(hTh, uh_p)
                      fix_boundary(edge, uh_p.t[:, :, :], uh_p.d, actT)
                  else:
                      fix_boundary(edge, EDGE[(g - 1) % 2].t[:, :, 2:4], EDGE[(g - 1) % 2].d, actT)
                  if smp:
                      fix_boundary(edge_s, uh_s.t[:, :, :], uh_s.d, actTs)
                  for i in range(4):
                      tg = g * 4 + i
                      down_res(128, actT, i * 128, XK[i])
                      cx.dma("sp", xout.ap()[tg * 128:(tg + 1) * 128, :], XK[i].t[:, :], reads=[XK[i].d],
                             writes=[d_x1 if l == 0 else d_out], sem=sem_out)
                  if smp:
                      down_res(16, actTs, 0, XKs)
                      cx.dma("sp", xsout.ap()[:, :], XKs.t[0:16, :], reads=[XKs.d], writes=[d_xs1 if l == 0 else d_out], sem=sem_out)
                      with nc.allow_non_contiguous_dma(reason="conv state rows"):
                          for r_ in range(2):
                              cx.dma("sp", CVS.ap()[l, r_].rearrange("(c p) -> p c", p=128), edge_s.t[:, :, 2 + r_], reads=[edge_s.d],
                                     writes=[d_out], sem=sem_out)
              with nc.allow_non_contiguous_dma(reason="conv state rows (2 x 5632 elements)"):
                  for r_ in range(2):
                      cx.dma("sp", CVP.ap()[l, r_].rearrange("(c p) -> p c", p=128), EDGE[1].t[:, :, 2 + r_], reads=[EDGE[1].d],
                             writes=[d_out], sem=sem_out)
              cx.barrier()

    except _Stop:
        pass
    cx.finish()
    build.stats = (cx.n_inst, cx.n_wait, len(cx.sems))
    return nc


_NC_CACHE = {}


def kernel(x_prompt, x_sample, cache_k_att, cache_v_att, state_mlstm_c, state_mlstm_n, state_mlstm_m,
           cache_ffn_conv, norm_g, w_in, b_i, b_f, rel_table, g_att, g_mlstm, w_out, w_up, w_conv,
           b_conv, w_down):
    f = lambda a: np.ascontiguousarray(np.asarray(a), dtype=np.float32)
    x_prompt, x_sample = f(x_prompt), f(x_sample)
    shared = {
        "w_in": f(w_in), "w_out": f(w_out), "w_up": f(w_up), "w_down": f(w_down), "norm_g": f(norm_g),
        "g_att": f(g_att), "g_mlstm": f(g_mlstm), "b_i": f(b_i), "b_f": f(b_f), "rel_table": f(rel_table),
        "w_conv": f(w_conv), "b_conv": f(b_conv),
        "ident": np.eye(128, dtype=np.float32), "triu": np.triu(np.ones((128, 128), np.float32)),
        "ones": np.ones((128, 128), np.float32),
    }
    ck, cv = f(cache_k_att), f(cache_v_att)
    sc, sn, sm, ccv = f(state_mlstm_c), f(state_mlstm_n), f(state_mlstm_m), f(cache_ffn_conv)
    in_maps = []
    for c in range(8):
        b, j = c // 4, c % 4
        cmask = np.zeros(12, np.float32)
        for r in range(4):
            cmask[r] = 1.0 if r == j - 1 else 0.0
            cmask[4 + r] = 1.0 if r < j else 0.0
            cmask[8 + r] = 1.0 if r <= j else 0.0
        m = dict(shared)
        m.update({
            "xp": np.ascontiguousarray(x_prompt[b, j * TOK:(j + 1) * TOK]), "xs": np.ascontiguousarray(x_sample[c]),
            "ck": np.ascontiguousarray(ck[:, c]), "cv": np.ascontiguousarray(cv[:, c]),
            "sc": np.ascontiguousarray(sc[:, c]), "sn": np.ascontiguousarray(sn[:, c]),
            "sm": np.ascontiguousarray(sm[:, c]), "cconv": np.ascontiguousarray(ccv[:, c]),
            "cmask": cmask,
        })
        in_maps.append(m)
    if "nc" not in _NC_CACHE:
        _NC_CACHE["nc"] = build()
    res = run_bass_kernel_spmd(_NC_CACHE["nc"], in_maps, core_ids=list(range(8)))
    R = res.results
    yp = np.stack([np.concatenate([R[b * 4 + j]["yp"] for j in range(4)], 0) for b in range(2)], 0)
    ys = np.stack([R[c]["ys"] for c in range(8)], 0)
    last = [3, 7]
    pick = lambda k: np.stack([R[c][k] for c in last], 1)
    allc = lambda k: np.stack([R[c][k] for c in range(8)], 1)
    outs = (yp, ys, pick("kp"), pick("vp"), pick("cp"), pick("np"), pick("mp"), pick("cvp"),
            allc("ks"), allc("vs"), allc("cs"), allc("ns"), allc("ms"), allc("cvs"))
    return tuple(np.ascontiguousarray(o, dtype=np.float32) for o in outs)
```

```python
import contextlib
import math
import numpy as np
import concourse.bass as bass
import concourse.mybir as mybir
from concourse.bass_utils import run_bass_kernel_spmd

F32 = mybir.dt.float32
BF16 = mybir.dt.bfloat16
AF = mybir.ActivationFunctionType
ALU = mybir.AluOpType
AX = mybir.AxisListType

D = 1024
TOK = 2048
NTILE = 16
DIN = 3592
DFF = 2816
NCH = 22
EPS = 1e-6
C_QA, C_KA, C_VA, C_QB, C_KB, C_VB, C_OB, C_G = 0, 512, 1024, 1536, 2048, 2560, 3072, 3584
LNK = -0.5 * math.log(128.0)
NEG = -1.0e30
DSZ = 128 * 769 + 768
XROWS = 129
XCOLS = 516


class Dep:
    __slots__ = ("name", "w", "r", "dsem")

    def __init__(self, name):
        self.name = name
        self.w = None
        self.r = {}
        self.dsem = None


class Sem:
    __slots__ = ("key", "h", "count", "is_dma")

    def __init__(self, key, h, is_dma):
        self.key = key
        self.h = h
        self.count = 0
        self.is_dma = is_dma


class Eng:
    __slots__ = ("name", "eng", "sem", "seen", "same_sync")

    def __init__(self, name, eng, sem, same_sync):
        self.name = name
        self.eng = eng
        self.sem = sem
        self.seen = {}
        self.same_sync = same_sync


class Ctx:
    def __init__(self, nc, same_sync=True):
        self.nc = nc
        self.sems = {}
        self.engs = {}
        for name, eng, ss in (("pe", nc.tensor, False), ("act", nc.scalar, same_sync),
                              ("dve", nc.vector, same_sync), ("pool", nc.gpsimd, same_sync),
                              ("sp", nc.sync, False)):
            s = self.new_sem("e_" + name, False)
            self.engs[name] = Eng(name, eng, s, ss)
        self.n_wait = 0
        self.n_inst = 0
        self.shared = {}

    def new_sem(self, key, is_dma=True):
        h = self.nc.alloc_semaphore(key)
        s = Sem(key, h, is_dma)
        self.sems[key] = s
        return s

    def shared_sem(self, key):
        if key not in self.shared:
            self.shared[key] = self.new_sem("sh_" + key)
        return self.shared[key]

    def _waits(self, es, reads, writes):
        need = {}
        for d in reads:
            if d.w is not None:
                k, v = d.w
                if need.get(k, 0) < v:
                    need[k] = v
        for d in writes:
            if d.w is not None:
                k, v = d.w
                if need.get(k, 0) < v:
                    need[k] = v
            for k, v in d.r.items():
                if need.get(k, 0) < v:
                    need[k] = v
        for k, v in need.items():
            if k == es.sem.key and not es.same_sync:
                continue
            s = self.sems[k]
            if s.is_dma:
                v = s.count
            if es.seen.get(k, 0) < v:
                es.eng.wait_ge(s.h, v)
                es.seen[k] = v
                self.n_wait += 1

    def op(self, E, fns, reads=(), writes=()):
        es = self.engs[E]
        self._waits(es, reads, writes)
        if not isinstance(fns, (list, tuple)):
            fns = [fns]
        inst = None
        for f in fns:
            inst = f(es.eng)
            self.n_inst += 1
        es.sem.count += 1
        inst.then_inc(es.sem.h, 1)
        k, c = es.sem.key, es.sem.count
        for d in reads:
            d.r[k] = c
        for d in writes:
            d.w = (k, c)
            d.r = {}
        return inst

    def _sem_for(self, reads, writes, sem):
        if sem is not None:
            return sem
        d0 = writes[0] if writes else reads[0]
        if d0.dsem is None:
            key = "d_" + d0.name
            if key not in self.sems:
                self.new_sem(key)
            d0.dsem = key
        return self.sems[d0.dsem]

    def dma(self, Q, out, in_, reads=(), writes=(), sem=None, **kw):
        es = self.engs[Q]
        self._waits(es, reads, writes)
        sem = self._sem_for(reads, writes, sem)
        inst = es.eng.dma_start(out=out, in_=in_, **kw)
        self.n_inst += 1
        sem.count += 16
        inst.then_inc(sem.h, 16)
        for d in reads:
            d.r[sem.key] = sem.count
        for d in writes:
            d.w = (sem.key, sem.count)
            d.r = {}
        return inst

    def allgather(self, src_ap, dst_ap, reads, writes, sem):
        es = self.engs["pool"]
        self._waits(es, reads, writes)
        inst = es.eng.collective_compute("AllGather", ALU.bypass, replica_groups=[[0, 1, 2, 3], [4, 5, 6, 7]],
                                         ins=[src_ap], outs=[dst_ap])
        self.n_inst += 1
        sem.count += 1
        inst.then_inc(sem.h, 1)
        for d in reads:
            d.r[sem.key] = sem.count
        for d in writes:
            d.w = (sem.key, sem.count)
            d.r = {}

    def barrier(self, exclude=()):
        for es in self.engs.values():
            for k, s in self.sems.items():
                if k in exclude:
                    continue
                if s.count > 0 and k != es.sem.key and es.seen.get(k, 0) < s.count:
                    es.eng.wait_ge(s.h, s.count)
                    es.seen[k] = s.count
                    self.n_wait += 1

    def finish(self, E="sp"):
        es = self.engs[E]
        for k, s in self.sems.items():
            if s.count > 0 and k != es.sem.key and es.seen.get(k, 0) < s.count:
                es.eng.wait_ge(s.h, s.count)
                es.seen[k] = s.count


class TD:
    __slots__ = ("t", "d")

    def __init__(self, t, name):
        self.t = t
        self.d = Dep(name)


class _Stop(Exception):
    pass


def build(do_sample=True):
    import os
    kstop = int(os.environ.get("KSTOP", "99"))

    def stop_at(n):
        if kstop == n:
            raise _Stop()
    nc = bass.Bass("TRN2", target_bir_lowering=False)
    cx = Ctx(nc)
    uid = [0]

    def nm(s):
        uid[0] += 1
        return f"{s}_{uid[0]}"

    def din(name, shape):
        return nc.dram_tensor(name, list(shape), F32, kind="ExternalInput")

    def dout(name, shape):
        return nc.dram_tensor(name, list(shape), F32, kind="ExternalOutput")

    def dscr(name, shape, dt=F32):
        return nc.dram_tensor(name, list(shape), dt)

    XP = din("xp", [TOK, D]); XS = din("xs", [16, D])
    WIN = din("w_in", [2, D, DIN]); WOUT = din("w_out", [2, D, D])
    WUP = din("w_up", [2, D, 2 * DFF]); WDN = din("w_down", [2, DFF, D])
    NG = din("norm_g", [2, 4, D]); GATT = din("g_att", [2, 512]); GML = din("g_mlstm", [2, 512])
    BI = din("b_i", [2, 4]); BFG = din("b_f", [2, 4]); REL = din("rel_table", [2, 8, 513])
    WCV = din("w_conv", [2, 3, 2 * DFF]); BCV = din("b_conv", [2, 2 * DFF])
    CK = din("ck", [2, 8, 512, 64]); CV = din("cv", [2, 8, 512, 64])
    SC = din("sc", [2, 4, 128, 128]); SN = din("sn", [2, 4, 128]); SM = din("sm", [2, 4])
    CCV = din("cconv", [2, 2, 2 * DFF])
    IDN = din("ident", [128, 128]); TRIU = din("triu", [128, 128]); ONES = din("ones", [128, 128])
    CMASK = din("cmask", [12])

    YP = dout("yp", [TOK, D]); YS = dout("ys", [16, D])
    KP = dout("kp", [2, 8, 512, 64]); VP = dout("vp", [2, 8, 512, 64])
    CP = dout("cp", [2, 4, 128, 128]); NP = dout("np", [2, 4, 128]); MP = dout("mp", [2, 4])
    CVP = dout("cvp", [2, 2, 2 * DFF])
    KS = dout("ks", [2, 8, 16, 64]); VS = dout("vs", [2, 8, 16, 64])
    CS = dout("cs", [2, 4, 128, 128]); NS = dout("ns", [2, 4, 128]); MS = dout("ms", [2, 4])
    CVS = dout("cvs", [2, 2, 2 * DFF])

    XM = dscr("xm", [TOK, D]); X1 = dscr("x1", [TOK, D])
    XSM = dscr("xsm", [16, D]); XS1 = dscr("xs1", [16, D])
    DTO = dscr("dtoep", [2, 8, DSZ])
    EXK_S = dscr("exk_s", [512, 512], BF16); EXK_D = dscr("exk_d", [4 * 512, 512], BF16)
    EXV_S = dscr("exv_s", [512, 512], BF16); EXV_D = dscr("exv_d", [4 * 512, 512], BF16)
    EXS_S = dscr("exs_s", [XROWS, XCOLS]); EXS_D = dscr("exs_d", [4 * XROWS, XCOLS])
    EXX_S = dscr("exx_s", [2, D]); EXX_D = dscr("exx_d", [8, D])
    d_xm = Dep("xm"); d_x1 = Dep("x1"); d_xsm = Dep("xsm"); d_xs1 = Dep("xs1"); d_dto = Dep("dto")
    d_exk = [Dep("exk_s"), Dep("exk_d")]; d_exv = [Dep("exv_s"), Dep("exv_d")]
    d_exs = [Dep("exs_s"), Dep("exs_d")]; d_exx = [Dep("exx_s"), Dep("exx_d")]
    d_out = Dep("outs")
    sem_out = cx.shared_sem("out")
    sem_cc = cx.new_sem("cc", True)

    PB = [nc.alloc_psum_tensor(f"pb{i}", [128, 512], F32) for i in range(8)]
    PD = [Dep(f"pb{i}") for i in range(8)]
    MM0, MM1, TRB, S0, S1, O0, O1, ML = range(8)

    def sb(name, shape, dt=F32):
        return TD(nc.alloc_sbuf_tensor(nm(name), list(shape), dt), name)

    ident_f = sb("ident_f", [128, 128]); triu_f = sb("triu_f", [128, 128]); ones_f = sb("ones_f", [128, 128])
    ident_b = sb("ident_b", [128, 128], BF16); masku_b = sb("masku_b", [128, 128], BF16)
    cm = sb("cm", [128, 12])
    EALL1 = sb("eall", [128, 8, 5, 128], BF16)
    EALL = [EALL1, EALL1]
    gbc = sb("gbc", [128, 4, D]); gatt = sb("gatt", [128, 512]); gml = sb("gml", [128, 512])
    bif = sb("bif", [128, 8]); wcT = sb("wcT", [128, 3, 2 * NCH]); bcT = sb("bcT", [128, 2 * NCH])
    WB = [sb(f"wb{i}", [128, 8, 512], BF16) for i in range(3)]
    wg = sb("wg", [128, 8, 8], BF16)
    C32 = sb("c32", [128, 4, 129]); CBF = sb("cbf", [128, 4, 129], BF16)
    ngrun = sb("ngrun", [128, 4]); amaxr = sb("amaxr", [128, 4])
    uh_p = sb("uh_p", [128, 2 * NCH, 2]); uh_s = sb("uh_s", [128, 2 * NCH, 2])
    ME = [sb(f"mE{i}", [128, 8]) for i in range(2)]
    SMALL = [sb(f"sm{i}", [128, 64]) for i in range(6)]
    sm_i = [0]

    def small():
        sm_i[0] = (sm_i[0] + 1) % len(SMALL)
        return SMALL[sm_i[0]]

    xt_i = [0]


    class WS:
        def __init__(self):
            self.pending = []
            self.inflight = []
            self.k = 0

        def schedule(self, loaders):
            self.pending.extend(loaders)

        def prefetch(self):
            while len(self.inflight) < 3 and self.pending:
                slot = WB[self.k % 3]
                self.k += 1
                self.pending.pop(0)(slot)
                self.inflight.append(slot)

        def get(self):
            while len(self.inflight) < 3 and self.pending:
                slot = WB[self.k % 3]
                self.k += 1
                self.pending.pop(0)(slot)
                self.inflight.append(slot)
            return self.inflight.pop(0)

    ws = WS()

    def win_loader(l, c0, n=512):
        def f(slot):
            cx.dma("pool", slot.t[:, :, 0:n], WIN.ap()[l, :, c0:c0 + n].rearrange("(k p) n -> p k n", p=128),
                   writes=[slot.d])
        return f

    def wup_loader(l, j):
        def f(slot):
            cx.dma("pool", slot.t[:, :, 0:256], WUP.ap()[l, :, j * 256:(j + 1) * 256].rearrange("(k p) n -> p k n", p=128),
                   writes=[slot.d])
            cx.dma("pool", slot.t[:, :, 256:512],
                   WUP.ap()[l, :, DFF + j * 256:DFF + (j + 1) * 256].rearrange("(k p) n -> p k n", p=128),
                   writes=[slot.d])
        return f

    def mm(out, lhsT, rhs, start=True, stop=True):
        return lambda e: e.matmul(out, lhsT, rhs, start=start, stop=stop)

    def acc_group(out, pairs):
        n = len(pairs)
        return [mm(out, a, b, start=(i == 0), stop=(i == n - 1)) for i, (a, b) in enumerate(pairs)]

    def rstd_from_ss(st, nt, c_in, c_out, inv_n, width=1):
        cx.op("act", lambda e: e.activation(st.t[:nt, c_out:c_out + width], st.t[:nt, c_in:c_in + width], AF.Ln,
                                            scale=inv_n, bias=eps_t.t[:nt, 0:1]), reads=[st.d, eps_t.d], writes=[st.d])
        cx.op("act", lambda e: e.activation(st.t[:nt, c_out:c_out + width], st.t[:nt, c_out:c_out + width], AF.Exp,
                                            scale=-0.5), reads=[st.d], writes=[st.d])

    eps_t = sb("eps_t", [128, 4])

    cx.dma("sp", ident_f.t[:], IDN.ap()[:, :], writes=[ident_f.d])
    cx.dma("sp", triu_f.t[:], TRIU.ap()[:, :], writes=[triu_f.d])
    cx.dma("sp", ones_f.t[:], ONES.ap()[:, :], writes=[ones_f.d])
    cx.dma("sp", cm.t[:], CMASK.ap().partition_broadcast(128), writes=[cm.d])
    cx.op("dve", lambda e: e.tensor_copy(ident_b.t[:], ident_f.t[:]), reads=[ident_f.d], writes=[ident_b.d])
    cx.op("dve", lambda e: e.tensor_copy(masku_b.t[:], triu_f.t[:]), reads=[triu_f.d], writes=[masku_b.d])
    cx.op("dve", lambda e: e.memset(eps_t.t[:, 0:1], EPS), writes=[eps_t.d])
    cx.op("dve", lambda e: e.memset(eps_t.t[:, 1:2], LNK), writes=[eps_t.d])
    cx.op("dve", lambda e: e.memset(eps_t.t[:, 2:3], 1.0), writes=[eps_t.d])
    cx.op("dve", lambda e: e.memset(eps_t.t[:, 3:4], 0.0), writes=[eps_t.d])

    def build_E(l, es_):
        R = TD(es_.enter_context(nc.sbuf_tensor(nm("Rtoep"), [128, 8, 768], F32)), "Rtoep")
        cl = TD(es_.enter_context(nc.sbuf_tensor(nm("cl"), [128, 8, 1], F32)), "cl")
        EP = TD(es_.enter_context(nc.sbuf_tensor(nm("epre"), [128, 8, 5, 128], F32)), "epre")
        cx.dma("sp", R.t[:, :, 0:384], bass.AP(REL, l * 8 * 513 + 129, [[0, 128], [513, 8], [1, 384]]), writes=[R.d])
        cx.dma("sp", cl.t[:], bass.AP(REL, l * 8 * 513 + 512, [[0, 128], [513, 8], [1, 1]]), writes=[cl.d],
               allow_slow_non_contiguous=True)
        cx.op("dve", lambda e: e.tensor_copy(R.t[:, :, 384:768], cl.t[:].to_broadcast([128, 8, 384])),
              reads=[cl.d], writes=[R.d])
        cx.dma("sp", bass.AP(DTO, l * 8 * DSZ, [[769, 128], [DSZ, 8], [1, 768]]), R.t[:], reads=[R.d], writes=[d_dto])
        for j in range(5):
            cx.dma("sp", EP.t[:, :, j, :], bass.AP(DTO, l * 8 * DSZ + 127 + 128 * (4 - j), [[768, 128], [DSZ, 8], [1, 128]]),
                   reads=[d_dto], writes=[EP.d])
        cx.op("act", lambda e: e.activation(EALL1.t[:].rearrange("p h j q -> p (h j q)"),
                                            EP.t[:].rearrange("p h j q -> p (h j q)"), AF.Exp),
              reads=[EP.d], writes=[EALL1.d])
        cx.op("dve", lambda e: e.memset(EALL1.t[0:64, :, 0, 64:128], 0.0), writes=[EALL1.d])
        cx.op("dve", lambda e: e.memset(EALL1.t[64:128, :, 4, 0:64], 0.0), writes=[EALL1.d])

    def load_x(XT, src_ap, nt):
        xt_i[0] ^= 1
        xt = XT[xt_i[0]]
        cx.dma("sp", xt.t[:nt, :], src_ap, writes=[xt.d])
        return xt

    def norm_T(xt, nt, gidx, hT, col0, sq, xn):
        st = small()
        cx.op("act", lambda e: e.activation(sq.t[:nt, :], xt.t[:nt, :], AF.Square, accum_out=st.t[:nt, 0:1]),
              reads=[xt.d], writes=[sq.d, st.d])
        rstd_from_ss(st, nt, 0, 1, 1.0 / D)
        cx.op("dve", lambda e: e.scalar_tensor_tensor(xn.t[:nt, :], xt.t[:nt, :], st.t[:nt, 1:2], gbc.t[:nt, gidx, :],
                                                      ALU.mult, ALU.mult), reads=[xt.d, st.d, gbc.d], writes=[xn.d])
        transpose_to(xn, nt, 8, hT, col0)

    def transpose_to(src, nt, nk, dst, col0, eng="act"):
        trv = PB[TRB][:].bitcast(BF16).rearrange("p (k c) -> p k c", k=8)
        cx.op("pe", [(lambda e, k=k: e.transpose(trv[:, k, :nt], src.t[:nt, k * 128:(k + 1) * 128], ident_b.t[:nt, :nt]))
                     for k in range(nk)], reads=[src.d, ident_b.d], writes=[PD[TRB]])
        if eng == "act":
            cx.op("act", lambda e: e.activation(dst.t[:, 0:nk, col0:col0 + nt], trv[:, 0:nk, :nt], AF.Copy),
                  writes=[PD[TRB], dst.d])
        else:
            cx.op("dve", lambda e: e.tensor_copy(dst.t[:, 0:nk, col0:col0 + nt], trv[:, 0:nk, :nt]),
                  writes=[PD[TRB], dst.d])

    def norm_seq(items, gidx, hT, xns, junks):
        trv = PB[TRB][:].bitcast(BF16).rearrange("p (k c) -> p k c", k=8)
        prev = None
        for idx, (xt, nt, col0) in enumerate(items):
            xn = xns[idx % len(xns)]
            junk = junks[idx % len(junks)]
            st = small()
            cx.op("act", lambda e: e.activation(junk.t[:nt, :], xt.t[:nt, :], AF.Square, accum_out=st.t[:nt, 0:1]),
                  reads=[xt.d], writes=[junk.d, st.d])
            rstd_from_ss(st, nt, 0, 1, 1.0 / D)
            cx.op("dve", lambda e: e.scalar_tensor_tensor(xn.t[:nt, :], xt.t[:nt, :], st.t[:nt, 1:2], gbc.t[:nt, gidx, :],
                                                          ALU.mult, ALU.mult), reads=[xt.d, st.d, gbc.d], writes=[xn.d])
            if prev is not None:
                pnt, pcol = prev
                cx.op("dve", lambda e: e.tensor_copy(hT.t[:, 0:8, pcol:pcol + pnt], trv[:, 0:8, :pnt]), writes=[PD[TRB], hT.d])
            cx.op("pe", [(lambda e, k=k: e.transpose(trv[:, k, :nt], xn.t[:nt, k * 128:(k + 1) * 128], ident_b.t[:nt, :nt]))
                         for k in range(8)], reads=[xn.d, ident_b.d], writes=[PD[TRB]])
            prev = (nt, col0)
        pnt, pcol = prev
        cx.op("dve", lambda e: e.tensor_copy(hT.t[:, 0:8, pcol:pcol + pnt], trv[:, 0:8, :pnt]), writes=[PD[TRB], hT.d])

    mm_i = [0]
    mm_ring = [[MM0, MM1]]
    RING2 = [MM0, MM1]
    RING6 = [MM0, MM1, S0, S1, O0, O1]

    def next_mm():
        mm_i[0] += 1
        r = mm_ring[0]
        return r[mm_i[0] % len(r)]

    s_i = [0]

    def next_s():
        s_i[0] ^= 1
        return S0 if s_i[0] else S1

    ev_i = [0]

    def evac(out_ap, in_ap, reads_bank, out_dep, extra_reads=()):
        ev_i[0] ^= 1
        if ev_i[0]:
            cx.op("act", lambda e: e.activation(out_ap, in_ap, AF.Copy), reads=list(extra_reads),
                  writes=[PD[reads_bank], out_dep])
        else:
            cx.op("dve", lambda e: e.tensor_copy(out_ap, in_ap), reads=list(extra_reads), writes=[PD[reads_bank], out_dep])

    def fm_proj(wslot, cchunks, hT, ncols, dst, dst_k0, dst_col0):
        for i, cc in enumerate(cchunks):
            b = next_mm()
            cx.op("pe", acc_group(PB[b][:, 0:ncols], [(wslot.t[:, k, cc * 128:(cc + 1) * 128], hT.t[:, k, 0:ncols])
                                                      for k in range(8)]),
                  reads=[wslot.d, hT.d], writes=[PD[b]])
            evac(dst.t[:, dst_k0 + i, dst_col0:dst_col0 + ncols], PB[b][:, 0:ncols], b, dst.d)

    def tm_proj(w_ap_fn, w_dep, hT, col0, nt, ncols):
        b = next_mm()
        cx.op("pe", acc_group(PB[b][:nt, 0:ncols], [(hT.t[:, k, col0:col0 + nt], w_ap_fn(k)) for k in range(8)]),
              reads=[w_dep, hT.d], writes=[PD[b]])
        return b

    def gate_math(hT, tiles, gs):
        ntl = len(tiles)
        gp = PB[ML][:, 0:8 * ntl].rearrange("p (t c) -> p t c", c=8)
        for ti, (col0, nt) in enumerate(tiles):
            cx.op("pe", acc_group(PB[ML][:nt, 8 * ti:8 * ti + 8], [(hT.t[:, k, col0:col0 + nt], wg.t[:, k, :]) for k in range(8)]),
                  reads=[wg.d, hT.d], writes=[PD[ML]])
        nt = tiles[0][1]
        ig, sp = gs["ig"], gs["sp"]
        cx.op("dve", lambda e: e.tensor_tensor(ig.t[:nt, 0:ntl, :], gp[:nt, :, 0:4],
                                               bif.t[:nt, 0:4].unsqueeze(1).to_broadcast([nt, ntl, 4]), ALU.add),
              reads=[bif.d], writes=[PD[ML], ig.d])
        cx.op("dve", lambda e: e.tensor_tensor(sp.t[:nt, 0:ntl, :], gp[:nt, :, 4:8],
                                               bif.t[:nt, 4:8].unsqueeze(1).to_broadcast([nt, ntl, 4]), ALU.add),
              reads=[bif.d], writes=[PD[ML], sp.d])
        spf = sp.t[:nt, 0:ntl, :].rearrange("p t h -> p (t h)")
        cx.op("act", lambda e: e.activation(spf, spf, AF.Exp, scale=-1.0), reads=[sp.d], writes=[sp.d])
        cx.op("act", lambda e: e.activation(spf, spf, AF.Ln, bias=eps_t.t[:nt, 2:3]), reads=[sp.d, eps_t.d], writes=[sp.d])
        w = 4 * ntl
        cx.op("pe", [mm(PB[ML][:nt, 128:128 + w], triu_f.t[:nt, :nt], spf),
                     mm(PB[ML][:, 192:192 + w], ones_f.t[:nt, :], spf)],
              reads=[triu_f.d, ones_f.d, sp.d], writes=[PD[ML]])
        negb = PB[ML][:nt, 128:128 + w]
        nbt = PB[ML][:, 192:192 + w]

        def fl(x):
            return x.t[:nt, 0:ntl, :].rearrange("p t h -> p (t h)")

        def flA(x):
            return x.t[:, 0:ntl, :].rearrange("p t h -> p (t h)")
        a, nB, amb, wA, wB, eb, eB = (gs[k] for k in ("a", "nB", "amb", "wA", "wB", "eb", "eB"))
        cx.op("dve", lambda e: e.tensor_tensor(fl(a), fl(ig), negb, ALU.add), reads=[ig.d], writes=[PD[ML], a.d])
        cx.op("dve", lambda e: e.tensor_copy(flA(nB), nbt), writes=[PD[ML], nB.d])
        cx.op("act", lambda e: e.activation(fl(eb), negb, AF.Exp, scale=-1.0), writes=[PD[ML], eb.d])
        cx.op("dve", lambda e: e.tensor_tensor(fl(amb), fl(a), fl(nB), ALU.subtract), reads=[a.d, nB.d], writes=[amb.d])
        cx.op("act", lambda e: e.activation(fl(wA), fl(a), AF.Exp, bias=eps_t.t[:nt, 1:2]), reads=[a.d, eps_t.d], writes=[wA.d])
        cx.op("act", lambda e: e.activation(fl(wB), fl(amb), AF.Exp, bias=eps_t.t[:nt, 1:2]), reads=[amb.d, eps_t.d], writes=[wB.d])
        cx.op("act", lambda e: e.activation(flA(eB), flA(nB), AF.Exp, scale=-1.0), reads=[nB.d], writes=[eB.d])

    def m_track(gs, ti, nt):
        a, nB = gs["a"], gs["nB"]
        t = small()
        cx.op("dve", lambda e: e.tensor_tensor(t.t[:nt, 0:4], a.t[:nt, ti, :], ngrun.t[:nt, :], ALU.add),
              reads=[a.d, ngrun.d], writes=[t.d])
        cx.op("dve", lambda e: e.tensor_tensor(amaxr.t[:nt, :], amaxr.t[:nt, :], t.t[:nt, 0:4], ALU.max),
              reads=[t.d, amaxr.d], writes=[amaxr.d])
        cx.op("dve", lambda e: e.tensor_tensor(ngrun.t[:, :], ngrun.t[:, :], nB.t[:, ti, :], ALU.add),
              reads=[nB.d, ngrun.d], writes=[ngrun.d])

    def new_gs(es_, ntl):
        return {k: TD(es_.enter_context(nc.sbuf_tensor(nm("gs_" + k), [128, ntl, 4], F32)), "gs_" + k)
                for k in ("ig", "sp", "a", "nB", "amb", "wA", "wB", "eb", "eB")}

    def state_update(K3, VB1, ti, nt, gs, with_bf, C32=C32, CBF=CBF):
        for h in range(4):
            cx.op("pe", mm(PB[ML][:, 256:385], K3.t[:nt, ti, h, :], VB1.t[:nt, ti, h, :]),
                  reads=[K3.d, VB1.d], writes=[PD[ML]])
            cx.op("dve", lambda e, h=h: e.scalar_tensor_tensor(C32.t[:, h, :], C32.t[:, h, :], gs["eB"].t[:, ti, h:h + 1],
                                                              PB[ML][:, 256:385], ALU.mult, ALU.add),
                  reads=[gs["eB"].d], writes=[PD[ML], C32.d])
        if with_bf:
            cx.op("act", lambda e: e.activation(CBF.t[:].rearrange("p h e -> p (h e)"),
                                                C32.t[:].rearrange("p h e -> p (h e)"), AF.Copy),
                  reads=[C32.d], writes=[CBF.d])

    def evac_k3(hT, col0, nt, ti, wkb, gs, K3):
        b = tm_proj(lambda k: wkb.t[:, k, :], wkb.d, hT, col0, nt, 512)
        cx.op("dve", lambda e: e.tensor_tensor(K3.t[:nt, ti, :, :], PB[b][:nt, :].rearrange("p (h d) -> p h d", h=4),
                                               gs["wB"].t[:nt, ti, :].unsqueeze(2).to_broadcast([nt, 4, 128]), ALU.mult),
              reads=[gs["wB"].d], writes=[PD[b], K3.d])

    def evac_vb1(hT, col0, nt, ti, wvb, VB1):
        b = tm_proj(lambda k: wvb.t[:, k, :], wvb.d, hT, col0, nt, 512)
        cx.op("act", lambda e: e.activation(VB1.t[:nt, ti, :, 0:128], PB[b][:nt, :].rearrange("p (h d) -> p h d", h=4), AF.Copy),
              writes=[PD[b], VB1.d])
        cx.op("dve", lambda e: e.memset(VB1.t[:nt, ti, :, 128:129], 1.0), writes=[VB1.d])

    try:
      for l in range(2):
          stop_at(1 + 10 * l)
          xin = XP if l == 0 else X1
          xin_d = None if l == 0 else d_x1
          xout = X1 if l == 0 else YP
          xsin = XS if l == 0 else XS1
          xsout = XS1 if l == 0 else YS

          sem_par = cx.shared_sem("par")
          cx.dma("sp", gbc.t[:].rearrange("p a d -> p (a d)"), NG.ap()[l].rearrange("a d -> (a d)").partition_broadcast(128),
                 writes=[gbc.d], sem=sem_par)
          cx.dma("sp", gatt.t[:], GATT.ap()[l].partition_broadcast(128), writes=[gatt.d], sem=sem_par)
          cx.dma("sp", gml.t[:], GML.ap()[l].partition_broadcast(128), writes=[gml.d], sem=sem_par)
          cx.dma("sp", bif.t[:, 0:4], BI.ap()[l].partition_broadcast(128), writes=[bif.d], sem=sem_par)
          cx.dma("sp", bif.t[:, 4:8], BFG.ap()[l].partition_broadcast(128), writes=[bif.d], sem=sem_par)
          cx.dma("pool", wg.t[:], WIN.ap()[l, :, C_G:C_G + 8].rearrange("(k p) n -> p k n", p=128), writes=[wg.d])

          with contextlib.ExitStack() as es_:
              def lt(name, shape, dt=F32):
                  return TD(es_.enter_context(nc.sbuf_tensor(nm(name), list(shape), dt)), name)
              mm_ring[0] = [MM0, MM1, TRB]
              XT = [lt(f"a_xt{i}", [128, D]) for i in range(2)]
              hTA = lt("a_hTA", [128, 8, TOK], BF16)
              sqs = [lt(f"a_sq{i}", [128, D]) for i in range(2)]; xns = [lt(f"a_xn{i}", [128, D], BF16) for i in range(2)]
              wkb = lt("a_wkb", [128, 8, 512], BF16); wvb = lt("a_wvb", [128, 8, 512], BF16)
              K3L = [lt(f"a_k3{i}", [128, 4, 128], BF16) for i in range(2)]
              VB1L = [lt(f"a_vb1{i}", [128, 4, 129], BF16) for i in range(2)]
              khT = lt("a_khT", [128, 4, 512], BF16); vh = lt("a_vh", [128, 4, 512], BF16)
              gs = new_gs(es_, NTILE)
              pre = lt("a_pre", [128, NTILE, 4]); wC = lt("a_wC", [128, NTILE, 4])
              cx.dma("pool", wkb.t[:], WIN.ap()[l, :, C_KB:C_KB + 512].rearrange("(k p) n -> p k n", p=128), writes=[wkb.d])
              cx.dma("pool", wvb.t[:], WIN.ap()[l, :, C_VB:C_VB + 512].rearrange("(k p) n -> p k n", p=128), writes=[wvb.d])
              ws.schedule([win_loader(l, C_KA), win_loader(l, C_VA)])
              XT4 = XT + sqs
              jk = [lt(f"a_jk{i}", [128, D], BF16) for i in range(2)]
              items = []
              for t in range(NTILE):
                  xt = XT4[t % 4]
                  items.append((xt, 128, t * 128))
              for t0 in range(0, NTILE, 4):
                  for t in range(t0, t0 + 4):
                      cx.dma("sp", XT4[t % 4].t[:, :], xin.ap()[t * 128:(t + 1) * 128, :], writes=[XT4[t % 4].d])
                  norm_seq(items[t0:t0 + 4], 0, hTA, xns, jk)
              hT = TD(None, "hT_alias"); hT.t = hTA.t[:, :, TOK - 512:TOK]; hT.d = hTA.d
              wka = ws.get()
              fm_proj(wka, range(4), hT, 512, khT, 0, 0)
              cx.dma("sp", EXK_S.ap()[:, :].rearrange("(c p) n -> p c n", p=128), khT.t[:], reads=[khT.d],
                     writes=[d_exk[0]], sem=sem_out)
              wva = ws.get()
              for i in range(4):
                  b = tm_proj(lambda k: wva.t[:, k, :], wva.d, hT, i * 128, 128, 512)
                  evac(vh.t[:, i, :], PB[b][:, :], b, vh.d)
              cx.dma("sp", EXV_S.ap()[:, :].rearrange("(c p) n -> p c n", p=128), vh.t[:], reads=[vh.d],
                     writes=[d_exv[0]], sem=sem_out)
              cx.allgather(EXK_S.ap().opt(), EXK_D.ap().opt(), [d_exk[0]], [d_exk[1]], sem_cc)
              cx.allgather(EXV_S.ap().opt(), EXV_D.ap().opt(), [d_exv[0]], [d_exv[1]], sem_cc)
              gate_math(hTA, [(t * 128, 128) for t in range(NTILE)], gs)
              cx.op("dve", lambda e: e.memset(pre.t[:, 0, :], 0.0), writes=[pre.d])
              for t in range(1, NTILE):
                  cx.op("dve", lambda e, t=t: e.tensor_tensor(pre.t[:, t, :], pre.t[:, t - 1, :], gs["nB"].t[:, t - 1, :], ALU.add),
                        reads=[gs["nB"].d], writes=[pre.d])
              cx.op("dve", lambda e: e.tensor_tensor(ngrun.t[:, :], pre.t[:, NTILE - 1, :], gs["nB"].t[:, NTILE - 1, :], ALU.add),
                    reads=[gs["nB"].d, pre.d], writes=[ngrun.d])
              cx.op("dve", lambda e: e.tensor_tensor(gs["amb"].t[:], gs["a"].t[:], pre.t[:], ALU.add),
                    reads=[gs["a"].d, pre.d], writes=[gs["amb"].d])
              cx.op("dve", lambda e: e.tensor_reduce(amaxr.t[:, :], gs["amb"].t[:].rearrange("p t h -> p h t"), AX.X, ALU.max),
                    reads=[gs["amb"].d], writes=[amaxr.d])
              cx.op("dve", lambda e: e.tensor_tensor(pre.t[:], pre.t[:], ngrun.t[:, :].unsqueeze(1).to_broadcast([128, NTILE, 4]), ALU.subtract),
                    reads=[ngrun.d], writes=[pre.d])
              cx.op("dve", lambda e: e.tensor_tensor(gs["amb"].t[:], gs["a"].t[:], pre.t[:], ALU.add),
                    reads=[gs["a"].d, pre.d], writes=[gs["amb"].d])
              cx.op("act", lambda e: e.activation(wC.t[:].rearrange("p t h -> p (t h)"), gs["amb"].t[:].rearrange("p t h -> p (t h)"),
                                                  AF.Exp, bias=eps_t.t[:, 1:2]), reads=[gs["amb"].d, eps_t.d], writes=[wC.d])
              CB = [S0, S1, O0, O1]
              for t in range(NTILE):
                  K3, VB1 = K3L[t % 2], VB1L[t % 2]
                  b = tm_proj(lambda k: wkb.t[:, k, :], wkb.d, hTA, t * 128, 128, 512)
                  cx.op("dve", lambda e, t=t, b=b, K3=K3: e.tensor_tensor(K3.t[:, :, :], PB[b][:, :].rearrange("p (h d) -> p h d", h=4),
                                                                       wC.t[:, t, :].unsqueeze(2).to_broadcast([128, 4, 128]), ALU.mult),
                        reads=[wC.d], writes=[PD[b], K3.d])
                  b = tm_proj(lambda k: wvb.t[:, k, :], wvb.d, hTA, t * 128, 128, 512)
                  cx.op("act", lambda e, b=b, VB1=VB1: e.activation(VB1.t[:, :, 0:128], PB[b][:, :].rearrange("p (h d) -> p h d", h=4), AF.Copy),
                        writes=[PD[b], VB1.d])
                  cx.op("dve", lambda e, VB1=VB1: e.memset(VB1.t[:, :, 128:129], 1.0), writes=[VB1.d])
                  for h in range(4):
                      cx.op("pe", mm(PB[CB[h]][:, 0:129], K3.t[:, h, :], VB1.t[:, h, :], t == 0, t == NTILE - 1),
                            reads=[K3.d, VB1.d], writes=[PD[CB[h]]])
              for h in range(4):
                  cx.op("dve" if h % 2 else "act",
                        (lambda e, h=h: e.tensor_copy(C32.t[:, h, :], PB[CB[h]][:, 0:129])) if h % 2 else
                        (lambda e, h=h: e.activation(C32.t[:, h, :], PB[CB[h]][:, 0:129], AF.Copy)),
                        writes=[PD[CB[h]], C32.d])
              build_E(l, es_)
              cx.dma("sp", EXS_S.ap()[0:128, 0:516], C32.t[:].rearrange("p h e -> p (h e)"), reads=[C32.d],
                     writes=[d_exs[0]], sem=sem_out)
              srow = lt("a_srow", [128, XCOLS])
              cx.op("dve", lambda e: e.memset(srow.t[0:1, :], 0.0), writes=[srow.d])
              cx.op("dve", lambda e: e.tensor_copy(srow.t[0:1, 0:4], ngrun.t[0:1, :]), reads=[ngrun.d], writes=[srow.d])
              cx.op("pe", lambda e: e.transpose(PB[ML][0:4, 0:128], amaxr.t[:, 0:4], ident_f.t[:, :]),
                    reads=[amaxr.d, ident_f.d], writes=[PD[ML]])
              mu4 = small()
              cx.op("dve", lambda e: e.tensor_reduce(mu4.t[0:4, 0:1], PB[ML][0:4, 0:128], AX.X, ALU.max),
                    writes=[PD[ML], mu4.d])
              cx.op("pe", mm(PB[ML][0:1, 200:204], mu4.t[0:4, 0:1], ident_f.t[0:4, 0:4]), reads=[mu4.d, ident_f.d], writes=[PD[ML]])
              cx.op("dve", lambda e: e.tensor_copy(srow.t[0:1, 4:8], PB[ML][0:1, 200:204]), writes=[PD[ML], srow.d])
              cx.dma("sp", EXS_S.ap()[128:129, :], srow.t[0:1, :], reads=[srow.d], writes=[d_exs[0]], sem=sem_out)
              stop_at(2 + 10 * l)
              cx.allgather(EXS_S.ap().opt(), EXS_D.ap().opt(), [d_exs[0]], [d_exs[1]], sem_cc)
              stop_at(3 + 10 * l)
              cx.barrier(exclude=(sem_cc.key,))

          with contextlib.ExitStack() as es_:
              def lt(name, shape, dt=F32):
                  return TD(es_.enter_context(nc.sbuf_tensor(nm(name), list(shape), dt)), name)
              KTA = lt("b_kta", [128, 4, 1024], BF16)
              VA = lt("b_va", [128, 8, 8, 65], BF16)
              def consume_exchange():
                  kxs = PT.t[:, 0:16, :].rearrange("p (c a) q -> p c (a q)", c=4)
                  vxs = PT.t[:, 16:32, :].rearrange("p (c a) q -> p c (a q)", c=4)
                  cxs = yb.t[:, 0:516]
                  scbt = small()
                  scb = scbt.t[:, 0:32].rearrange("p (w r h) -> p w r h", w=2, r=4)
                  for w_ in range(2):
                      cx.dma("sp", scb[:, w_, :, :], bass.AP(EXS_D, 128 * XCOLS + 4 * w_, [[0, 128], [XROWS * XCOLS, 4], [1, 4]]),
                             reads=[d_exs[1]], writes=[scbt.d])
                  er = small()
                  cx.op("act", lambda e: e.activation(er.t[:, 0:16], scb[:, 0, :, :].rearrange("p r h -> p (r h)"), AF.Exp, scale=-1.0),
                        reads=[scbt.d], writes=[er.d])
                  cx.op("dve", lambda e: e.tensor_scalar(er.t[:, 0:16], er.t[:, 0:16], -1.0, None, ALU.add), reads=[er.d], writes=[er.d])
                  cx.op("dve", lambda e: e.tensor_tensor(er.t[:, 0:16].rearrange("p (r h) -> p r h", r=4),
                                                         er.t[:, 0:16].rearrange("p (r h) -> p r h", r=4),
                                                         cm.t[:, 4:8].unsqueeze(2).to_broadcast([128, 4, 4]), ALU.mult),
                        reads=[er.d, cm.d], writes=[er.d])
                  cx.op("dve", lambda e: e.tensor_scalar(er.t[:, 0:16], er.t[:, 0:16], 1.0, None, ALU.add), reads=[er.d], writes=[er.d])
                  cx.op("dve", lambda e: e.memset(C32.t[:].rearrange("p h e -> p (h e)"), 0.0), writes=[C32.d])
                  kdst = KTA.t[:, :, 512:1024]
                  for r in range(3):
                      cx.dma("sp", kxs, EXK_D.ap()[r * 512:(r + 1) * 512, :].rearrange("(c p) n -> p c n", p=128),
                             reads=[d_exk[1]], writes=[PT.d], sem=cx.shared_sem("xst"))
                      cx.dma("sp", vxs, EXV_D.ap()[r * 512:(r + 1) * 512, :].rearrange("(c p) n -> p c n", p=128),
                             reads=[d_exv[1]], writes=[PT.d], sem=cx.shared_sem("xst"))
                      if r < 3:
                          cx.dma("sp", cxs, EXS_D.ap()[r * XROWS:r * XROWS + 128, 0:516], reads=[d_exs[1]], writes=[yb.d],
                                 sem=cx.shared_sem("xst"))
                      if r == 0:
                          cx.op("dve", lambda e, r=r: e.tensor_scalar(kdst, kxs, cm.t[:, r:r + 1], None, ALU.mult),
                                reads=[PT.d, cm.d], writes=[KTA.d])
                      else:
                          cx.op("dve", lambda e, r=r: e.scalar_tensor_tensor(kdst, kxs, cm.t[:, r:r + 1], kdst, ALU.mult, ALU.add),
                                reads=[PT.d, cm.d], writes=[KTA.d])
                      for tt_ in range(4):
                          vsrc = vxs[:, tt_, :].rearrange("p (h d) -> p h d", h=8)
                          vdst = VA.t[:, 4 + tt_, :, 0:64]
                          if r == 0:
                              cx.op("dve", lambda e, r=r, vsrc=vsrc, vdst=vdst: e.tensor_scalar(vdst, vsrc, cm.t[:, r:r + 1], None, ALU.mult),
                                    reads=[PT.d, cm.d], writes=[VA.d])
                          else:
                              cx.op("dve", lambda e, r=r, vsrc=vsrc, vdst=vdst: e.scalar_tensor_tensor(
                                  vdst, vsrc, cm.t[:, r:r + 1], vdst, ALU.mult, ALU.add), reads=[PT.d, cm.d], writes=[VA.d])
                      if r < 3:
                          for h in range(4):
                              cx.op("dve", lambda e, r=r, h=h: e.tensor_scalar(C32.t[:, h, :], C32.t[:, h, :], er.t[:, r * 4 + h:r * 4 + h + 1],
                                                                              None, ALU.mult), reads=[er.d], writes=[C32.d])
                              cx.op("dve", lambda e, r=r, h=h: e.scalar_tensor_tensor(C32.t[:, h, :], cxs[:, h * 129:(h + 1) * 129],
                                                                                     cm.t[:, 4 + r:5 + r], C32.t[:, h, :], ALU.mult, ALU.add),
                                    reads=[yb.d, cm.d], writes=[C32.d])
                  fl_ = small()
                  cx.op("dve", lambda e: e.tensor_reduce(fl_.t[:, 0:1], cm.t[:, 0:4], AX.X, ALU.add), reads=[cm.d], writes=[fl_.d])
                  for tt_ in range(4):
                      cx.op("dve", lambda e, tt_=tt_: e.tensor_copy(VA.t[:, 4 + tt_, :, 64:65],
                                                                   fl_.t[:, 0:1].unsqueeze(1).to_broadcast([128, 8, 1])),
                            reads=[fl_.d], writes=[VA.d])
                  cx.op("act", lambda e: e.activation(CBF.t[:].rearrange("p h e -> p (h e)"),
                                                      C32.t[:].rearrange("p h e -> p (h e)"), AF.Copy), reads=[C32.d], writes=[CBF.d])
                  mt = small()
                  cx.op("dve", lambda e: e.memset(mt.t[:, 0:16], 0.0), writes=[mt.d])
                  for r in range(4):
                      cx.op("dve", lambda e, r=r: e.tensor_tensor(mt.t[:, 12:16], scb[:, 1, r, :], mt.t[:, 0:4], ALU.add),
                            reads=[scbt.d], writes=[mt.d])
                      cx.op("dve", lambda e, r=r: e.tensor_scalar(mt.t[:, 16:17], cm.t[:, 8 + r:9 + r], -1.0, -NEG, ALU.add, ALU.mult),
                            reads=[cm.d], writes=[mt.d])
                      cx.op("dve", lambda e: e.tensor_scalar(mt.t[:, 12:16], mt.t[:, 12:16], mt.t[:, 16:17], None, ALU.add),
                            writes=[mt.d])
                      cx.op("dve", lambda e: e.tensor_tensor(mt.t[:, 4:8], mt.t[:, 4:8], mt.t[:, 12:16], ALU.max), writes=[mt.d])
                      cx.op("dve", lambda e, r=r: e.tensor_tensor(mt.t[:, 0:4], mt.t[:, 0:4], scb[:, 0, r, :], ALU.add),
                            reads=[scbt.d], writes=[mt.d])
                      cx.op("dve", lambda e, r=r: e.scalar_tensor_tensor(mt.t[:, 8:12], scb[:, 0, r, :], cm.t[:, 8 + r:9 + r],
                                                                        mt.t[:, 8:12], ALU.mult, ALU.add),
                            reads=[scbt.d, cm.d], writes=[mt.d])
                  mE = ME[l]
                  cx.op("dve", lambda e: e.tensor_tensor(mE.t[:, 0:4], mt.t[:, 4:8], mt.t[:, 8:12], ALU.subtract),
                        reads=[mt.d], writes=[mE.d])
                  cx.op("act", lambda e: e.activation(mE.t[:, 4:8], mE.t[:, 0:4], AF.Exp, scale=-1.0), reads=[mE.d], writes=[mE.d])
                  cx.dma("sp", MP.ap()[l:l + 1, :], mE.t[0:1, 0:4], reads=[mE.d], writes=[d_out], sem=sem_out)

              hT = lt("b_hT", [128, 8, 512], BF16); sq = lt("b_sq", [128, D]); xn = lt("b_xn", [128, D], BF16)
              xn2 = lt("b_xn2", [128, D], BF16); jk0 = TD(None, "jk0"); jk0.t = sq.t[:, 0:512].bitcast(BF16); jk0.d = sq.d
              jk1 = TD(None, "jk1"); jk1.t = sq.t[:, 512:1024].bitcast(BF16); jk1.d = sq.d
              QTA = lt("b_qta", [128, 8, 512], BF16)
              QTB = lt("b_qtb", [128, 4, 512], BF16); KTB = lt("b_ktb", [128, 4, 512], BF16)
              K3 = lt("b_k3", [128, 4, 4, 128], BF16); VB1 = lt("b_vb1", [128, 4, 4, 129], BF16)
              OG = lt("b_og", [128, 4, 512])
              PT = lt("b_pt", [128, 40, 128], BF16)
              wout = lt("b_wout", [128, 8, D], BF16)
              att = lt("b_att", [128, 8, 64]); mix = lt("b_mix", [128, D], BF16); mix2 = lt("b_mix2", [128, D], BF16); mixT = lt("b_mixT", [128, 8, 128], BF16)
              yb = lt("b_y", [128, D]); hh = lt("b_hh", [128, 4, 128]); GT = lt("b_gt", [128, 4, 128], BF16)
              XK = [lt(f"b_xk{i}", [128, D]) for i in range(4)]
              gs = new_gs(es_, 4)
              cx.dma("pool", wout.t[:], WOUT.ap()[l].rearrange("(k p) n -> p k n", p=128), writes=[wout.d])
              cx.op("dve", lambda e: e.memset(QTA.t[:].rearrange("p h q -> p (h q)"), 0.0), writes=[QTA.d])
              ov = [PB[O0][:, 0:260].rearrange("p (h e) -> p h e", h=4), PB[O1][:, 0:260].rearrange("p (h e) -> p h e", h=4)]
              numv = [PB[O0][:, :].rearrange("p (h e) -> p h e", h=2), PB[O1][:, :].rearrange("p (h e) -> p h e", h=2)]
              trv = PB[TRB][:].bitcast(BF16).rearrange("p (k c) -> p k c", k=8)

              if do_sample:
                  hTs = lt("s_hT", [128, 8, 16], BF16)
                  QTAs = lt("s_qta", [128, 8, 16], BF16); KTAs = lt("s_kta", [128, 4, 16], BF16)
                  QTBs = lt("s_qtb", [128, 4, 16], BF16); KTBs = lt("s_ktb", [128, 4, 16], BF16)
                  K3s = lt("s_k3", [128, 1, 4, 128], BF16); VB1s = lt("s_vb1", [128, 1, 4, 129], BF16)
                  OGs = lt("s_og", [128, 1, 512]); VAs = lt("s_va", [128, 8, 65], BF16)
                  KcT = lt("s_kct", [128, 4, 512], BF16); Vc = lt("s_vc", [128, 4, 8, 65], BF16)
                  ckb = TD(None, "ckb_alias"); ckb.t = None; ckb.d = None
                  PTs = lt("s_pt", [128, 8, 5, 16], BF16)
                  C32s = lt("s_c32", [128, 4, 129]); CBFs = lt("s_cbf", [128, 4, 129], BF16)
                  m0b = lt("s_m0", [128, 16]); XKs = lt("s_xk", [128, D])
                  gss = new_gs(es_, 1)
                  cx.op("dve", lambda e: e.memset(QTAs.t[:].rearrange("p h q -> p (h q)"), 0.0), writes=[QTAs.d])
                  def sample_setup():
                      ckb.t = yb.t[:, :].bitcast(BF16).rearrange("p (j q) -> p j q", j=4)
                      ckb.d = yb.d
                      for j in range(4):
                          cx.dma("pool", ckb.t[:, j, :].rearrange("p (h d) -> p h d", h=8),
                                 CK.ap()[l, :, j * 128:(j + 1) * 128, :].rearrange("h t d -> t h d"), writes=[ckb.d])
                          cx.dma("pool", Vc.t[:, j, :, 0:64], CV.ap()[l, :, j * 128:(j + 1) * 128, :].rearrange("h t d -> t h d"),
                                 writes=[Vc.d])
                          cx.op("dve", lambda e, j=j: e.memset(Vc.t[:, j, :, 64:65], 1.0), writes=[Vc.d])
                      for jj in range(2):
                          cx.op("pe", [(lambda e, a_=a_, hp_=hp_: e.transpose(trv[:, a_ * 4 + hp_, :],
                                                                               ckb.t[:, 2 * jj + a_, hp_ * 128:(hp_ + 1) * 128], ident_b.t[:, :]))
                                       for a_ in range(2) for hp_ in range(4)], reads=[ckb.d, ident_b.d], writes=[PD[TRB]])
                          for a_ in range(2):
                              j = 2 * jj + a_
                              cx.op("act", lambda e, a_=a_, j=j: e.activation(KcT.t[:, :, j * 128:(j + 1) * 128], trv[:, a_ * 4:(a_ + 1) * 4, :], AF.Copy),
                                    writes=[PD[TRB], KcT.d])
                      cx.dma("sp", C32s.t[:, :, 0:128], SC.ap()[l].rearrange("h d e -> d h e"), writes=[C32s.d])
                      with nc.allow_non_contiguous_dma(reason="n state column"):
                          cx.dma("sp", C32s.t[:, :, 128], SN.ap()[l].rearrange("h d -> d h"), writes=[C32s.d])
                      cx.dma("sp", m0b.t[:, 0:4], SM.ap()[l].partition_broadcast(128), writes=[m0b.d])
                      cx.op("act", lambda e: e.activation(m0b.t[:, 4:8], m0b.t[:, 0:4], AF.Exp), reads=[m0b.d], writes=[m0b.d])
                      for h in range(4):
                          cx.op("dve", lambda e, h=h: e.tensor_scalar(C32s.t[:, h, :], C32s.t[:, h, :], m0b.t[:, 4 + h:5 + h], None, ALU.mult),
                                reads=[m0b.d], writes=[C32s.d])
                      cx.op("act", lambda e: e.activation(CBFs.t[:].rearrange("p h e -> p (h e)"),
                                                          C32s.t[:].rearrange("p h e -> p (h e)"), AF.Copy), reads=[C32s.d], writes=[CBFs.d])

              PTD = [Dep(f"pt{n_}") for n_ in range(10)]
              GTD = [Dep(f"gt{h_}") for h_ in range(4)]

              def attn_prompt(tg, c0, slices):
                  def S(n):
                      sbk = S0 if n % 2 == 0 else S1
                      fns = []
                      for q in range(4):
                          blk = n * 4 + q
                          h, j = blk // 5, blk % 5
                          kt = (tg - 4 + j) % 8
                          fns.append(mm(PB[sbk][:, q * 128:(q + 1) * 128], KTA.t[:, h // 2, kt * 128:(kt + 1) * 128],
                                        QTA.t[:, h, c0:c0 + 128]))
                      cx.op("pe", fns, reads=[KTA.d, QTA.d], writes=[PD[sbk]])
                      return sbk

                  def pv(h):
                      ob = O0 if h < 4 else O1
                      pairs = []
                      for j in range(5):
                          kt = (tg - 4 + j) % 8
                          pairs.append((PT.t[:, h * 5 + j, :], VA.t[:, kt, h, :]))
                      n0, n1 = (5 * h) // 4, (5 * h + 4) // 4
                      cx.op("pe", acc_group(ov[h // 4][:, h % 4, :], pairs), reads=[PTD[n0], PTD[n1], VA.d], writes=[PD[ob]])
                  done_h = 0
                  sbk = S(0)
                  for n in range(10):
                      nxt = S(n + 1) if n + 1 < 10 else None
                      ptv = PT.t[:, n * 4:(n + 1) * 4, :].rearrange("p b q -> p (b q)")
                      cx.op("act", lambda e, sbk=sbk, ptv=ptv: e.activation(ptv, PB[sbk][:, :], AF.Exp, scale=0.125),
                            writes=[PD[sbk], PTD[n]])
                      ev = EALL[l].t[:].rearrange("p h j q -> p (h j) q")[:, n * 4:(n + 1) * 4, :].rearrange("p b q -> p (b q)")
                      cx.op("dve", lambda e, ptv=ptv, ev=ev: e.tensor_tensor(ptv, ptv, ev, ALU.mult), reads=[EALL[l].d], writes=[PTD[n]])
                      while (done_h + 1) * 5 <= (n + 1) * 4:
                          pv(done_h)
                          done_h += 1
                      if slices:
                          slices.pop(0)()
                      sbk = nxt
                  while slices:
                      slices.pop(0)()

              def attn_sample():
                  for hb, sbk in ((0, S0), (1, S1)):
                      fns = []
                      for hq in range(4):
                          h = hb * 4 + hq
                          for j in range(4):
                              fns.append(mm(PB[sbk][:, hq * 80 + j * 16:hq * 80 + j * 16 + 16], KcT.t[:, h // 2, j * 128:(j + 1) * 128],
                                            QTAs.t[:, h, 0:16]))
                          fns.append(mm(PB[sbk][0:16, hq * 80 + 64:hq * 80 + 80], KTAs.t[:, h // 2, 0:16], QTAs.t[:, h, 0:16]))
                      cx.op("pe", fns, reads=[KcT.d, KTAs.d, QTAs.d], writes=[PD[sbk]])
                      ptv = PTs.t[:, hb * 4:(hb + 1) * 4, :, :].rearrange("p h j q -> p (h j q)")
                      cx.op("act", lambda e, sbk=sbk, ptv=ptv: e.activation(ptv, PB[sbk][:, 0:320], AF.Exp, scale=0.125),
                            writes=[PD[sbk], PTs.d])
                  cx.op("dve", lambda e: e.tensor_tensor(PTs.t[:, :, :, :].rearrange("p h j q -> p (h j) q"),
                                                         PTs.t[:, :, :, :].rearrange("p h j q -> p (h j) q"),
                                                         EALL[l].t[:].rearrange("p h j q -> p (h j) q")[:, :, 0:16], ALU.mult),
                        reads=[EALL[l].d], writes=[PTs.d])
                  for h in range(8):
                      ob = O0 if h < 4 else O1
                      pairs = [(PTs.t[:, h, j, :], Vc.t[:, j, h, :]) for j in range(4)]
                      pairs.append((PTs.t[0:16, h, 4, :], VAs.t[0:16, h, :]))
                      cx.op("pe", acc_group(ov[h // 4][0:16, h % 4, :], pairs), reads=[PTs.d, Vc.d, VAs.d], writes=[PD[ob]])

              def attn_epi(nt, mix):
                  rd = small()
                  for hb, ob in ((0, O0), (1, O1)):
                      cx.op("dve", lambda e, hb=hb: e.reciprocal(rd.t[:nt, hb * 4:hb * 4 + 4], ov[hb][:nt, :, 64]),
                            writes=[PD[ob], rd.d])
                      cx.op("dve", lambda e, hb=hb: e.tensor_tensor(att.t[:nt, hb * 4:hb * 4 + 4, :], ov[hb][:nt, :, 0:64],
                                                                    rd.t[:nt, hb * 4:hb * 4 + 4].unsqueeze(2).to_broadcast([nt, 4, 64]),
                                                                    ALU.mult), reads=[rd.d], writes=[PD[ob], att.d])
                  attf = att.t[:nt].rearrange("p h d -> p (h d)")
                  st = small()
                  cx.op("act", lambda e: e.activation(sq.t[:nt, 0:512], attf, AF.Square, accum_out=st.t[:nt, 0:1]),
                        reads=[att.d], writes=[sq.d, st.d])
                  rstd_from_ss(st, nt, 0, 1, 1.0 / 512)
                  cx.op("dve", lambda e: e.scalar_tensor_tensor(mix.t[:nt, 0:512], attf, st.t[:nt, 1:2], gatt.t[:nt, :], ALU.mult, ALU.mult),
                        reads=[att.d, st.d, gatt.d], writes=[mix.d])

              numv = [PB[MM0][:, :].rearrange("p (h e) -> p h e", h=2), PB[MM1][:, :].rearrange("p (h e) -> p h e", h=2)]
              NB_ = (MM0, MM1)

              def make_slices(nt, c0, ti, QTBx, KTBx, K3x, VB1x, OGx, ogd, gsx, C32x, CBFx, track, xk, after, mix):
                  sl = []

                  def head(h):
                      def f():
                          cx.op("pe", mm(PB[ML][:nt, 0:nt], KTBx.t[:, h, c0:c0 + nt], QTBx.t[:, h, c0:c0 + nt]),
                                reads=[KTBx.d, QTBx.d], writes=[PD[ML]])
                          cx.op("dve", lambda e: e.scalar_tensor_tensor(GT.t[:nt, h, :nt], PB[ML][:nt, 0:nt],
                                                                        gsx["wA"].t[:nt, ti, h:h + 1], masku_b.t[:nt, :nt],
                                                                        ALU.mult, ALU.mult),
                                reads=[gsx["wA"].d, masku_b.d], writes=[PD[ML], GTD[h]])
                          cx.op("pe", [mm(numv[h // 2][:nt, h % 2, 0:129], GT.t[:nt, h, :nt], VB1x.t[:nt, ti, h, :], True, False),
                                       mm(numv[h // 2][:nt, h % 2, 0:129], QTBx.t[:, h, c0:c0 + nt], CBFx.t[:, h, :], False, True)],
                                reads=[GTD[h], VB1x.d, QTBx.d, CBFx.d], writes=[PD[NB_[h // 2]]])
                      return f
                  for h in range(4):
                      sl.append(head(h))

                  def st_upd():
                      if track:
                          m_track(gsx, ti, nt)
                      state_update(K3x, VB1x, ti, nt, gsx, True, C32x, CBFx)
                  sl.append(st_upd)

                  def hnorm():
                      dn = small()
                      for hb in range(2):
                          cx.op("dve", lambda e, hb=hb: e.tensor_tensor(dn.t[:nt, hb * 2:hb * 2 + 2], numv[hb][:nt, :, 128],
                                                                        gsx["eb"].t[:nt, ti, hb * 2:hb * 2 + 2], ALU.mult),
                                reads=[gsx["eb"].d], writes=[PD[NB_[hb]], dn.d])
                      cx.op("act", lambda e: e.activation(dn.t[:nt, 0:4], dn.t[:nt, 0:4], AF.Abs), writes=[dn.d])
                      cx.op("dve", lambda e: e.tensor_scalar(dn.t[:nt, 0:4], dn.t[:nt, 0:4], 1.0, None, ALU.max), writes=[dn.d])
                      cx.op("dve", lambda e: e.reciprocal(dn.t[:nt, 0:4], dn.t[:nt, 0:4]), writes=[dn.d])
                      cx.op("dve", lambda e: e.tensor_tensor(dn.t[:nt, 4:8], dn.t[:nt, 0:4], gsx["eb"].t[:nt, ti, :], ALU.mult),
                            reads=[gsx["eb"].d], writes=[dn.d])
                      for hb in range(2):
                          cx.op("dve", lambda e, hb=hb: e.tensor_tensor(hh.t[:nt, hb * 2:hb * 2 + 2, :], numv[hb][:nt, :, 0:128],
                                                                        dn.t[:nt, 4 + hb * 2:6 + hb * 2].unsqueeze(2).to_broadcast([nt, 2, 128]),
                                                                        ALU.mult), reads=[dn.d], writes=[PD[NB_[hb]], hh.d])
                      hhf = hh.t[:nt].rearrange("p h d -> p (h d)")
                      cx.op("act", lambda e: e.activation(sq.t[:nt, 0:512], hhf, AF.Square), reads=[hh.d], writes=[sq.d])
                      s4 = small()
                      cx.op("dve", lambda e: e.tensor_reduce(s4.t[:nt, 0:4], sq.t[:nt, 0:512].rearrange("p (h d) -> p h d", h=4), AX.X, ALU.add),
                            reads=[sq.d], writes=[s4.d])
                      rstd_from_ss(s4, nt, 0, 4, 1.0 / 128, width=4)
                      cx.op("dve", lambda e: e.tensor_tensor(hh.t[:nt], hh.t[:nt], s4.t[:nt, 4:8].unsqueeze(2).to_broadcast([nt, 4, 128]), ALU.mult),
                            reads=[s4.d], writes=[hh.d])
                      cx.op("dve", lambda e: e.tensor_tensor(mix.t[:nt, 512:1024], hhf, OGx, ALU.mult),
                            reads=[hh.d, ogd], writes=[mix.d])
                  sl.append(hnorm)

                  def tr():
                      transpose_to(mix, nt, 8, mixT, 0, eng="dve")
                  sl.append(tr)

                  def wo(half):
                      def f():
                          b = TRB
                          cx.op("pe", acc_group(PB[b][:nt, :], [(mixT.t[:, k, 0:nt], wout.t[:, k, half * 512:(half + 1) * 512]) for k in range(8)]),
                                reads=[mixT.d, wout.d], writes=[PD[b]])
                          evac(yb.t[:nt, half * 512:(half + 1) * 512], PB[b][:nt, :], b, yb.d)
                      return f
                  sl.append(wo(0))
                  sl.append(wo(1))

                  def res():
                      st = small()
                      cx.op("act", lambda e: e.activation(sq.t[:nt, :], yb.t[:nt, :], AF.Square, accum_out=st.t[:nt, 0:1]),
                            reads=[yb.d], writes=[sq.d, st.d])
                      rstd_from_ss(st, nt, 0, 1, 1.0 / D)
                      cx.op("dve", lambda e: e.scalar_tensor_tensor(yb.t[:nt, :], yb.t[:nt, :], st.t[:nt, 1:2], gbc.t[:nt, 1, :], ALU.mult, ALU.mult),
                            reads=[st.d, gbc.d], writes=[yb.d])
                      cx.op("dve", lambda e: e.tensor_tensor(xk.t[:nt, :], xk.t[:nt, :], yb.t[:nt, :], ALU.add),
                            reads=[yb.d], writes=[xk.d])
                      after()
                  sl.append(res)
                  return sl[0:6], sl[6:10]

              stop_at(4 + 10 * l)
              blocks = []
              for g in range(4):
                  blocks += [win_loader(l, C_QA), win_loader(l, C_KA), win_loader(l, C_QB), win_loader(l, C_KB),
                             win_loader(l, C_VB), win_loader(l, C_OB), win_loader(l, C_VA)]
              ws.schedule(blocks)

              for g in range(4):
                  smp = do_sample and g == 3
                  mm_ring[0] = RING6
                  for i in range(4):
                      r0 = (g * 4 + i) * 128
                      if xin_d is not None:
                          cx.dma("sp", XK[i].t[:, :], xin.ap()[r0:r0 + 128, :], reads=[xin_d], writes=[XK[i].d])
                      else:
                          cx.dma("sp", XK[i].t[:, :], xin.ap()[r0:r0 + 128, :], writes=[XK[i].d])
                  norm_seq([(XK[i], 128, i * 128) for i in range(4)], 0, hT, [xn, xn2], [jk0, jk1])
                  tiles = [(i * 128, 128) for i in range(4)]
                  gate_math(hT, tiles, gs)
                  if g == 0:
                      with nc.allow_non_contiguous_dma(reason="tiny conv params"):
                          cx.dma("sp", wcT.t[:], WCV.ap()[l].rearrange("j (c p) -> p j c", p=128), writes=[wcT.d], sem=cx.shared_sem("parc"))
                          cx.dma("sp", bcT.t[:], BCV.ap()[l].rearrange("(c p) -> p c", p=128), writes=[bcT.d], sem=cx.shared_sem("parc"))
                  if smp:
                      sample_setup()
                      if l == 0:
                          cx.dma("sp", XKs.t[0:16, :], xsin.ap()[:, :], writes=[XKs.d])
                      else:
                          cx.dma("sp", XKs.t[0:16, :], xsin.ap()[:, :], reads=[d_xs1], writes=[XKs.d])
                      norm_T(XKs, 16, 0, hTs, 0, sq, xn)
                      gate_math(hTs, [(0, 16)], gss)
                  slot_g = (g % 2) * 4

                  def q_masked(w, hsrc, ncols, dst):
                      for hp_ in range(4):
                          b = next_mm()
                          cx.op("pe", acc_group(PB[b][:, 0:ncols], [(w.t[:, k, hp_ * 128:(hp_ + 1) * 128], hsrc.t[:, k, 0:ncols]) for k in range(8)]),
                                reads=[w.d, hsrc.d], writes=[PD[b]])
                          cx.op("act", lambda e, hp_=hp_, b=b: e.activation(dst.t[0:64, 2 * hp_, 0:ncols], PB[b][0:64, 0:ncols], AF.Copy),
                                writes=[PD[b], dst.d])
                          cx.op("dve", lambda e, hp_=hp_, b=b: e.tensor_copy(dst.t[64:128, 2 * hp_ + 1, 0:ncols], PB[b][64:128, 0:ncols]),
                                writes=[PD[b], dst.d])
                  w = ws.get()
                  q_masked(w, hT, 512, QTA)
                  if smp:
                      q_masked(w, hTs, 16, QTAs)
                  w = ws.get(); fm_proj(w, range(4), hT, 512, KTA, 0, slot_g * 128)
                  if g == 3:
                      ko = TD(yb.t, "ko_alias"); ko.d = yb.d
                      for i in range(4):
                          b = tm_proj(lambda k: w.t[:, k, :], w.d, hT, i * 128, 128, 512)
                          evac(yb.t[:, 0:512], PB[b][:, :], b, yb.d)
                          cx.dma("sp", KP.ap()[l, :, i * 128:(i + 1) * 128, :].rearrange("h t d -> t h d"),
                                 yb.t[:, 0:512].rearrange("p (h d) -> p h d", h=8), reads=[yb.d], writes=[d_out], sem=sem_out)
                  if smp:
                      fm_proj(w, range(4), hTs, 16, KTAs, 0, 0)
                      b = tm_proj(lambda k: w.t[:, k, :], w.d, hTs, 0, 16, 512)
                      evac(yb.t[0:16, 0:512], PB[b][0:16, :], b, yb.d)
                      cx.dma("sp", KS.ap()[l].rearrange("h t d -> t h d"), yb.t[0:16, 0:512].rearrange("p (h d) -> p h d", h=8),
                             reads=[yb.d], writes=[d_out], sem=sem_out)
                  w = ws.get(); fm_proj(w, range(4), hT, 512, QTB, 0, 0)
                  if smp:
                      fm_proj(w, range(4), hTs, 16, QTBs, 0, 0)
                  w = ws.get(); fm_proj(w, range(4), hT, 512, KTB, 0, 0)
                  for i in range(4):
                      evac_k3(hT, i * 128, 128, i, w, gs, K3)
                  if smp:
                      fm_proj(w, range(4), hTs, 16, KTBs, 0, 0)
                      evac_k3(hTs, 0, 16, 0, w, gss, K3s)
                  w = ws.get()
                  for i in range(4):
                      evac_vb1(hT, i * 128, 128, i, w, VB1)
                  if smp:
                      evac_vb1(hTs, 0, 16, 0, w, VB1s)

                  def og_tile(hsrc, col0, nt, dst_ap, dd):
                      b = tm_proj(lambda k: w.t[:, k, :], w.d, hsrc, col0, nt, 512)
                      cx.op("act", lambda e: e.activation(dst_ap, PB[b][:nt, :], AF.Exp, scale=-1.0), writes=[PD[b], dd])
                      cx.op("dve", lambda e: e.tensor_scalar(dst_ap, dst_ap, 1.0, None, ALU.add), writes=[dd])
                      cx.op("dve", lambda e: e.reciprocal(dst_ap, dst_ap), writes=[dd])
                      cx.op("dve", lambda e: e.tensor_tensor(dst_ap, dst_ap, gml.t[:nt, :], ALU.mult), reads=[gml.d], writes=[dd])
                  w = ws.get()
                  for i in range(4):
                      og_tile(hT, i * 128, 128, OG.t[:, i, :], OG.d)
                  if smp:
                      og_tile(hTs, 0, 16, OGs.t[0:16, 0, :], OGs.d)
                  w = ws.get()
                  for i in range(4):
                      b = tm_proj(lambda k: w.t[:, k, :], w.d, hT, i * 128, 128, 512)
                      cx.op("act", lambda e, i=i, b=b: e.activation(VA.t[:, slot_g + i, :, 0:64],
                                                                    PB[b][:, :].rearrange("p (h d) -> p h d", h=8), AF.Copy),
                            writes=[PD[b], VA.d])
                      cx.op("dve", lambda e, i=i: e.memset(VA.t[:, slot_g + i, :, 64:65], 1.0), writes=[VA.d])
                      if g == 3:
                          cx.op("dve", lambda e, b=b: e.tensor_copy(yb.t[:, 512:1024], PB[b][:, :]), writes=[PD[b], yb.d])
                          cx.dma("sp", VP.ap()[l, :, i * 128:(i + 1) * 128, :].rearrange("h t d -> t h d"),
                                 yb.t[:, 512:1024].rearrange("p (h d) -> p h d", h=8), reads=[yb.d], writes=[d_out], sem=sem_out)
                  if smp:
                      b = tm_proj(lambda k: w.t[:, k, :], w.d, hTs, 0, 16, 512)
                      cx.op("act", lambda e, b=b: e.activation(VAs.t[0:16, :, 0:64], PB[b][0:16, :].rearrange("p (h d) -> p h d", h=8), AF.Copy),
                            writes=[PD[b], VAs.d])
                      cx.op("dve", lambda e: e.memset(VAs.t[0:16, :, 64:65], 1.0), writes=[VAs.d])
                      cx.op("dve", lambda e, b=b: e.tensor_copy(yb.t[0:16, 512:1024], PB[b][0:16, :]), writes=[PD[b], yb.d])
                      cx.dma("sp", VS.ap()[l].rearrange("h t d -> t h d"), yb.t[0:16, 512:1024].rearrange("p (h d) -> p h d", h=8),
                             reads=[yb.d], writes=[d_out], sem=sem_out)

                  stop_at(41 + 100 * l)
                  if g == 3:
                      stop_at(45 + 100 * l)
                  if g == 0:
                      consume_exchange()
                  mm_ring[0] = RING2
                  back_prev = []
                  mixL = [mix, mix2]
                  for i in range(4):
                      tg = g * 4 + i
                      mixx = mixL[i % 2]

                      def after(i=i, tg=tg):
                          cx.dma("sp", XM.ap()[tg * 128:(tg + 1) * 128, :], XK[i].t[:, :], reads=[XK[i].d], writes=[d_xm], sem=sem_out)
                          if tg == NTILE - 1:
                              cx.dma("sp", EXX_S.ap()[:, :], XK[i].t[126:128, :], reads=[XK[i].d], writes=[d_exx[0]], sem=sem_out)
                      front, back = make_slices(128, i * 128, i, QTB, KTB, K3, VB1, OG.t[:, i, :], OG.d, gs, C32, CBF, True, XK[i], after, mixx)
                      order = []
                      bp, fr = list(back_prev), list(front)
                      while bp or fr:
                          if bp:
                              order.append(bp.pop(0))
                          if fr:
                              order.append(fr.pop(0))
                      attn_prompt(tg, i * 128, order)
                      attn_epi(128, mixx)
                      back_prev = back
                  for f_ in back_prev:
                      f_()
                  if smp:
                      def after_s():
                          cx.dma("sp", XSM.ap()[:, :], XKs.t[0:16, :], reads=[XKs.d], writes=[d_xsm], sem=sem_out)
                      front, back = make_slices(16, 0, 0, QTBs, KTBs, K3s, VB1s, OGs.t[0:16, 0, :], OGs.d, gss, C32s, CBFs, False, XKs, after_s, mix)
                      attn_sample()
                      attn_epi(16, mix)
                      for f_ in front + back:
                          f_()
                      cx.op("pe", lambda e: e.transpose(PB[ML][0:4, 0:16], gss["a"].t[0:16, 0, :], ident_f.t[0:16, 0:16]),
                            reads=[gss["a"].d, ident_f.d], writes=[PD[ML]])
                      mu4 = small()
                      cx.op("dve", lambda e: e.tensor_reduce(mu4.t[0:4, 0:1], PB[ML][0:4, 0:16], AX.X, ALU.max), writes=[PD[ML], mu4.d])
                      cx.op("pe", mm(PB[ML][0:1, 200:204], mu4.t[0:4, 0:1], ident_f.t[0:4, 0:4]), reads=[mu4.d, ident_f.d], writes=[PD[ML]])
                      cx.op("dve", lambda e: e.tensor_copy(mu4.t[0:1, 4:8], PB[ML][0:1, 200:204]), writes=[PD[ML], mu4.d])
                      cx.op("pe", mm(PB[ML][:, 208:212], ones_f.t[0:1, :], mu4.t[0:1, 4:8]), reads=[mu4.d, ones_f.d], writes=[PD[ML]])
                      cx.op("dve", lambda e: e.tensor_tensor(m0b.t[:, 8:12], m0b.t[:, 0:4], PB[ML][:, 208:212], ALU.max),
                            writes=[PD[ML], m0b.d])
                      cx.op("dve", lambda e: e.tensor_tensor(m0b.t[:, 8:12], m0b.t[:, 8:12], gss["nB"].t[:, 0, :], ALU.subtract),
                            reads=[gss["nB"].d], writes=[m0b.d])
                      cx.op("act", lambda e: e.activation(m0b.t[:, 12:16], m0b.t[:, 8:12], AF.Exp, scale=-1.0), writes=[m0b.d])
                      cos_ = hh
                      for h in range(4):
                          cx.op("dve", lambda e, h=h: e.tensor_scalar(sq.t[:, h * 129:(h + 1) * 129], C32s.t[:, h, :], m0b.t[:, 12 + h:13 + h], None, ALU.mult),
                                reads=[C32s.d, m0b.d], writes=[sq.d])
                      sqv = sq.t[:, 0:516].rearrange("p (h e) -> p h e", h=4)
                      cx.dma("sp", CS.ap()[l].rearrange("h d e -> d h e"), sqv[:, :, 0:128], reads=[sq.d], writes=[d_out], sem=sem_out)
                      with nc.allow_non_contiguous_dma(reason="n state column"):
                          cx.dma("sp", NS.ap()[l].rearrange("h d -> d h"), sqv[:, :, 128], reads=[sq.d], writes=[d_out], sem=sem_out)
                      cx.dma("sp", MS.ap()[l:l + 1, :], m0b.t[0:1, 8:12], reads=[m0b.d], writes=[d_out], sem=sem_out)

              mE = ME[l]
              co = TD(None, "co_alias"); co.t = sq.t[:, 0:516].rearrange("p (h e) -> p h e", h=4); co.d = sq.d
              for h in range(4):
                  cx.op("dve", lambda e, h=h: e.tensor_scalar(co.t[:, h, :], C32.t[:, h, :], mE.t[:, 4 + h:5 + h], None, ALU.mult),
                        reads=[C32.d, mE.d], writes=[co.d])
              cx.dma("sp", CP.ap()[l].rearrange("h d e -> d h e"), co.t[:, :, 0:128], reads=[co.d], writes=[d_out], sem=sem_out)
              with nc.allow_non_contiguous_dma(reason="n state column"):
                  cx.dma("sp", NP.ap()[l].rearrange("h d -> d h"), co.t[:, :, 128], reads=[co.d], writes=[d_out], sem=sem_out)
              stop_at(5 + 10 * l)
              cx.allgather(EXX_S.ap().opt(), EXX_D.ap().opt(), [d_exx[0]], [d_exx[1]], sem_cc)
              cx.barrier(exclude=(sem_cc.key,))

          with contextlib.ExitStack() as es_:
              def lt(name, shape, dt=F32):
                  return TD(es_.enter_context(nc.sbuf_tensor(nm(name), list(shape), dt)), name)
              hT = lt("c_hT", [128, 8, 512], BF16); sq = lt("c_sq", [128, D]); xn = lt("c_xn", [128, D], BF16)
              xn2 = lt("c_xn2", [128, D], BF16); jk0 = lt("c_jk0", [128, D], BF16); jk1 = lt("c_jk1", [128, D], BF16)
              actT = lt("c_actT", [128, NCH, 512], BF16)
              wd = lt("c_wd", [128, NCH, D], BF16)
              XK = [lt(f"c_xk{i}", [128, D]) for i in range(4)]
              yg = [lt(f"c_yg{i}", [128, 512]) for i in range(2)]
              yu = [lt(f"c_yu{i}", [128, 512]) for i in range(2)]
              yb = lt("c_y", [128, D])
              hTh = lt("c_hTh", [128, 8, 2], BF16); xh = yb
              blocks = []
              for g in range(4):
                  blocks += [wup_loader(l, j) for j in range(11)]
                  if g == 0:
                      blocks += [wup_loader(l, j) for j in range(3)]
              ws.schedule(blocks)
              ws.prefetch()
              for hf in range(2):
                  cx.dma("pool", wd.t[:, hf * 11:(hf + 1) * 11, :],
                         WDN.ap()[l, hf * 11 * 128:(hf + 1) * 11 * 128, :].rearrange("(c p) n -> p c n", p=128), writes=[wd.d])

              UC_RING = [S0, S1, O0, O1, ML, MM0, MM1]
              uc_i = [0]
              mm_ring[0] = RING2

              def edge_src(bk, ncols):
                  a_ = PB[bk][:, 0:2]
                  return bass.AP(a_.tensor, a_.offset, [list(a_.ap[0]), [ncols - 2, 2], [1, 2]])

              HS = 3

              def halo_block(w, j, hsrc, uh):
                  for pr in range(2):
                      for typ in range(2):
                          cidx = 2 * j + pr + typ * NCH
                          uc_i[0] += 1
                          bk = UC_RING[uc_i[0] % len(UC_RING)]
                          cx.op("pe", acc_group(PB[bk][:, 0:2], [(w.t[:, k, typ * 256 + pr * 128:typ * 256 + (pr + 1) * 128],
                                                                  hsrc.t[:, k, 0:2]) for k in range(8)]),
                                reads=[w.d, hsrc.d], writes=[PD[bk]])
                          cx.op("act", lambda e, bk=bk, cidx=cidx: e.activation(uh.t[:, cidx, :], PB[bk][:, 0:2], AF.Copy),
                                writes=[PD[bk], uh.d])

              def up_main(streams, halo=None):
                  for j in range(11):
                      w = ws.get()
                      if halo is not None and j == HS:
                          halo_prep()
                      if halo is not None and j >= HS:
                          halo_block(w, j, halo[0], halo[1])
                      for (hsrc, ncols, edge, act_dst) in streams:
                          for pr in range(2):
                              ch = 2 * j + pr
                              for typ in range(2):
                                  cidx = ch + typ * NCH
                                  uc_i[0] += 1
                                  bk = UC_RING[uc_i[0] % len(UC_RING)]
                                  cx.op("pe", acc_group(PB[bk][:, 0:ncols], [(w.t[:, k, typ * 256 + pr * 128:typ * 256 + (pr + 1) * 128],
                                                                              hsrc.t[:, k, 0:ncols]) for k in range(8)]),
                                        reads=[w.d, hsrc.d], writes=[PD[bk]])
                                  y = (yg if typ == 0 else yu)[pr]
                                  cx.op("act", lambda e, bk=bk, cidx=cidx, y=y: e.activation(y.t[:, 0:ncols], PB[bk][:, 0:ncols], AF.Identity,
                                                                                            scale=wcT.t[:, 2, cidx:cidx + 1], bias=bcT.t[:, cidx:cidx + 1]),
                                        reads=[wcT.d, bcT.d], writes=[PD[bk], y.d])
                                  cx.op("act", lambda e, bk=bk, cidx=cidx: e.activation(edge.t[:, cidx, :].rearrange("p (a b) -> p a b", a=2),
                                                                                       edge_src(bk, ncols), AF.Copy),
                                        writes=[PD[bk], edge.d])
                                  cx.op("dve", lambda e, bk=bk, cidx=cidx, y=y: e.scalar_tensor_tensor(
                                      y.t[:, 1:ncols], PB[bk][:, 0:ncols - 1], wcT.t[:, 1, cidx:cidx + 1], y.t[:, 1:ncols], ALU.mult, ALU.add),
                                      reads=[wcT.d], writes=[PD[bk], y.d])
                                  cx.op("dve", lambda e, bk=bk, cidx=cidx, y=y: e.scalar_tensor_tensor(
                                      y.t[:, 2:ncols], PB[bk][:, 0:ncols - 2], wcT.t[:, 0, cidx:cidx + 1], y.t[:, 2:ncols], ALU.mult, ALU.add),
                                      reads=[wcT.d], writes=[PD[bk], y.d])
                              cx.op("act", lambda e, pr=pr: e.activation(yg[pr].t[:, 0:ncols], yg[pr].t[:, 0:ncols], AF.Gelu_apprx_tanh),
                                    writes=[yg[pr].d])
                              cx.op("dve", lambda e, pr=pr, ch=ch: e.tensor_tensor(act_dst.t[:, ch, 0:ncols], yg[pr].t[:, 0:ncols],
                                                                                  yu[pr].t[:, 0:ncols], ALU.mult),
                                    reads=[yg[pr].d, yu[pr].d], writes=[act_dst.d])

              def halo_pass(hsrc, uh):
                  for j in range(HS):
                      w = ws.get()
                      halo_block(w, j, hsrc, uh)

              fy = lt("c_fy", [128, 2, 2 * NCH]); ft = lt("c_ft", [128, 2 * NCH])

              def fix_boundary(edge, halo, halo_dep, act_dst):
                  u0, u1 = edge.t[:, :, 0], edge.t[:, :, 1]
                  h0, h1 = halo[:, :, 0], halo[:, :, 1]
                  W0, W1, W2 = wcT.t[:, 0, :], wcT.t[:, 1, :], wcT.t[:, 2, :]
                  rd = [edge.d, halo_dep, wcT.d, bcT.d]
                  for col, (ua, ta, tb) in enumerate(((u0, h1, h0), (u1, u0, h1))):
                      yv = fy.t[:, col, :]
                      cx.op("dve", lambda e, yv=yv, ua=ua: e.tensor_tensor(yv, ua, W2, ALU.mult), reads=rd, writes=[fy.d])
                      cx.op("dve", lambda e, yv=yv: e.tensor_tensor(yv, yv, bcT.t[:, :], ALU.add), reads=rd, writes=[fy.d])
                      cx.op("dve", lambda e, ta=ta: e.tensor_tensor(ft.t[:, :], ta, W1, ALU.mult), reads=rd, writes=[ft.d])
                      cx.op("dve", lambda e, yv=yv: e.tensor_tensor(yv, yv, ft.t[:, :], ALU.add), reads=[ft.d], writes=[fy.d])
                      cx.op("dve", lambda e, tb=tb: e.tensor_tensor(ft.t[:, :], tb, W0, ALU.mult), reads=rd, writes=[ft.d])
                      cx.op("dve", lambda e, yv=yv: e.tensor_tensor(yv, yv, ft.t[:, :], ALU.add), reads=[ft.d], writes=[fy.d])
                  cx.op("act", lambda e: e.activation(fy.t[:, :, 0:NCH], fy.t[:, :, 0:NCH], AF.Gelu_apprx_tanh), writes=[fy.d])
                  cx.op("dve", lambda e: e.tensor_tensor(act_dst.t[:, :, 0:2].rearrange("p c t -> p t c"), fy.t[:, :, 0:NCH], fy.t[:, :, NCH:2 * NCH],
                                                         ALU.mult), reads=[fy.d], writes=[act_dst.d])

              def down_res(nt, act_src, c0, xk):
                  for half in range(2):
                      b = next_mm()
                      cx.op("pe", acc_group(PB[b][:nt, :], [(act_src.t[:, c, c0:c0 + nt], wd.t[:, c, half * 512:(half + 1) * 512])
                                                            for c in range(NCH)]),
                            reads=[act_src.d, wd.d], writes=[PD[b]])
                      evac(yb.t[:nt, half * 512:(half + 1) * 512], PB[b][:nt, :], b, yb.d)
                  st = small()
                  cx.op("act", lambda e: e.activation(sq.t[:nt, :], yb.t[:nt, :], AF.Square, accum_out=st.t[:nt, 0:1]),
                        reads=[yb.d], writes=[sq.d, st.d])
                  rstd_from_ss(st, nt, 0, 1, 1.0 / D)
                  cx.op("dve", lambda e: e.scalar_tensor_tensor(yb.t[:nt, :], yb.t[:nt, :], st.t[:nt, 1:2], gbc.t[:nt, 3, :], ALU.mult, ALU.mult),
                        reads=[st.d, gbc.d], writes=[yb.d])
                  cx.op("dve", lambda e: e.tensor_tensor(xk.t[:nt, :], xk.t[:nt, :], yb.t[:nt, :], ALU.add),
                        reads=[yb.d], writes=[xk.d])

              EDGE = [lt(f"c_edge{i}", [128, 2 * NCH, 4]) for i in range(2)]
              edge_s = lt("c_edge_s", [128, 2 * NCH, 4])
              if do_sample:
                  XKs = lt("c_xks", [128, D]); hTs = lt("c_hTs", [128, 8, 16], BF16); actTs = lt("c_actTs", [128, NCH, 16], BF16)
                  with nc.allow_non_contiguous_dma(reason="conv cache rows"):
                      for r_ in range(2):
                          cx.dma("sp", uh_s.t[:, :, r_], CCV.ap()[l, r_].rearrange("(c p) -> p c", p=128), writes=[uh_s.d])

              def halo_prep():
                  for r in range(4):
                      cx.dma("sp", sq.t[0:2, :], EXX_D.ap()[2 * r:2 * r + 2, :], reads=[d_exx[1]], writes=[sq.d], sem=cx.shared_sem("xst"))
                      if r == 0:
                          cx.op("dve", lambda e: e.tensor_scalar(xh.t[0:2, :], sq.t[0:2, :], cm.t[0:2, 0:1], None, ALU.mult),
                                reads=[sq.d, cm.d], writes=[xh.d])
                      else:
                          cx.op("dve", lambda e, r=r: e.scalar_tensor_tensor(xh.t[0:2, :], sq.t[0:2, :], cm.t[0:2, r:r + 1], xh.t[0:2, :],
                                                                            ALU.mult, ALU.add), reads=[sq.d, cm.d], writes=[xh.d])
                  norm_T(xh, 2, 2, hTh, 0, sq, xn)

              for g in range(4):
                  smp = do_sample and g == 3
                  for i in range(4):
                      r0 = (g * 4 + i) * 128
                      cx.dma("sp", XK[i].t[:, :], XM.ap()[r0:r0 + 128, :], reads=[d_xm], writes=[XK[i].d])
                  norm_seq([(XK[i], 128, i * 128) for i in range(4)], 2, hT, [xn, xn2], [jk0, jk1])
                  edge = EDGE[g % 2]
                  streams = [(hT, 512, edge, actT)]
                  if smp:
                      cx.dma("sp", XKs.t[0:16, :], XSM.ap()[:, :], reads=[d_xsm], writes=[XKs.d])
                      norm_T(XKs, 16, 2, hTs, 0, sq, xn)
                      streams.append((hTs, 16, edge_s, actTs))
                  up_main(streams, halo=(hTh, uh_p) if g == 0 else None)
                  if g == 0:
                      halo_pass(hTh, uh_p)
                      fix_boundary(edge, uh_p.t[:, :, :], uh_p.d, actT)
                  else:
                      fix_boundary(edge, EDGE[(g - 1) % 2].t[:, :, 2:4], EDGE[(g - 1) % 2].d, actT)
                  if smp:
                      fix_boundary(edge_s, uh_s.t[:, :, :], uh_s.d, actTs)
                  for i in range(4):
                      tg = g * 4 + i
                      down_res(128, actT, i * 128, XK[i])
                      cx.dma("sp", xout.ap()[tg * 128:(tg + 1) * 128, :], XK[i].t[:, :], reads=[XK[i].d],
                             writes=[d_x1 if l == 0 else d_out], sem=sem_out)
                  if smp:
                      down_res(16, actTs, 0, XKs)
                      cx.dma("sp", xsout.ap()[:, :], XKs.t[0:16, :], reads=[XKs.d], writes=[d_xs1 if l == 0 else d_out], sem=sem_out)
                      with nc.allow_non_contiguous_dma(reason="conv state rows"):
                          for r_ in range(2):
                              cx.dma("sp", CVS.ap()[l, r_].rearrange("(c p) -> p c", p=128), edge_s.t[:, :, 2 + r_], reads=[edge_s.d],
                                     writes=[d_out], sem=sem_out)
              with nc.allow_non_contiguous_dma(reason="conv state rows (2 x 5632 elements)"):
                  for r_ in range(2):
                      cx.dma("sp", CVP.ap()[l, r_].rearrange("(c p) -> p c", p=128), EDGE[1].t[:, :, 2 + r_], reads=[EDGE[1].d],
                             writes=[d_out], sem=sem_out)
              cx.barrier()

    except _Stop:
        pass
    cx.finish()
    build.stats = (cx.n_inst, cx.n_wait, len(cx.sems))
    return nc


_NC_CACHE = {}


def kernel(x_prompt, x_sample, cache_k_att, cache_v_att, state_mlstm_c, state_mlstm_n, state_mlstm_m,
           cache_ffn_conv, norm_g, w_in, b_i, b_f, rel_table, g_att, g_mlstm, w_out, w_up, w_conv,
           b_conv, w_down):
    f = lambda a: np.ascontiguousarray(np.asarray(a), dtype=np.float32)
    x_prompt, x_sample = f(x_prompt), f(x_sample)
    shared = {
        "w_in": f(w_in), "w_out": f(w_out), "w_up": f(w_up), "w_down": f(w_down), "norm_g": f(norm_g),
        "g_att": f(g_att), "g_mlstm": f(g_mlstm), "b_i": f(b_i), "b_f": f(b_f), "rel_table": f(rel_table),
        "w_conv": f(w_conv), "b_conv": f(b_conv),
        "ident": np.eye(128, dtype=np.float32), "triu": np.triu(np.ones((128, 128), np.float32)),
        "ones": np.ones((128, 128), np.float32),
    }
    ck, cv = f(cache_k_att), f(cache_v_att)
    sc, sn, sm, ccv = f(state_mlstm_c), f(state_mlstm_n), f(state_mlstm_m), f(cache_ffn_conv)
    in_maps = []
    for c in range(8):
        b, j = c // 4, c % 4
        cmask = np.zeros(12, np.float32)
        for r in range(4):
            cmask[r] = 1.0 if r == j - 1 else 0.0
            cmask[4 + r] = 1.0 if r < j else 0.0
            cmask[8 + r] = 1.0 if r <= j else 0.0
        m = dict(shared)
        m.update({
            "xp": np.ascontiguousarray(x_prompt[b, j * TOK:(j + 1) * TOK]), "xs": np.ascontiguousarray(x_sample[c]),
            "ck": np.ascontiguousarray(ck[:, c]), "cv": np.ascontiguousarray(cv[:, c]),
            "sc": np.ascontiguousarray(sc[:, c]), "sn": np.ascontiguousarray(sn[:, c]),
            "sm": np.ascontiguousarray(sm[:, c]), "cconv": np.ascontiguousarray(ccv[:, c]),
            "cmask": cmask,
        })
        in_maps.append(m)
    if "nc" not in _NC_CACHE:
        _NC_CACHE["nc"] = build()
    res = run_bass_kernel_spmd(_NC_CACHE["nc"], in_maps, core_ids=list(range(8)))
    R = res.results
    yp = np.stack([np.concatenate([R[b * 4 + j]["yp"] for j in range(4)], 0) for b in range(2)], 0)
    ys = np.stack([R[c]["ys"] for c in range(8)], 0)
    last = [3, 7]
    pick = lambda k: np.stack([R[c][k] for c in last], 1)
    allc = lambda k: np.stack([R[c][k] for c in range(8)], 1)
    outs = (yp, ys, pick("kp"), pick("vp"), pick("cp"), pick("np"), pick("mp"), pick("cvp"),
            allc("ks"), allc("vs"), allc("cs"), allc("ns"), allc("ms"), allc("cvs"))
    return tuple(np.ascontiguousarray(o, dtype=np.float32) for o in outs)
```
